# Optimizing a Trainium2 kernel written in Bass

```python
import math
import jax, jax.numpy as jnp
from jax import lax
import numpy as np

D_MODEL = 1024
BATCH = 4
SEQ = 4096
DEPTH = 4
DEC_BATCH = 32
DEC_SEQ = 8
PAST_LEN = 8192
PAGE_SIZE = 128

N_A_LAYERS = DEPTH // 2
N_B_LAYERS = DEPTH - N_A_LAYERS
HEAD_DIM = 64
MIX_WIDTH = D_MODEL
MEM_HEADS = 4
MEM_DIM = MEM_HEADS * HEAD_DIM
N_MEM = 256
CONV_DIM = MIX_WIDTH - MEM_DIM
CONV_WIDTH = 3
NSA_HEADS = (MIX_WIDTH - MEM_DIM) // HEAD_DIM
NSA_KV_HEADS = 4
NSA_GROUP = NSA_HEADS // NSA_KV_HEADS
NSA_DIM = NSA_HEADS * HEAD_DIM
KV_DIM = NSA_KV_HEADS * HEAD_DIM
CMP_STRIDE = 16
CMP_LEN = 2 * CMP_STRIDE
CMP_HID = 2 * HEAD_DIM
SEL_BLOCK = 64
N_SEL = 16
WINDOW = 512
Q_BLOCK = 64
D_FF = -(-8 * D_MODEL // (3 * 256)) * 256
RMS_EPS = 1e-6
NEG_INF = -1e30
FORCE_SCORE = 1e4
ATT_SCALE = HEAD_DIM ** -0.5

kernel_name = 'yoco_shortconv_nsa_memory_decode_step'


def _alibi_list(n):
    def pow2(m):
        start = 2.0 ** (-8.0 / m)
        return [start ** (i + 1) for i in range(m)]
    if n & (n - 1) == 0:
        return pow2(n)
    c = 2 ** int(math.floor(math.log2(n)))
    return pow2(c) + _alibi_list(2 * c)[0::2][: n - c]


def rmsnorm(x, g):
    x32 = x.astype(jnp.float32)
    y = x32 * lax.rsqrt(jnp.mean(x32 * x32, axis=-1, keepdims=True) + RMS_EPS)
    return (y * g.astype(jnp.float32)).astype(x.dtype)


def swiglu(h, w_gu, w_dn):
    gate, up = jnp.split(h @ w_gu, 2, axis=-1)
    return (jax.nn.silu(gate) * up) @ w_dn


def short_conv(proj, conv_w, state):
    b_gate, c_gate, h = jnp.split(proj, 3, axis=-1)
    u = c_gate * h
    t = u.shape[1]
    u_ext = jnp.concatenate([state.astype(u.dtype), u], axis=1)
    y = conv_w[0] * u_ext[:, 0:t]
    for j in range(1, CONV_WIDTH):
        y = y + conv_w[j] * u_ext[:, j:j + t]
    return b_gate * y, u_ext[:, t:]


def mem_attention(mq, mk, mv):
    b, t = mq.shape[:2]
    q = mq.reshape(b, t, MEM_HEADS, HEAD_DIM)
    s = jnp.einsum('bthd,bmhd->bhtm', q, mk).astype(jnp.float32) * ATT_SCALE
    p = jax.nn.softmax(s, axis=-1).astype(mv.dtype)
    return jnp.einsum('bhtm,bmhd->bthd', p, mv).reshape(b, t, MEM_DIM)


def mix_a(h, conv_state, mk, mv, w_in, conv_w):
    proj = h @ w_in
    y_conv, new_state = short_conv(proj[..., :3 * CONV_DIM], conv_w, conv_state)
    y_mem = mem_attention(proj[..., 3 * CONV_DIM:], mk, mv)
    return jnp.concatenate([y_conv, y_mem], axis=-1), new_state


def project_b(h, w_in):
    proj = h @ w_in
    b, t = h.shape[:2]
    q = proj[..., :NSA_DIM].reshape(b, t, NSA_HEADS, HEAD_DIM)
    gates = jax.nn.sigmoid(proj[..., NSA_DIM:NSA_DIM + 3 * NSA_HEADS]).reshape(b, t, NSA_HEADS, 3)
    mq = proj[..., NSA_DIM + 3 * NSA_HEADS:]
    return q, gates, mq


def shared_kv(x, g_kv, w_kv):
    b, t = x.shape[:2]
    return (rmsnorm(x, g_kv) @ w_kv).reshape(b, t, 3, 2, NSA_KV_HEADS, HEAD_DIM)


def gather_pages(pool, page_table):
    g = pool[page_table]
    return g.reshape(page_table.shape[0], page_table.shape[1] * pool.shape[1], *pool.shape[2:])


def compress_blocks(k, pe, w1, w2):
    b, t = k.shape[:2]
    n_chunks = -(-t // CMP_STRIDE)
    k = jnp.pad(k, ((0, 0), (0, n_chunks * CMP_STRIDE - t), (0, 0), (0, 0)))
    ch = k.reshape(b, n_chunks, CMP_STRIDE, NSA_KV_HEADS, HEAD_DIM)
    lead = jnp.einsum('bcjkd,jde->bcke', ch + pe[None, None, :CMP_STRIDE, None, :], w1[:CMP_STRIDE])
    tail = jnp.einsum('bcjkd,jde->bcke', ch + pe[None, None, CMP_STRIDE:, None, :], w1[CMP_STRIDE:])
    out = jax.nn.silu(lead[:, :-1] + tail[:, 1:]) @ w2
    end_pos = jnp.arange(n_chunks - 1, dtype=jnp.int32) * CMP_STRIDE + (CMP_LEN - 1)
    return out, end_pos


def sel_blocks(k):
    b, t = k.shape[:2]
    n_sb = max(-(-t // SEL_BLOCK), N_SEL)
    k = jnp.pad(k, ((0, 0), (0, n_sb * SEL_BLOCK - t), (0, 0), (0, 0)))
    return k.reshape(b, n_sb, SEL_BLOCK, NSA_KV_HEADS, HEAD_DIM).transpose(0, 3, 1, 2, 4)


def nsa_context(cmp_rows, slc_rows, pe_ck, w1_ck, w2_ck, pe_cv, w1_cv, w2_cv):
    ck, c_end = compress_blocks(cmp_rows[:, :, 0], pe_ck, w1_ck, w2_ck)
    cv, _ = compress_blocks(cmp_rows[:, :, 1], pe_cv, w1_cv, w2_cv)
    return (ck, cv, c_end, sel_blocks(slc_rows[:, :, 0]), sel_blocks(slc_rows[:, :, 1]))


def nsa_block(q, gates, q_pos, ck, cv, c_end, ks_blk, vs_blk, kw, vw, w_pos, slopes):
    b, tq = q.shape[:2]
    qg = q.reshape(b, tq, NSA_KV_HEADS, NSA_GROUP, HEAD_DIM)
    sl = slopes.reshape(NSA_KV_HEADS, NSA_GROUP)
    s_c = jnp.einsum('btkgd,bnkd->btkgn', qg, ck).astype(jnp.float32) * ATT_SCALE
    dist_c = q_pos[:, None] - c_end[None, :]
    vis_c = dist_c >= 0
    s_c = s_c - sl[None, None, :, :, None] * dist_c.astype(jnp.float32)[None, :, None, None, :]
    s_c = jnp.where(vis_c[None, :, None, None, :], s_c, NEG_INF)
    any_c = jnp.any(vis_c, axis=-1).astype(jnp.float32)
    p_c = jax.nn.softmax(s_c, axis=-1) * any_c[None, :, None, None, None]
    o_c = jnp.einsum('btkgn,bnkd->btkgd', p_c.astype(cv.dtype), cv)
    n_sb = ks_blk.shape[2]
    per = SEL_BLOCK // CMP_STRIDE
    imp = p_c.sum(axis=3)
    imp = jnp.pad(imp, ((0, 0), (0, 0), (0, 0), (0, n_sb * per - imp.shape[-1])))
    imp = imp.reshape(b, tq, NSA_KV_HEADS, n_sb, per).sum(-1)
    blk = jnp.arange(n_sb, dtype=jnp.int32)[None, :]
    cur = (q_pos // SEL_BLOCK)[:, None]
    causal_b = blk <= cur
    forced = (blk == 0) | (blk == cur) | (blk == cur - 1)
    score = jnp.where(forced[None, :, None, :], FORCE_SCORE,
                      jnp.where(causal_b[None, :, None, :], imp, -jnp.inf))
    _, idx = lax.top_k(score, N_SEL)
    idx = idx.transpose(0, 2, 1, 3)
    b_ix = jnp.arange(b)[:, None, None, None]
    k_ix = jnp.arange(NSA_KV_HEADS)[None, :, None, None]
    ks = ks_blk[b_ix, k_ix, idx]
    vs = vs_blk[b_ix, k_ix, idx]
    pos_s = idx[..., None] * SEL_BLOCK + jnp.arange(SEL_BLOCK, dtype=jnp.int32)
    dist_s = q_pos[None, None, :, None, None] - pos_s
    qt = qg.transpose(0, 2, 1, 3, 4)
    s_s = jnp.einsum('bktgd,bktsld->bktgsl', qt, ks).astype(jnp.float32) * ATT_SCALE
    s_s = s_s - sl[None, :, None, :, None, None] * dist_s[:, :, :, None].astype(jnp.float32)
    s_s = jnp.where((dist_s >= 0)[:, :, :, None], s_s, NEG_INF)
    p_s = jax.nn.softmax(s_s.reshape(*s_s.shape[:4], N_SEL * SEL_BLOCK), axis=-1).reshape(s_s.shape)
    o_s = jnp.einsum('bktgsl,bktsld->bktgd', p_s.astype(vs.dtype), vs).transpose(0, 2, 1, 3, 4)
    s_w = jnp.einsum('btkgd,blkd->btkgl', qg, kw).astype(jnp.float32) * ATT_SCALE
    dist_w = q_pos[:, None] - w_pos[None, :]
    vis_w = (dist_w >= 0) & (dist_w <= WINDOW) & (w_pos >= 0)[None, :]
    s_w = s_w - sl[None, None, :, :, None] * dist_w.astype(jnp.float32)[None, :, None, None, :]
    s_w = jnp.where(vis_w[None, :, None, None, :], s_w, NEG_INF)
    p_w = jax.nn.softmax(s_w, axis=-1)
    o_w = jnp.einsum('btkgl,blkd->btkgd', p_w.astype(vw.dtype), vw)
    gg = gates.reshape(b, tq, NSA_KV_HEADS, NSA_GROUP, 3)
    o = gg[..., 0:1] * o_c + gg[..., 1:2] * o_s + gg[..., 2:3] * o_w
    return o.reshape(b, tq, NSA_DIM)


def nsa_prompt(q, gates, ctx, win_rows, slopes):
    ck, cv, c_end, ks_blk, vs_blk = ctx
    b, t = q.shape[:2]
    nqb = t // Q_BLOCK
    win_pad = jnp.pad(win_rows, ((0, 0), (WINDOW, 0), (0, 0), (0, 0), (0, 0)))
    q_b = q.reshape(b, nqb, Q_BLOCK, NSA_HEADS, HEAD_DIM).swapaxes(0, 1)
    g_b = gates.reshape(b, nqb, Q_BLOCK, NSA_HEADS, 3).swapaxes(0, 1)

    def one_block(args):
        i, qb, gb = args
        start = i * Q_BLOCK
        q_pos = start + jnp.arange(Q_BLOCK, dtype=jnp.int32)
        wb = lax.dynamic_slice_in_dim(win_pad, start, WINDOW + Q_BLOCK, axis=1)
        w_pos = start - WINDOW + jnp.arange(WINDOW + Q_BLOCK, dtype=jnp.int32)
        return nsa_block(qb, gb, q_pos, ck, cv, c_end, ks_blk, vs_blk, wb[:, :, 0], wb[:, :, 1], w_pos, slopes)

    out = lax.map(one_block, (jnp.arange(nqb, dtype=jnp.int32), q_b, g_b))
    return out.swapaxes(0, 1).reshape(b, t, NSA_DIM)


def setup_inputs(seed: int = 0) -> dict:
    key = jax.random.key(seed)
    ks = jax.random.split(key, 32)
    f32 = jnp.float32
    n_pages = PAST_LEN // PAGE_SIZE
    n_pool = (5 * DEC_BATCH * n_pages + 3) // 4
    win_len = min(WINDOW, PAST_LEN)

    def nrm(k, shape, scale=1.0):
        return jax.random.normal(k, shape, f32) * scale

    def gain(k, shape):
        return 1.0 + 0.02 * jax.random.normal(k, shape, f32)

    page_table = jax.random.permutation(ks[0], n_pool)[: DEC_BATCH * n_pages].reshape(DEC_BATCH, n_pages).astype(jnp.int32)
    return {
        'x_prompt': nrm(ks[1], (BATCH, SEQ, D_MODEL)),
        'x_sample': nrm(ks[2], (DEC_BATCH, DEC_SEQ, D_MODEL)),
        'state_conv': nrm(ks[3], (N_A_LAYERS, DEC_BATCH, CONV_WIDTH - 1, CONV_DIM)),
        'cache_mem_kv': nrm(ks[4], (DEPTH, DEC_BATCH, N_MEM, 2, MEM_HEADS, HEAD_DIM)),
        'cache_cmp_kv': nrm(ks[5], (n_pool, PAGE_SIZE, 2, NSA_KV_HEADS, HEAD_DIM)),
        'cache_slc_kv': nrm(ks[6], (n_pool, PAGE_SIZE, 2, NSA_KV_HEADS, HEAD_DIM)),
        'state_win_kv': nrm(ks[7], (DEC_BATCH, win_len, 2, NSA_KV_HEADS, HEAD_DIM)),
        'page_table': page_table,
        'mem_prompt': nrm(ks[8], (BATCH, N_MEM, D_MODEL)),
        'g_mix': gain(ks[9], (DEPTH, D_MODEL)),
        'w_in_a': nrm(ks[10], (N_A_LAYERS, D_MODEL, 3 * CONV_DIM + MEM_DIM), D_MODEL ** -0.5),
        'conv_w': nrm(ks[11], (N_A_LAYERS, CONV_WIDTH, CONV_DIM), CONV_WIDTH ** -0.5),
        'w_in_b': nrm(ks[12], (N_B_LAYERS, D_MODEL, NSA_DIM + 3 * NSA_HEADS + MEM_DIM), D_MODEL ** -0.5),
        'w_o': nrm(ks[13], (DEPTH, MIX_WIDTH, D_MODEL), MIX_WIDTH ** -0.5),
        'w_mkv': nrm(ks[14], (DEPTH, D_MODEL, 2 * MEM_DIM), D_MODEL ** -0.5),
        'g_mem': gain(ks[15], (D_MODEL,)),
        'g_kv': gain(ks[16], (D_MODEL,)),
        'w_kv': nrm(ks[17], (D_MODEL, 6 * KV_DIM), D_MODEL ** -0.5),
        'pe_ck': nrm(ks[18], (CMP_LEN, HEAD_DIM), 0.1),
        'w1_ck': nrm(ks[19], (CMP_LEN, HEAD_DIM, CMP_HID), (CMP_LEN * HEAD_DIM) ** -0.5),
        'w2_ck': nrm(ks[20], (CMP_HID, HEAD_DIM), CMP_HID ** -0.5),
        'pe_cv': nrm(ks[21], (CMP_LEN, HEAD_DIM), 0.1),
        'w1_cv': nrm(ks[22], (CMP_LEN, HEAD_DIM, CMP_HID), (CMP_LEN * HEAD_DIM) ** -0.5),
        'w2_cv': nrm(ks[23], (CMP_HID, HEAD_DIM), CMP_HID ** -0.5),
        'g_ffn': gain(ks[24], (DEPTH, D_MODEL)),
        'w_gu': nrm(ks[25], (DEPTH, D_MODEL, 2 * D_FF), D_MODEL ** -0.5),
        'w_dn': nrm(ks[26], (DEPTH, D_FF, D_MODEL), D_FF ** -0.5),
        'g_final': gain(ks[27], (D_MODEL,)),
    }


def reference(x_prompt, x_sample, state_conv, cache_mem_kv, cache_cmp_kv, cache_slc_kv, state_win_kv,
              page_table, mem_prompt, g_mix, w_in_a, conv_w, w_in_b, w_o, w_mkv, g_mem, g_kv, w_kv,
              pe_ck, w1_ck, w2_ck, pe_cv, w1_cv, w2_cv, g_ffn, w_gu, w_dn, g_final):
    slopes = jnp.asarray(np.array(_alibi_list(NSA_HEADS), dtype=np.float32))
    bp, tp = x_prompt.shape[:2]
    bs, ts = x_sample.shape[:2]
    n_mem = mem_prompt.shape[1]
    win_len = state_win_kv.shape[1]
    mem_kv_p = jnp.einsum('bmd,lde->lbme', rmsnorm(mem_prompt, g_mem), w_mkv).reshape(
        DEPTH, bp, n_mem, 2, MEM_HEADS, HEAD_DIM)
    xp, xs = x_prompt, x_sample
    conv_p, conv_s = [], []
    for l in range(DEPTH):
        mk_p, mv_p = mem_kv_p[l, :, :, 0], mem_kv_p[l, :, :, 1]
        mk_s, mv_s = cache_mem_kv[l, :, :, 0], cache_mem_kv[l, :, :, 1]
        hp = rmsnorm(xp, g_mix[l])
        hs = rmsnorm(xs, g_mix[l])
        if l < N_A_LAYERS:
            zero_state = jnp.zeros((bp, CONV_WIDTH - 1, CONV_DIM), xp.dtype)
            yp, st_p = mix_a(hp, zero_state, mk_p, mv_p, w_in_a[l], conv_w[l])
            ys, st_s = mix_a(hs, state_conv[l], mk_s, mv_s, w_in_a[l], conv_w[l])
            conv_p.append(st_p)
            conv_s.append(st_s)
        else:
            if l == N_A_LAYERS:
                kv_p = shared_kv(xp, g_kv, w_kv)
                kv_s = shared_kv(xs, g_kv, w_kv)
                cmp_new_p, slc_new_p, win_new_p = kv_p[:, :, 0], kv_p[:, :, 1], kv_p[:, :, 2]
                cmp_new_s, slc_new_s, win_new_s = kv_s[:, :, 0], kv_s[:, :, 1], kv_s[:, :, 2]
                full_cmp_s = jnp.concatenate([gather_pages(cache_cmp_kv, page_table).astype(cmp_new_s.dtype), cmp_new_s], axis=1)
                full_slc_s = jnp.concatenate([gather_pages(cache_slc_kv, page_table).astype(slc_new_s.dtype), slc_new_s], axis=1)
                ctx_p = nsa_context(cmp_new_p, slc_new_p, pe_ck, w1_ck, w2_ck, pe_cv, w1_cv, w2_cv)
                ctx_s = nsa_context(full_cmp_s, full_slc_s, pe_ck, w1_ck, w2_ck, pe_cv, w1_cv, w2_cv)
                win_full_s = jnp.concatenate([state_win_kv.astype(win_new_s.dtype), win_new_s], axis=1)
                q_pos_s = PAST_LEN + jnp.arange(ts, dtype=jnp.int32)
                w_pos_s = PAST_LEN - win_len + jnp.arange(win_len + ts, dtype=jnp.int32)
            j = l - N_A_LAYERS
            qp, gp, mqp = project_b(hp, w_in_b[j])
            qs, gs, mqs = project_b(hs, w_in_b[j])
            yp = jnp.concatenate([nsa_prompt(qp, gp, ctx_p, win_new_p, slopes),
                                  mem_attention(mqp, mk_p, mv_p)], axis=-1)
            ys = jnp.concatenate([nsa_block(qs, gs, q_pos_s, *ctx_s, win_full_s[:, :, 0], win_full_s[:, :, 1], w_pos_s, slopes),
                                  mem_attention(mqs, mk_s, mv_s)], axis=-1)
        xp = xp + yp @ w_o[l]
        xs = xs + ys @ w_o[l]
        xp = xp + swiglu(rmsnorm(xp, g_ffn[l]), w_gu[l], w_dn[l])
        xs = xs + swiglu(rmsnorm(xs, g_ffn[l]), w_gu[l], w_dn[l])
    y_prompt = rmsnorm(xp, g_final)
    y_sample = rmsnorm(xs, g_final)
    conv_state_p = jnp.stack(conv_p)
    conv_state_s = jnp.stack(conv_s)
    win_kv_p = win_new_p[:, tp - min(WINDOW, tp):]
    win_kv_s = win_full_s[:, ts:]
    return (y_prompt, y_sample, conv_state_p, conv_state_s, mem_kv_p, cmp_new_p, slc_new_p, win_kv_p, cmp_new_s, slc_new_s, win_kv_s)
```

```python
import contextlib
import numpy as np
import concourse.bass as bass
import concourse.mybir as mybir
from concourse.bass_utils import run_bass_kernel_spmd

F32 = mybir.dt.float32
BF16 = mybir.dt.bfloat16
I32 = mybir.dt.int32
AF = mybir.ActivationFunctionType
ALU = mybir.AluOpType
AX = mybir.AxisListType

ENGS = ("pe", "act", "dve", "pool", "sp")
DMA_K = 8
EPOCH = 8192

D = 1024
SEQ = 4096
NS = 32
NTOK = SEQ + NS
DFF = 2816
CONV = 768
T = 512
NT = SEQ // T
EPS = 1e-6
import os
DBG = int(os.environ.get('DBG', '9'))
DBGOUT = int(os.environ.get('DBGOUT', '0'))
SAMPLE_NSA = int(os.environ.get('SAMPLE_NSA', '1'))


class Buf:
    __slots__ = ("name", "w", "r", "rp", "excl")

    def __init__(self, name="", excl=False):
        self.name = name
        self.w = {}
        self.r = {}
        self.rp = {}
        self.excl = excl


class Emitter:
    def __init__(self, nc):
        self.nc = nc
        self.q = {e: [] for e in ENGS}
        self.cnt = {e: 0 for e in ENGS}
        self.waited = {e: {} for e in ENGS}
        self.dma_n = {e: 0 for e in ENGS}
        self.semkeys = set()

    def _need(self, eng, ev, waits):
        key, val = ev
        if key == ("e", "pe") and eng == "pe":
            return
        if self.waited[eng].get(key, 0) >= val:
            return
        waits[key] = max(waits.get(key, 0), val)

    def _deps(self, eng, reads, writes, par=False):
        waits = {}
        for b in reads:
            for k, v in b.w.items():
                self._need(eng, (k, v), waits)
        for b in writes:
            if not par:
                for k, v in b.w.items():
                    self._need(eng, (k, v), waits)
            else:
                for k, v in b.rp.items():
                    self._need(eng, (k, v), waits)
            for k, v in b.r.items():
                self._need(eng, (k, v), waits)
        for k, v in waits.items():
            self.waited[eng][k] = v
        return list(waits.items())

    def _mark(self, ev, reads, writes, par=False):
        k, v = ev
        for b in reads:
            if b.r.get(k, 0) < v:
                b.r[k] = v
        for b in writes:
            if par:
                b.w[k] = max(b.w.get(k, 0), v)
                for kk, vv in b.r.items():
                    b.rp[kk] = max(b.rp.get(kk, 0), vv)
            else:
                b.w = {k: v}
                b.rp = dict(b.r)
            b.r = {}

    def op(self, eng, fn, reads=(), writes=(), signal=True):
        writes = list(writes) + [b for b in reads if b.excl]
        reads = [b for b in reads if not b.excl]
        waits = self._deps(eng, reads, writes)
        if signal:
            self.cnt[eng] += 1
            ev = (("e", eng), self.cnt[eng])
            inc = (ev[0], ev[1], 1)
        else:
            ev = (("e", eng), self.cnt[eng] + 1)
            inc = None
        self.semkeys.add(ev[0])
        self.q[eng].append((waits, fn, inc))
        self._mark(ev, reads, writes)
        return ev

    def dma(self, eng, fn, reads=(), writes=(), par=False):
        n = self.dma_n[eng]
        self.dma_n[eng] += 1
        j, m = n % DMA_K, n // DMA_K
        key = ("d", eng, j)
        self.semkeys.add(key)
        waits = dict(self._deps(eng, reads, writes, par))
        if m > 0 and self.waited[eng].get(key, 0) < 16 * m:
            waits[key] = max(waits.get(key, 0), 16 * m)
            self.waited[eng][key] = waits[key]
        ev = (key, 16 * (m + 1))
        self.q[eng].append((list(waits.items()), fn, (key, ev[1], 16)))
        self._mark(ev, reads, writes, par)
        return ev

    def barrier(self):
        evs = []
        for eng in ENGS:
            n = self.dma_n[eng]
            for j in range(min(n, DMA_K)):
                cnt = (n - j + DMA_K - 1) // DMA_K
                evs.append((("d", eng, j), 16 * cnt))
            if self.cnt[eng] > 0:
                evs.append((("e", eng), self.cnt[eng]))
        for eng in ENGS:
            waits = {}
            for k, v in evs:
                if k == ("e", eng):
                    continue
                if self.waited[eng].get(k, 0) < v:
                    waits[k] = v
                    self.waited[eng][k] = v
            if waits:
                self.q[eng].append((list(waits.items()), None, None))

    def finish(self):
        waits = []
        for eng in ENGS:
            n = self.dma_n[eng]
            for j in range(min(n, DMA_K)):
                cnt = (n - j + DMA_K - 1) // DMA_K
                waits.append((("d", eng, j), 16 * cnt))
        for eng in ENGS:
            if eng != "sp" and self.cnt[eng] > 0:
                waits.append((("e", eng), self.cnt[eng]))
        self.q["sp"].append((waits, None, None))

    def emit(self):
        nc = self.nc
        with contextlib.ExitStack() as st:
            sems = {}
            for k in sorted(self.semkeys, key=str):
                if k[0] == "e":
                    for ep in range((self.cnt[k[1]] + EPOCH - 1) // EPOCH):
                        sems[(k, ep)] = st.enter_context(nc.semaphore("s_e_%s_%d" % (k[1], ep)))
                else:
                    sems[k] = st.enter_context(nc.semaphore("s_" + "_".join(str(x) for x in k)))

            def semval(k, v):
                if k[0] == "e":
                    ep = (v - 1) // EPOCH
                    return sems[(k, ep)], v - ep * EPOCH
                return sems[k], v

            block = st.enter_context(nc.Block())

            def runner(ops):
                def run(e):
                    for waits, fn, inc in ops:
                        for k, v in waits:
                            sm, vv = semval(k, v)
                            e.wait_ge(sm, vv)
                        if fn is not None:
                            ins = fn(e)
                            if inc is not None:
                                k, v, amt = inc
                                ins.then_inc(semval(k, v)[0], amt)
                return run

            if self.q["pe"]:
                block.tensor(runner(self.q["pe"]))
            if self.q["act"]:
                block.scalar(runner(self.q["act"]))
            if self.q["dve"]:
                block.vector(runner(self.q["dve"]))
            if self.q["pool"]:
                block.gpsimd(runner(self.q["pool"]))
            if self.q["sp"]:
                block.sync(runner(self.q["sp"]))


class TileDesc:
    def __init__(self, idx):
        self.idx = idx
        self.sample = idx == NT
        if self.sample:
            self.row0, self.T, self.blocks = SEQ, NS, [(0, NS)]
        else:
            self.row0, self.T, self.blocks = idx * T, T, [(i * 128, 128) for i in range(4)]


def build_program(stop_after=None):
    nc = bass.Bass("TRN2", target_bir_lowering=False)
    em = Emitter(nc)

    def din(name, shape, dt=F32):
        return nc.dram_tensor(name, list(shape), dt, kind="ExternalInput").ap()

    def dout(name, shape, dt=F32):
        return nc.dram_tensor(name, list(shape), dt, kind="ExternalOutput").ap()

    def dscr(name, shape, dt=F32):
        return nc.dram_tensor(name, list(shape), dt, kind=("ExternalOutput" if DBGOUT else "Internal")).ap()

    x_in = din("x_in", [NTOK, D])
    stc = din("stc", [2, 8, CONV])
    cmem = din("cmem", [4, 4, 256, 512])
    memp = din("memp", [256, D])
    g_mix = din("g_mix", [4, D])
    w_in_a = din("w_in_a", [2, D, 2560])
    conv_w = din("conv_w", [2, 3, CONV])
    w_o = din("w_o", [4, D, D])
    w_mkv = din("w_mkv", [4, D, 512])
    g_mem = din("g_mem", [1, D])
    g_kv = din("g_kv", [1, D])
    w_kv = din("w_kv", [D, 1536])
    g_ffn = din("g_ffn", [4, D])
    w_gu = din("w_gu", [4, D, 2 * DFF])
    w_dn = din("w_dn", [4, DFF, D])
    g_final = din("g_final", [1, D])
    win_in = din("win_in", [4, 512, 512])

    y_out = dout("y_out", [NTOK, D])
    cs_p = dout("cs_p", [2, 2, CONV])
    cs_s = dout("cs_s", [2, 8, CONV])
    mkv_p = dout("mkv_p", [4, 256, 512])
    cmp_o = dout("cmp_o", [NTOK, 512])
    slc_o = dout("slc_o", [NTOK, 512])
    winp_o = dout("winp_o", [512, 512])
    wins_o = dout("wins_o", [4, 512, 512])

    w_in_b = din("w_in_b", [2, D, 1060])
    w1_c = [din("w1_ck", [32, 64, 128]), din("w1_cv", [32, 64, 128])]
    w2_c = [din("w2_ck", [128, 64]), din("w2_cv", [128, 64])]
    pe_c = [din("pe_ck", [32, 64]), din("pe_cv", [32, 64])]
    kaug_t = din("kaug_t", [6, 4096])
    kaug_c = din("kaug_c", [6, 256])
    qaug_t = din("qaug_t", [12, 6, 4096])
    frc_t = din("frc_t", [4096, 64])
    gm_t = din("gm_t", [128, 32])
    pool_c = din("pool_c", [2560 * 128, 512])
    pool_s = din("pool_s", [2560 * 128, 512])
    ptab = din("ptab", [4, 64], I32)
    kaug_s = din("kaug_s", [6, 8320])
    kaug_cs = din("kaug_cs", [6, 512])
    kaug_ws = din("kaug_ws", [6, 640])
    qaug_s = din("qaug_s", [6, 384])
    frc_s = din("frc_s", [8, 130])
    LS, LW, LC = 8320, 640, 8208
    XCS = dscr("xcs", [4, 64, 8 * LC], BF16)
    KSS = dscr("kss", [4, 64, 4 * LS], BF16)
    VSS = dscr("vss", [4, LS, 260], BF16)
    KWS = dscr("kws", [4, 64, 4 * LW], BF16)
    VWS = dscr("vws", [4, LW, 260], BF16)
    CKS = dscr("cks", [4, 64, 4 * 512], BF16)
    CVS = dscr("cvs", [4, 512, 260], BF16)
    KS_T = dscr("ks_t", [64, 4 * NTOK], BF16)
    KW_T = dscr("kw_t", [64, 4 * NTOK], BF16)
    XC_T = dscr("xc_t", [64, 8 * NTOK], BF16)
    VS_S = dscr("vs_s", [NTOK, 260], BF16)
    VW_S = dscr("vw_s", [NTOK, 260], BF16)
    CKT_S = dscr("ckt_s", [64, 4 * 256], BF16)
    CV_S = dscr("cv_s", [256, 260], BF16)
    XS = [dscr("xs%d" % i, [NTOK, D]) for i in range(3)]
    HTS = dscr("hts", [NT + 1, 128, 8 * T], BF16)
    WINP = dscr("winp_s", [NTOK, 512])
    xbufs = {}

    def xb(key, ti):
        return xbufs.setdefault((key, ti), Buf("x%s_%d" % (key, ti)))

    tiles = [TileDesc(i) for i in range(NT + 1)]

    with contextlib.ExitStack() as st:
        def sb(name, shape, dt):
            return st.enter_context(nc.sbuf_tensor(name, list(shape), dt))

        WA = sb("WA", [128, 33792], BF16)
        XT = [sb("XT0", [128, 4, D], F32), None]
        HT = [sb("HT0", [128, 8, T], BF16), None]
        Hh = sb("Hh", [128, 4, D], BF16)
        BA = sb("BA", [128, 11 * T], BF16)
        CS = [sb("CS%d" % i, [128, T], F32) for i in range(2)]
        TM = [sb("TM%d" % i, [128, T], F32) for i in range(2)]
        STG = [sb("STG%d" % i, [128, 512], F32) for i in range(2)]
        GB = sb("GB", [128, D], F32)
        IDB = sb("IDB", [128, 128], BF16)
        IDF = sb("IDF", [128, 128], F32)
        IOT = sb("IOT", [128, 128], I32)
        ONE = sb("ONE", [128, 128], BF16)
        MKT = sb("MKT", [128, 4, 2, 256], BF16)
        VP = sb("VP", [128, 2, 4, 256], BF16)
        SKT = sb("SKT", [128, 4, 2, 256], BF16)
        SKS = sb("SKS", [128, 2, 4, 256], BF16)
        SV = sb("SV", [128, 2, 4, 256], BF16)
        SS = sb("SS", [128, 8], F32)
        JK = sb("JK", [128, D], BF16)
        scope_n = [0]
        addr_chk = {}

        class Scope:
            def __init__(self, xt1=True, ht1=True):
                self.xt1, self.ht1 = xt1, ht1

            def __enter__(self):
                self.st = contextlib.ExitStack()
                self.st.__enter__()
                scope_n[0] += 1
                if self.xt1:
                    XT[1] = self.sb("XT1", [128, 4, D], F32)
                    a = nc.lookup_mloc(XT[1]).addr
                    assert addr_chk.setdefault("xt1", a) == a
                    if self.ht1:
                        HT[1] = self.sb("HT1", [128, 8, T], BF16)
                        a = nc.lookup_mloc(HT[1]).addr
                        assert addr_chk.setdefault("ht1", a) == a
                return self

            def sb(self, name, shape, dt):
                return self.st.enter_context(nc.sbuf_tensor("%s_s%d" % (name, scope_n[0]), list(shape), dt))

            def __exit__(self, *a):
                em.barrier()
                self.st.__exit__(None, None, None)
                return False

        PT = [st.enter_context(nc.psum_tensor("PT%d" % i, [128, 1024], BF16)) for i in range(2)]
        PS = [st.enter_context(nc.psum_tensor("PS%d" % i, [128, 512], F32)) for i in range(6)]
        bPT = [Buf("PT%d" % i, True) for i in range(2)]
        bPS = [Buf("PS%d" % i, True) for i in range(6)]
        rr = {"ps": 0, "pt": 0, "cs": 0, "tm": 0, "stg": 0, "ss": 0}

        def nxt(kind, n):
            i = rr[kind]
            rr[kind] = (i + 1) % n
            return i

        def ps_next():
            i = nxt("ps", 6)
            return PS[i], bPS[i]

        def pt_next():
            i = nxt("pt", 2)
            return PT[i], bPT[i]

        bWA = Buf("WA")
        bXT = [Buf("XT0"), Buf("XT1")]
        bHT = [Buf("HT0"), Buf("HT1")]
        bHh = [Buf("Hh%d" % i) for i in range(4)]
        bBA = [Buf("BA%d" % i) for i in range(11)]
        bUT = [Buf("UT%d" % i) for i in range(6)]
        bCS = [Buf("CS0"), Buf("CS1")]
        bTM = [Buf("TM0"), Buf("TM1")]
        bSTG = [Buf("STG0"), Buf("STG1")]
        bGB = Buf("GB")
        bC = Buf("consts")
        bMKT = Buf("MKT")
        bVP = Buf("VP")
        bSKT = Buf("SKT")
        bSKS = Buf("SKS")
        bSV = Buf("SV")
        bSS = [Buf("SS%d" % i) for i in range(8)]
        bJK = Buf("JK")
        bST8 = Buf("ST8")
        bSO8 = Buf("SO8")
        bOUT = Buf("outs")

        em.op("pool", lambda e: e.iota(IOT[:, :], [[-1, 128]], base=0, channel_multiplier=1), writes=[bC])
        em.op("dve", lambda e: e.tensor_single_scalar(out=IDF[:, :], in_=IOT[:, :], scalar=0, op=ALU.is_equal),
              reads=[bC], writes=[bC])
        em.op("dve", lambda e: e.tensor_copy(out=IDB[:, :], in_=IDF[:, :]), reads=[bC], writes=[bC])
        em.op("pool", lambda e: e.memset(ONE[:, :], 1.0), writes=[bC])
        def load_gain(g_ap_row):
            em.dma("sp", lambda e: e.dma_start(out=GB[:, :], in_=g_ap_row.partition_broadcast(128)), writes=[bGB])

        def load_w(dst_off, src, kch, ncols, col0=0, first=True):
            step = 2 if kch % 2 == 0 else 1
            for k0 in range(0, kch, step):
                def f(e, k0=k0):
                    dst = WA[:, dst_off + k0 * ncols: dst_off + (k0 + step) * ncols].rearrange(
                        "p (k c) -> p k c", k=step)
                    s_ = src[k0 * 128:(k0 + step) * 128, col0:col0 + ncols].rearrange("(k p) c -> p k c", p=128)
                    return e.dma_start(out=dst, in_=s_)
                em.dma("pool", f, writes=[bWA], par=not (first and k0 == 0))

        def wa(off, k, ncols, c0, n):
            return WA[:, off + k * ncols + c0: off + k * ncols + c0 + n]

        def load_x(src, key, td, slot):
            if td.sample:
                em.dma("sp", lambda e: e.dma_start(out=XT[slot][0:NS, 0, :], in_=src[SEQ:SEQ + NS, :]),
                       reads=[xb(key, td.idx)], writes=[bXT[slot]])
            else:
                em.dma("sp", lambda e: e.dma_start(
                    out=XT[slot][:, :, :], in_=src[td.row0:td.row0 + T, :].rearrange("(b p) d -> p b d", p=128)),
                    reads=[xb(key, td.idx)], writes=[bXT[slot]])

        def store_x(dst, key, td, slot):
            if td.sample:
                em.dma("sp", lambda e: e.dma_start(out=dst[SEQ:SEQ + NS, :], in_=XT[slot][0:NS, 0, :]),
                       reads=[bXT[slot]], writes=[xb(key, td.idx)])
            else:
                em.dma("sp", lambda e: e.dma_start(
                    out=dst[td.row0:td.row0 + T, :].rearrange("(b p) d -> p b d", p=128), in_=XT[slot][:, :, :]),
                    reads=[bXT[slot]], writes=[xb(key, td.idx)])

        def norm_T(td, xslot, hslot, src_t=None, src_b=None):
            xt = XT[xslot] if src_t is None else src_t
            bx = bXT[xslot] if src_b is None else src_b
            for bi, (r0, nr) in enumerate(td.blocks):
                si = nxt("ss", 8)
                em.op("pool", lambda e, si=si: e.memset(SS[:, si:si + 1], 0.0), writes=[bSS[si]])
                em.op("act", lambda e, bi=bi, nr=nr, si=si: e.activation(
                    out=JK[0:nr, :], in_=xt[0:nr, bi, :], func=AF.Square, accum_out=SS[0:nr, si:si + 1]),
                    reads=[bx], writes=[bJK, bSS[si]])
                em.op("dve", lambda e, nr=nr, si=si: e.tensor_scalar(
                    out=SS[0:nr, si:si + 1], in0=SS[0:nr, si:si + 1], scalar1=1.0 / D, scalar2=EPS,
                    op0=ALU.mult, op1=ALU.add), reads=[bSS[si]], writes=[bSS[si]])
                em.op("act", lambda e, nr=nr, si=si: e.sqrt(out=SS[0:nr, si:si + 1], in_=SS[0:nr, si:si + 1]),
                      reads=[bSS[si]], writes=[bSS[si]])
                em.op("dve", lambda e, nr=nr, si=si: e.reciprocal(out=SS[0:nr, si:si + 1], in_=SS[0:nr, si:si + 1]),
                      reads=[bSS[si]], writes=[bSS[si]])
                em.op("dve", lambda e, bi=bi, nr=nr, si=si: e.scalar_tensor_tensor(
                    out=Hh[0:nr, bi, :], in0=xt[0:nr, bi, :], scalar=SS[0:nr, si:si + 1], in1=GB[0:nr, :],
                    op0=ALU.mult, op1=ALU.mult), reads=[bx, bSS[si], bGB], writes=[bHh[bi]])
            TT = td.T
            for kk in range(4):
                pt, bpt = pt_next()
                nb = len(td.blocks)
                for k2 in range(2):
                    k = kk * 2 + k2
                    for bi, (r0, nr) in enumerate(td.blocks):
                        last = (k2 == 1 and bi == nb - 1)
                        em.op("pe", lambda e, k=k, k2=k2, bi=bi, r0=r0, nr=nr, pt=pt: e.transpose(
                            pt[:, k2 * 512 + r0: k2 * 512 + r0 + nr], Hh[0:nr, bi, k * 128:(k + 1) * 128],
                            IDB[0:nr, 0:nr]), reads=[bHh[bi], bC], writes=[bpt], signal=last)
                eng = "act" if kk % 2 == 0 else "dve"
                if eng == "act":
                    em.op("act", lambda e, kk=kk, pt=pt: e.copy(
                        out=HT[hslot][:, 2 * kk:2 * kk + 2, 0:TT],
                        in_=pt[:, :].rearrange("p (a t) -> p a t", a=2)[:, :, 0:TT]),
                        reads=[bpt], writes=[bHT[hslot]])
                else:
                    em.op("dve", lambda e, kk=kk, pt=pt: e.tensor_copy(
                        out=HT[hslot][:, 2 * kk:2 * kk + 2, 0:TT],
                        in_=pt[:, :].rearrange("p (a t) -> p a t", a=2)[:, :, 0:TT]),
                        reads=[bpt], writes=[bHT[hslot]])

        def mm_group(out_ap, bout, pairs, extra_reads):
            n = len(pairs)
            for i, (l_, r_) in enumerate(pairs):
                em.op("pe", lambda e, l_=l_, r_=r_, i=i: e.matmul(out_ap, l_, r_, start=(i == 0), stop=(i == n - 1)),
                      reads=extra_reads, writes=[bout], signal=(i == n - 1))

        def phase_mem():
            load_gain(g_mem[0:1, :])
            for l in range(4):
                load_w(l * 4096, w_mkv[l], 8, 512, first=(l == 0))
            em.dma("sp", lambda e: e.dma_start(out=XT[0][:, 0:2, :], in_=memp.rearrange("(b p) d -> p b d", p=128)),
                   writes=[bXT[0]])
            td = TileDesc(0)
            td.T, td.blocks = 256, [(0, 128), (128, 128)]
            if DBG >= 1:
                norm_T(td, 0, 0)
            for l in range(4 if DBG >= 2 else 0):
                for blk in range(2):
                    ps, bps = ps_next()
                    mm_group(ps[:, :], bps, [(HT[0][:, k, blk * 128:(blk + 1) * 128], wa(l * 4096, k, 512, 0, 512))
                                             for k in range(8)], [bHT[0], bWA])
                    si = nxt("stg", 2)
                    em.op("act", lambda e, ps=ps, si=si: e.copy(out=STG[si][:, :], in_=ps[:, :]),
                          reads=[bps], writes=[bSTG[si]])
                    em.op("dve", lambda e, ps=ps, blk=blk, l=l: e.tensor_copy(out=VP[:, blk, l, :], in_=ps[:, 256:512]),
                          reads=[bps], writes=[bVP])
                    em.dma("sp", lambda e, si=si, l=l, blk=blk: e.dma_start(
                        out=mkv_p[l, blk * 128:(blk + 1) * 128, :], in_=STG[si][:, :]), reads=[bSTG[si]], writes=[])
                for hp in range(2 if DBG >= 3 else 0):
                    ps, bps = ps_next()
                    mm_group(ps[:, 0:256], bps, [(wa(l * 4096, k, 512, hp * 128, 128), HT[0][:, k, 0:256])
                                                 for k in range(8)], [bHT[0], bWA])
                    em.op("act", lambda e, ps=ps, l=l, hp=hp: e.copy(out=MKT[:, l, hp, :], in_=ps[:, 0:256]),
                          reads=[bps], writes=[bMKT])

        def load_sample_mem(l):
            for s in range(4):
                em.dma("pool", lambda e, s=s: e.dma_start(
                    out=SV[:, :, s, :], in_=cmem[l, s, :, 256:512].rearrange("(c p) f -> p c f", p=128)), writes=[bSV])
                em.dma("pool", lambda e, s=s: e.dma_start(
                    out=SKS[:, :, s, :], in_=cmem[l, s, :, 0:256].rearrange("(c p) f -> p c f", p=128)), writes=[bSKS])
            for s in range(4):
                pt, bpt = pt_next()
                for hp in range(2):
                    for c in range(2):
                        em.op("pe", lambda e, s=s, hp=hp, c=c, pt=pt: e.transpose(
                            pt[:, hp * 256 + c * 128: hp * 256 + (c + 1) * 128],
                            SKS[:, c, s, hp * 128:(hp + 1) * 128], IDB[:, :]),
                            reads=[bSKS, bC], writes=[bpt], signal=(hp == 1 and c == 1))
                em.op("dve", lambda e, s=s, pt=pt: e.tensor_copy(
                    out=SKT[:, s, :, :], in_=pt[:, 0:512].rearrange("p (h m) -> p h m", h=2)),
                    reads=[bpt], writes=[bSKT])

        def mem_attention(td, l, q_ps, q_bufs):
            TT = td.T
            groups = [(0, TT, None)] if not td.sample else [(s * 8, 8, s) for s in range(4)]
            for hp in range(2):
                for (c0, n, s) in groups:
                    for half in range(2):
                        hs = slice(half * 64, half * 64 + 64)
                        for mc in range(2):
                            ps, bps = ps_next()
                            if s is None:
                                kT = MKT[hs, l, hp, mc * 128:(mc + 1) * 128]
                                kb = bMKT
                            else:
                                kT = SKT[hs, s, hp, mc * 128:(mc + 1) * 128]
                                kb = bSKT
                            mm_group(ps[:, 0:n], bps, [(kT, q_ps[hp][hs, c0:c0 + n])], [kb, q_bufs[hp]])
                            em.op("act", lambda e, ps=ps, mc=mc, n=n: e.activation(
                                out=BA[:, 8 * T + mc * T: 8 * T + mc * T + n], in_=ps[:, 0:n], func=AF.Exp, scale=0.125),
                                reads=[bps], writes=[bBA[8 + mc]])
                        psn, bpsn = ps_next()
                        psd, bpsd = ps_next()
                        if s is None:
                            vv = [VP[:, mc, l, hp * 128:(hp + 1) * 128] for mc in range(2)]
                            vb = bVP
                        else:
                            vv = [SV[:, mc, s, hp * 128:(hp + 1) * 128] for mc in range(2)]
                            vb = bSV
                        pp = [BA[:, 8 * T + mc * T: 8 * T + mc * T + n] for mc in range(2)]
                        mm_group(psn[:, 0:n], bpsn, [(vv[mc], pp[mc]) for mc in range(2)], [vb, bBA[8], bBA[9]])
                        mm_group(psd[:, 0:n], bpsd, [(ONE[:, :], pp[mc]) for mc in range(2)], [bC, bBA[8], bBA[9]])
                        ti = nxt("tm", 2)
                        em.op("dve", lambda e, psd=psd, ti=ti, n=n, hs=hs: e.reciprocal(out=TM[ti][hs, 0:n], in_=psd[hs, 0:n]),
                              reads=[bpsd], writes=[bTM[ti]])
                        em.op("dve", lambda e, psn=psn, ti=ti, n=n, hs=hs, hp=hp, c0=c0: e.tensor_tensor(
                            out=BA[hs, (6 + hp) * T + c0:(6 + hp) * T + c0 + n], in0=psn[hs, 0:n], in1=TM[ti][hs, 0:n],
                            op=ALU.mult), reads=[bpsn, bTM[ti]], writes=[bBA[6 + hp]])

        W_IN, W_O = 0, 8 * 2560

        def uwin(td, j, k):
            if td.sample:
                return UT[:, j, 0:40].rearrange("p (s t) -> p s t", t=10)[:, :, k:k + 8]
            return UT[:, j, k:k + T]

        def tv(td, ap):
            if td.sample:
                return ap.rearrange("p (s t) -> p s t", t=8)
            return ap

        def mixer_a_tile(td, l, xslot):
            TT = td.T
            norm_T(td, xslot, 0)
            hT = HT[0]
            qaps = []
            for hp in range(2):
                ps, bps = ps_next()
                mm_group(ps[:, 0:TT], bps, [(wa(W_IN, k, 2560, 2304 + hp * 128, 128), hT[:, k, 0:TT]) for k in range(8)],
                         [bHT[0], bWA])
                qaps.append(None)
                qap = (BA[:, 10 * T: 10 * T + TT] if hp == 0 else HT[1][:, 0, 0:TT])
                qb = bBA[10] if hp == 0 else bHT[1]
                em.op("act", lambda e, ps=ps, qap=qap: e.copy(out=qap, in_=ps[:, 0:TT]), reads=[bps], writes=[qb])
                qaps[hp] = (qap, qb)
            mem_attention(td, l, [qaps[0][0], qaps[1][0]], [qaps[0][1], qaps[1][1]])
            for j in range(6):
                psC, bC_ = ps_next()
                mm_group(psC[:, 0:TT], bC_, [(wa(W_IN, k, 2560, CONV + j * 128, 128), hT[:, k, 0:TT]) for k in range(8)],
                         [bHT[0], bWA])
                psH, bH_ = ps_next()
                mm_group(psH[:, 0:TT], bH_, [(wa(W_IN, k, 2560, 2 * CONV + j * 128, 128), hT[:, k, 0:TT]) for k in range(8)],
                         [bHT[0], bWA])
                psB, bB_ = ps_next()
                mm_group(psB[:, 0:TT], bB_, [(wa(W_IN, k, 2560, j * 128, 128), hT[:, k, 0:TT]) for k in range(8)],
                         [bHT[0], bWA])
                ci = nxt("cs", 2)
                em.op("act", lambda e, psC=psC, ci=ci: e.copy(out=CS[ci][:, 0:TT], in_=psC[:, 0:TT]),
                      reads=[bC_], writes=[bCS[ci]])
                em.op("dve", lambda e, psH=psH, ci=ci, j=j: e.tensor_tensor(
                    out=uwin(td, j, 2), in0=tv(td, psH[:, 0:TT]), in1=tv(td, CS[ci][:, 0:TT]), op=ALU.mult),
                    reads=[bH_, bCS[ci]], writes=[bUT[j]])
                ti = nxt("tm", 2)
                em.op("pool", lambda e, ti=ti, j=j: e.tensor_scalar(
                    out=tv(td, TM[ti][:, 0:TT]), in0=uwin(td, j, 0), scalar1=CW[:, l, j, 0:1], scalar2=None,
                    op0=ALU.mult), reads=[bUT[j], bC], writes=[bTM[ti]])
                for k in (1, 2):
                    em.op("dve", lambda e, ti=ti, j=j, k=k: e.scalar_tensor_tensor(
                        out=tv(td, TM[ti][:, 0:TT]), in0=uwin(td, j, k), scalar=CW[:, l, j, k:k + 1],
                        in1=tv(td, TM[ti][:, 0:TT]), op0=ALU.mult, op1=ALU.add),
                        reads=[bUT[j], bC, bTM[ti]], writes=[bTM[ti]])
                em.op("dve", lambda e, psB=psB, ti=ti, j=j: e.tensor_tensor(
                    out=BA[:, j * T: j * T + TT], in0=psB[:, 0:TT], in1=TM[ti][:, 0:TT], op=ALU.mult),
                    reads=[bB_, bTM[ti]], writes=[bBA[j]])
            for bi, (r0, nr) in enumerate(td.blocks):
                for half in range(2):
                    ps, bps = ps_next()
                    mm_group(ps[0:nr, :], bps,
                             [(BA[:, k * T + r0: k * T + r0 + nr], wa(W_O, k, 1024, half * 512, 512)) for k in range(8)],
                             [bBA[k] for k in range(8)] + [bWA])
                    em.op("dve", lambda e, ps=ps, bi=bi, nr=nr, half=half: e.tensor_tensor(
                        out=XT[xslot][0:nr, bi, half * 512:(half + 1) * 512], in0=ps[0:nr, :],
                        in1=XT[xslot][0:nr, bi, half * 512:(half + 1) * 512], op=ALU.add),
                        reads=[bps, bXT[xslot]], writes=[bXT[xslot]])

        def conv_state_out(td, l):
            if td.sample:
                nrow = 8
                def src(j):
                    return UT[:, j, 0:40].rearrange("p (s t) -> p s t", t=10)[:, :, 8:10]
                dst = cs_s[l, :, :]
            else:
                nrow = 2
                def src(j):
                    return UT[:, j, T:T + 2]
                dst = cs_p[l, :, :]
            for j in range(6):
                ti = nxt("tm", 2)
                em.op("dve", lambda e, j=j, ti=ti: e.tensor_copy(
                    out=(TM[ti][:, 0:nrow].rearrange("p (s t) -> p s t", t=2) if td.sample else TM[ti][:, 0:nrow]),
                    in_=src(j)), reads=[bUT[j]], writes=[bTM[ti]])
                ps, bps = ps_next()
                em.op("pe", lambda e, ps=ps, ti=ti: e.transpose(ps[0:nrow, 0:128], TM[ti][:, 0:nrow], IDF[:, :]),
                      reads=[bTM[ti], bC], writes=[bps])
                em.op("act", lambda e, ps=ps, j=j: e.copy(out=SO8[0:nrow, j * 128:(j + 1) * 128], in_=ps[0:nrow, 0:128]),
                      reads=[bps], writes=[bSO8])
            em.dma("sp", lambda e: e.dma_start(out=dst, in_=SO8[0:nrow, :]), reads=[bSO8], writes=[])

        def phase_mixer_a(l, src, skey, dst, dkey):
            load_gain(g_mix[l:l + 1, :])
            load_w(W_IN, w_in_a[l], 8, 2560)
            load_w(W_O, w_o[l], 8, 1024, first=False)
            load_sample_mem(l)
            em.op("pool", lambda e: e.memset(UT[:, :, 0:2], 0.0), reads=bUT, writes=bUT)
            load_x(src, skey, tiles[0], 0)
            for i, td in enumerate(tiles):
                slot = i % 2
                if i + 1 < len(tiles):
                    load_x(src, skey, tiles[i + 1], (i + 1) % 2)
                if td.sample:
                    em.dma("sp", lambda e: e.dma_start(out=ST8[:, :], in_=stc[l, :, :]), writes=[bST8])
                    for j in range(6):
                        ps, bps = ps_next()
                        em.op("pe", lambda e, ps=ps, j=j: e.transpose(ps[:, 0:8], ST8[0:8, j * 128:(j + 1) * 128], IDF[0:8, 0:8]),
                              reads=[bST8, bC], writes=[bps])
                        em.op("act", lambda e, ps=ps, j=j: e.copy(
                            out=UT[:, j, 0:40].rearrange("p (s t) -> p s t", t=10)[:, :, 0:2],
                            in_=ps[:, 0:8].rearrange("p (s t) -> p s t", t=2)), reads=[bps], writes=[bUT[j]])
                mixer_a_tile(td, l, slot)
                store_x(dst, dkey, td, slot)
                if td.idx == NT - 1 or td.sample:
                    conv_state_out(td, l)
                elif not td.sample:
                    em.op("pool", lambda e: e.tensor_copy(out=UT[:, :, 0:2], in_=UT[:, :, T:T + 2]), reads=bUT, writes=bUT)

        G_OFF, D_OFF = 0, 8 * 2816

        def phase_ffn(l, half, src, skey, acc, akey, dst, dkey):
            c0 = half * 1408
            if half == 0:
                load_gain(g_ffn[l:l + 1, :])
            for k0 in range(0, 8, 2):
                for part in range(2):
                    def f(e, k0=k0, part=part):
                        dst_ = WA[:, k0 * 2816:(k0 + 2) * 2816].rearrange("p (k c) -> p k c", k=2)[:, :, part * 1408:(part + 1) * 1408]
                        s_ = w_gu[l, k0 * 128:(k0 + 2) * 128, part * DFF + c0: part * DFF + c0 + 1408].rearrange(
                            "(k p) c -> p k c", p=128)
                        return e.dma_start(out=dst_, in_=s_)
                    em.dma("pool", f, writes=[bWA], par=not (k0 == 0 and part == 0))
            for k0 in range(0, 11):
                em.dma("pool", lambda e, k0=k0: e.dma_start(
                    out=WA[:, D_OFF + k0 * 1024: D_OFF + (k0 + 1) * 1024],
                    in_=w_dn[l, c0 + k0 * 128: c0 + (k0 + 1) * 128, :]), writes=[bWA], par=True)

            def loads(i):
                td = tiles[i]
                slot = i % 2
                if half == 0:
                    load_x(src, skey, td, slot)
                else:
                    load_x(acc, akey, td, slot)
                    em.dma("sp", lambda e: e.dma_start(out=HT[slot][:, :, :], in_=HTS[td.idx].rearrange("p (k t) -> p k t", k=8)),
                           reads=[xb("hts", td.idx)], writes=[bHT[slot]])

            def ffn_tile(td, slot):
                TT = td.T
                if half == 0:
                    norm_T(td, slot, slot)
                    em.dma("sp", lambda e: e.dma_start(
                        out=HTS[td.idx].rearrange("p (k t) -> p k t", k=8), in_=HT[slot][:, :, :]),
                        reads=[bHT[slot]], writes=[xb("hts", td.idx)])
                hT = HT[slot]
                for c in range(11):
                    psG, bG_ = ps_next()
                    mm_group(psG[:, 0:TT], bG_, [(wa(G_OFF, k, 2816, c * 128, 128), hT[:, k, 0:TT]) for k in range(8)],
                             [bHT[slot], bWA])
                    psU, bU_ = ps_next()
                    mm_group(psU[:, 0:TT], bU_, [(wa(G_OFF, k, 2816, 1408 + c * 128, 128), hT[:, k, 0:TT]) for k in range(8)],
                             [bHT[slot], bWA])
                    ci = nxt("cs", 2)
                    em.op("act", lambda e, psG=psG, ci=ci: e.activation(out=CS[ci][:, 0:TT], in_=psG[:, 0:TT], func=AF.Silu),
                          reads=[bG_], writes=[bCS[ci]])
                    em.op("dve", lambda e, psU=psU, ci=ci, c=c: e.tensor_tensor(
                        out=BA[:, c * T: c * T + TT], in0=psU[:, 0:TT], in1=CS[ci][:, 0:TT], op=ALU.mult),
                        reads=[bU_, bCS[ci]], writes=[bBA[c]])
                for bi, (r0, nr) in enumerate(td.blocks):
                    for hf in range(2):
                        ps, bps = ps_next()
                        mm_group(ps[0:nr, :], bps,
                                 [(BA[:, c * T + r0: c * T + r0 + nr], wa(D_OFF, c, 1024, hf * 512, 512)) for c in range(11)],
                                 bBA + [bWA])
                        em.op("dve", lambda e, ps=ps, bi=bi, nr=nr, hf=hf: e.tensor_tensor(
                            out=XT[slot][0:nr, bi, hf * 512:(hf + 1) * 512], in0=ps[0:nr, :],
                            in1=XT[slot][0:nr, bi, hf * 512:(hf + 1) * 512], op=ALU.add),
                            reads=[bps, bXT[slot]], writes=[bXT[slot]])
                store_x(dst, dkey, td, slot)

            loads(0)
            for i, td in enumerate(tiles):
                if i + 1 < len(tiles):
                    loads(i + 1)
                ffn_tile(td, i % 2)

        bKB = Buf("KB16")
        bTS = Buf("TS")
        bVA = [Buf("VA0"), Buf("VA1")]
        bKVS = Buf("kvscratch")
        rr["va"] = 0

        def kv_extra(ps, bps, br, row, nr, dstT=None, C=None, L=None, dstV=None, bdst=None):
            if dstT is None:
                dstT, C, L = (XC_T, KS_T, KW_T)[br], (8 if br == 0 else 4), NTOK
                dstV = None if br == 0 else (VS_S if br == 1 else VW_S)
                bdst = bKVS
            ncol = 512 if br == 0 else 256
            em.op("dve", lambda e: e.tensor_copy(out=KB16[0:nr, 0:ncol], in_=ps[0:nr, 0:ncol]), reads=[bps], writes=[bKB])
            nch = ncol // 64
            pt, bpt = pt_next()
            for c in range(nch):
                em.op("pe", lambda e, c=c: e.transpose(pt[0:64, c * 128: c * 128 + nr], KB16[0:nr, c * 64:(c + 1) * 64],
                                                       IDB[0:nr, 0:nr]), reads=[bKB, bC], writes=[bpt], signal=(c == nch - 1))
            em.op("act", lambda e: e.copy(out=TS[:, 0:nch, 0:nr],
                                          in_=pt[0:64, 0:nch * 128].rearrange("p (c t) -> p c t", c=nch)[:, :, 0:nr]),
                  reads=[bpt], writes=[bTS])
            em.dma("sp", lambda e: e.dma_start(
                out=dstT.rearrange("d (c s) -> d c s", c=C)[:, :, row:row + nr], in_=TS[:, 0:nch, 0:nr]),
                reads=[bTS], writes=[bdst], par=True)
            if dstV is not None:
                vi = nxt("va", 2)
                em.op("dve", lambda e: e.tensor_copy(out=VA[vi][0:nr, :, 0:64],
                                                     in_=ps[0:nr, 256:512].rearrange("p (h d) -> p h d", h=4)),
                      reads=[bps], writes=[bVA[vi]])
                em.dma("sp", lambda e: e.dma_start(out=dstV[row:row + nr, :], in_=VA[vi][0:nr, :, :].rearrange("p h d -> p (h d)")),
                       reads=[bVA[vi]], writes=[bdst], par=True)

        bSCT = Buf("samplectx")
        bCSs_box = [None]

        def phase_sample_ctx():
            bG = [Buf("G0"), Buf("G1")]
            bIDX = Buf("IDX")
            rr["g"] = 0
            for vi in range(2):
                em.op("pool", lambda e, vi=vi: e.memset(VA[vi][:, :, :], 1.0), writes=[bVA[vi]])
            em.op("pool", lambda e: e.memset(ZR[:, :], 0.0), writes=[bIDX])
            em.op("dve", lambda e: e.tensor_copy(out=IOPF[:, :], in_=IOT[:, 0:1]), reads=[bC], writes=[bIDX])

            def one_seq(s):
                em.dma("sp", lambda e: e.dma_start(out=PTB[:, :], in_=ptab[s:s + 1, :].partition_broadcast(128)), writes=[bIDX])
                em.op("dve", lambda e: e.tensor_copy(out=PTF[:, :], in_=PTB[:, :]), reads=[bIDX], writes=[bIDX])
                em.op("dve", lambda e: e.tensor_scalar(out=PTF[:, :], in0=PTF[:, :], scalar1=128.0, scalar2=IOPF[:, 0:1],
                                                       op0=ALU.mult, op1=ALU.add), reads=[bIDX], writes=[bIDX])
                em.op("dve", lambda e: e.tensor_copy(out=IDX[:, :], in_=PTF[:, :]), reads=[bIDX], writes=[bIDX])
                for br, pool_ap in ((0, pool_c), (1, pool_s)):
                    for j in range(64):
                        gi = nxt("g", 2)
                        em.dma("pool", lambda e, gi=gi, j=j, pool_ap=pool_ap: e.indirect_dma_start(
                            out=G[gi][:, :], out_offset=None, in_=pool_ap[:, :],
                            in_offset=bass.IndirectOffsetOnAxis(ap=IDX[:, j:j + 1], axis=0)), reads=[bIDX], writes=[bG[gi]])
                        if br == 0:
                            kv_extra(G[gi], bG[gi], 0, j * 128, 128, dstT=XCS[s], C=8, L=LC, dstV=None, bdst=bSCT)
                        else:
                            kv_extra(G[gi], bG[gi], 1, j * 128, 128, dstT=KSS[s], C=4, L=LS, dstV=VSS[s], bdst=bSCT)
                for j in range(4):
                    gi = nxt("g", 2)
                    em.dma("sp", lambda e, gi=gi, j=j: e.dma_start(out=G[gi][:, :], in_=win_in[s, j * 128:(j + 1) * 128, :]),
                           writes=[bG[gi]])
                    kv_extra(G[gi], bG[gi], 2, j * 128, 128, dstT=KWS[s], C=4, L=LW, dstV=VWS[s], bdst=bSCT)
                r0 = SEQ + 8 * s
                for (srcT, C, dT, npad, pos) in ((XC_T, 8, XCS[s], 8, 8192), (KS_T, 4, KSS[s], 120, 8192), (KW_T, 4, KWS[s], 120, 512)):
                    em.dma("sp", lambda e, srcT=srcT, C=C, dT=dT, pos=pos: e.dma_start(
                        out=dT.rearrange("d (c s) -> d c s", c=C)[:, :, pos:pos + 8],
                        in_=srcT.rearrange("d (c s) -> d c s", c=C)[:, :, r0:r0 + 8]), reads=[bKVS], writes=[bSCT], par=True)
                    for c in range(C):
                        em.dma("sp", lambda e, dT=dT, C=C, pos=pos, c=c, npad=npad: e.dma_start(
                            out=dT.rearrange("d (c s) -> d c s", c=C)[:, c, pos + 8:pos + 8 + npad], in_=ZR[0:64, 0:npad]),
                            reads=[bIDX], writes=[bSCT], par=True)
                for (sV, dV, pos) in ((VS_S, VSS[s], 8192), (VW_S, VWS[s], 512)):
                    em.dma("sp", lambda e, sV=sV, dV=dV, pos=pos: e.dma_start(out=dV[pos:pos + 8, :], in_=sV[r0:r0 + 8, :]),
                           reads=[bKVS], writes=[bSCT], par=True)
                    em.dma("sp", lambda e, dV=dV, pos=pos: e.dma_start(out=dV[pos + 8:pos + 128, :], in_=ZR[0:120, 0:260]),
                           reads=[bIDX], writes=[bSCT], par=True)
            for s in range(4):
                one_seq(s)

        def phase_kv(src, skey):
            for vi in range(2):
                em.op("pool", lambda e, vi=vi: e.memset(VA[vi][:, :, :], 1.0), writes=[bVA[vi]])
            load_gain(g_kv[0:1, :])
            load_w(0, w_kv, 8, 1536)
            load_x(src, skey, tiles[0], 0)
            for i, td in enumerate(tiles):
                slot = i % 2
                if i + 1 < len(tiles):
                    load_x(src, skey, tiles[i + 1], (i + 1) % 2)
                norm_T(td, slot, 0)
                for bi, (r0, nr) in enumerate(td.blocks):
                    for br in range(3):
                        ps, bps = ps_next()
                        mm_group(ps[0:nr, :], bps,
                                 [(HT[0][:, k, r0:r0 + nr], wa(0, k, 1536, br * 512, 512)) for k in range(8)],
                                 [bHT[0], bWA])
                        si = nxt("stg", 2)
                        em.op("act" if br != 1 else "dve",
                              (lambda e, ps=ps, si=si, nr=nr: e.copy(out=STG[si][0:nr, :], in_=ps[0:nr, :])) if br != 1 else
                              (lambda e, ps=ps, si=si, nr=nr: e.tensor_copy(out=STG[si][0:nr, :], in_=ps[0:nr, :])),
                              reads=[bps], writes=[bSTG[si]])
                        row = td.row0 + r0
                        kv_extra(ps, bps, br, row, nr)
                        dsts = []
                        if br == 0:
                            dsts.append(cmp_o[row:row + nr, :])
                        elif br == 1:
                            dsts.append(slc_o[row:row + nr, :])
                        else:
                            dsts.append(WINP[row:row + nr, :])
                            if (not td.sample) and row >= SEQ - 512:
                                dsts.append(winp_o[row - (SEQ - 512): row - (SEQ - 512) + nr, :])
                            if td.sample:
                                for s in range(4):
                                    dsts.append((wins_o[s, 504:512, :], s))
                        for dd in dsts:
                            if isinstance(dd, tuple):
                                d_, s = dd
                                em.dma("sp", lambda e, d_=d_, s=s, si=si: e.dma_start(out=d_, in_=STG[si][s * 8:(s + 1) * 8, :]),
                                       reads=[bSTG[si]], writes=[])
                            else:
                                em.dma("sp", lambda e, dd=dd, si=si, nr=nr: e.dma_start(out=dd, in_=STG[si][0:nr, :]),
                                       reads=[bSTG[si]], writes=[])
            for s in range(4):
                em.dma("sp", lambda e, s=s: e.dma_start(out=wins_o[s, 0:504, :], in_=win_in[s, 8:512, :]), writes=[])

        def phase_compress(jobs):
            bW1 = Buf("W1"); bXC = Buf("XCc"); bAK = Buf("ACTK"); bAV = Buf("ACTV"); bBI = Buf("BIAS")
            bCK = Buf("CKTt"); bCV = Buf("CVt"); bCS_ = Buf("cmpscratch")
            for kv in range(2):
                em.dma("pool", lambda e, kv=kv: e.dma_start(out=W1[:, kv, :, :], in_=w1_c[kv].rearrange("j d e -> d j e")),
                       writes=[bW1], par=(kv > 0))
                for two in range(2):
                    em.dma("pool", lambda e, kv=kv, two=two: e.dma_start(
                        out=W1B[two * 64:(two + 1) * 64, kv, :, :],
                        in_=w1_c[kv].rearrange("(c two) d e -> two d c e", two=2)[two]), writes=[bW1], par=True)
                    em.dma("pool", lambda e, kv=kv, two=two: e.dma_start(
                        out=PEV[two * 64:(two + 1) * 64, kv, :],
                        in_=pe_c[kv].rearrange("(c two) d -> two d c", two=2)[two], allow_slow_non_contiguous=True),
                        writes=[bW1], par=True)
                em.dma("pool", lambda e, kv=kv: e.dma_start(out=W2[:, kv, :], in_=w2_c[kv]), writes=[bW1], par=True)
            em.op("pool", lambda e: e.memset(ACTK[:, :, :], 0.0), writes=[bAK])
            em.op("pool", lambda e: e.memset(ACTV[:, :, :], 0.0), writes=[bAV])
            em.op("pool", lambda e: e.memset(CVt[:, :, :], 1.0), writes=[bCV])
            for kv in range(2):
                ps, bps = ps_next()
                mm_group(ps[:, 0:1], bps, [(W1B[:, kv, c, :], PEV[:, kv, c:c + 1]) for c in range(16)], [bW1])
                em.op("act", lambda e, ps=ps, kv=kv: e.copy(out=BIAS[:, kv:kv + 1], in_=ps[:, 0:1]), reads=[bps], writes=[bBI])

            def chunk(q, srcT, nq, dstK, nK, dstV, bsrc, all_full):
                nb = 64 if (all_full or q < nq - 1) else 63
                ncol = 1040 if (all_full or q < nq - 1) else 1024
                em.dma("sp", lambda e: e.dma_start(
                    out=XCc[:, :, 0:ncol], in_=srcT.rearrange("d (c s) -> d c s", c=8)[:, :, 1024 * q: 1024 * q + ncol]),
                    reads=[bsrc], writes=[bXC])
                for kv in range(2):
                    ps, bps = ps_next()
                    for kh in range(4):
                        mm_group(ps[:, kh * 64: kh * 64 + nb], bps,
                                 [(W1[:, kv, j, :], XCc[:, kv * 4 + kh, j: j + 16 * (nb - 1) + 1: 16]) for j in range(32)],
                                 [bW1, bXC])
                    hq = q % 2
                    if kv == 0:
                        em.op("act", lambda e, ps=ps: e.activation(
                            out=ACTK[:, :, 0:nb], in_=ps[:, 0:256].rearrange("p (h n) -> p h n", h=4)[:, :, 0:nb],
                            func=AF.Silu, bias=BIAS[:, 0:1]), reads=[bps, bBI], writes=[bAK])
                        ps2, bps2 = ps_next()
                        mm_group(ps2[0:64, 0:256], bps2, [(W2[:, 0, :], ACTK[:, :, :].rearrange("p h n -> p (h n)"))], [bW1, bAK])
                        em.op("dve", lambda e, ps2=ps2: e.tensor_copy(
                            out=CKTt[:, :, :], in_=ps2[0:64, 0:256].rearrange("p (h n) -> p h n", h=4)), reads=[bps2], writes=[bCK])
                        em.dma("sp", lambda e: e.dma_start(
                            out=dstK.rearrange("d (h n) -> d h n", h=4)[:, :, q * 64:(q + 1) * 64], in_=CKTt[:, :, :]),
                            reads=[bCK], writes=[bCS_], par=True)
                    else:
                        em.op("act", lambda e, ps=ps: e.activation(
                            out=ACTV[:, :, hq * 64: hq * 64 + nb], in_=ps[:, 0:256].rearrange("p (h n) -> p h n", h=4)[:, :, 0:nb],
                            func=AF.Silu, bias=BIAS[:, 1:2]), reads=[bps, bBI], writes=[bAV])
                        ps3, bps3 = ps_next()
                        for kh in range(4):
                            mm_group(ps3[:, kh * 64:(kh + 1) * 64], bps3, [(ACTV[:, kh, :], W2[:, 1, :])], [bW1, bAV])
                        em.op("dve", lambda e, ps3=ps3: e.tensor_copy(
                            out=CVt[hq * 64:(hq + 1) * 64, :, 0:64],
                            in_=ps3[hq * 64:(hq + 1) * 64, 0:256].rearrange("p (h d) -> p h d", h=4)), reads=[bps3], writes=[bCV])
                        em.dma("sp", lambda e: e.dma_start(
                            out=dstV[q * 64:(q + 1) * 64, :], in_=CVt[hq * 64:(hq + 1) * 64, :, :].rearrange("p h d -> p (h d)")),
                            reads=[bCV], writes=[bCS_], par=True)
            for (srcT, nq, dstK, nK, dstV, bsrc, all_full) in jobs:
                for q in range(nq):
                    chunk(q, srcT, nq, dstK, nK, dstV, bsrc, all_full)
            return bCS_

        W_INB, W_OB, KS_OFF = 0, 8 * 1060, 17408
        bVSB = Buf("VSB"); bKWB = Buf("KWB"); bVWB = Buf("VWB"); bCKB = Buf("CKB"); bQA = Buf("QA")
        bPB = [Buf("PB%d" % i) for i in range(3)]
        bYt = [Buf("Yt%d" % i) for i in range(4)]
        bNMX = Buf("NMX"); bG36 = Buf("G36"); bFRC = Buf("FRC"); bTK = Buf("topk"); bR9 = Buf("R9"); bQ1 = Buf("Q1")
        rr["pb"] = 0
        rr["sc"] = 0

        def sc_next():
            i = 4 + nxt("sc", 2)
            return PS[i], bPS[i]

        def nsa_block(td, kh, bi):
            tg = td.idx * 4 + bi
            t0 = tg * 128
            row0 = td.row0
            oc, os_, ow = PS[0], PS[1], PS[2]
            psI = PS[3]
            for pz, bz in ((oc, bPS[0]), (os_, bPS[1]), (ow, bPS[2]), (psI, bPS[3])):
                em.op("dve", lambda e, pz=pz: e.memset(pz[:, 0:195], 0.0), writes=[bz])
            qa = QA[0:70, :, bi * 128:(bi + 1) * 128]

            def exp_to_pb(sc, bsc):
                pi = nxt("pb", 3)
                em.op("act", lambda e: e.activation(out=PB[pi][:, :], in_=sc[:, 0:384], func=AF.Exp),
                      reads=[bsc], writes=[bPB[pi]])
                return pi

            def select(pi, base, cm, tstep):
                em.op("pool", lambda e: e.affine_select(
                    out=PB[pi][:, :].rearrange("p (h t) -> p h t", h=3), in_=PB[pi][:, :].rearrange("p (h t) -> p h t", h=3),
                    pattern=[[0, 3], [tstep, 128]], compare_op=ALU.is_ge, fill=0.0, base=base, channel_multiplier=cm),
                    reads=[bPB[pi]], writes=[bPB[pi]])

            def pv(pi, acc, bacc, v_ap, vb):
                for hi in range(3):
                    em.op("pe", lambda e, hi=hi: e.matmul(acc[:, hi * 65:(hi + 1) * 65], PB[pi][:, hi * 128:(hi + 1) * 128], v_ap,
                                                          start=False, stop=True, skip_group_check=True),
                          reads=[bPB[pi], vb], writes=[bacc], signal=(hi == 2))

            for nt in range(2):
                if 16 * (nt * 128) + 31 > t0 + 127:
                    continue
                sc, bsc = sc_next()
                em.op("pe", lambda e, sc=sc, nt=nt: e.matmul(sc[:, 0:384], CKB[0:70, kh, nt * 128:(nt + 1) * 128], qa,
                                                             start=True, stop=True), reads=[bCKB, bQA], writes=[bsc])
                pi = exp_to_pb(sc, bsc)
                select(pi, t0 - 16 * nt * 128 - 31, -16, 1)
                pv(pi, oc, bPS[0], CVB[:, nt, kh * 65:(kh + 1) * 65], bCKB)
                for hi in range(3):
                    em.op("pe", lambda e, hi=hi, nt=nt, pi=pi: e.matmul(
                        psI[:, hi * 64 + nt * 32: hi * 64 + nt * 32 + 32], PB[pi][:, hi * 128:(hi + 1) * 128], GM[:, :],
                        start=False, stop=True, skip_group_check=True), reads=[bPB[pi], bC], writes=[bPS[3]], signal=(hi == 2))
            def den_recip(acc, bacc, br):
                em.op("dve", lambda e: e.tensor_scalar(
                    out=R9[:, br:9:3], in0=acc[:, 0:195].rearrange("p (h c) -> p h c", c=65)[:, :, 64],
                    scalar1=1e-30, scalar2=None, op0=ALU.max), reads=[bacc], writes=[bR9])
            den_recip(oc, bPS[0], 0)
            em.op("dve", lambda e: e.reciprocal(out=RC[:, 0:3], in_=R9[:, 0:9:3]), reads=[bR9], writes=[bTK])
            em.op("dve", lambda e: e.tensor_scalar(out=IMP[:, :], in0=psI[:, 0:64], scalar1=RC[:, 0:1], scalar2=None, op0=ALU.mult),
                  reads=[bPS[3], bTK], writes=[bTK])
            for hi in (1, 2):
                em.op("dve", lambda e, hi=hi: e.scalar_tensor_tensor(
                    out=IMP[:, :], in0=psI[:, hi * 64:(hi + 1) * 64], scalar=RC[:, hi:hi + 1], in1=IMP[:, :],
                    op0=ALU.mult, op1=ALU.add), reads=[bPS[3], bTK], writes=[bTK])
            em.op("dve", lambda e: e.tensor_tensor(out=SCR[:, :], in0=IMP[:, :], in1=FRC[:, bi, :], op=ALU.max),
                  reads=[bTK, bFRC], writes=[bTK])
            em.op("dve", lambda e: e.max(out=M8[:, 0:8], in_=SCR[:, :]), reads=[bTK], writes=[bTK])
            em.op("dve", lambda e: e.match_replace(out=SC2[:, :], in_to_replace=M8[:, 0:8], in_values=SCR[:, :], imm_value=-1e30),
                  reads=[bTK], writes=[bTK])
            em.op("dve", lambda e: e.max(out=M8[:, 8:16], in_=SC2[:, :]), reads=[bTK], writes=[bTK])
            em.op("dve", lambda e: e.tensor_reduce(out=M8[:, 16:17], in_=M8[:, 8:16], axis=AX.X, op=ALU.min), reads=[bTK], writes=[bTK])
            em.op("dve", lambda e: e.tensor_scalar(out=NMt[:, :], in0=SCR[:, :], scalar1=M8[:, 16:17], scalar2=None, op0=ALU.is_ge),
                  reads=[bTK], writes=[bTK])
            em.op("dve", lambda e: e.tensor_scalar(out=NMt[:, :], in0=NMt[:, :], scalar1=30000.0, scalar2=-30000.0,
                                                   op0=ALU.mult, op1=ALU.add), reads=[bTK], writes=[bTK])
            nblk = 2 * (tg + 1)
            em.op("dve", lambda e: e.tensor_copy(
                out=NMX[:, 0:nblk * 64].rearrange("p (j r) -> p j r", r=64),
                in_=NMt[:, 0:nblk].unsqueeze(2).to_broadcast([128, nblk, 64])), reads=[bTK], writes=[bNMX])
            for kb in range(tg + 1):
                sc, bsc = sc_next()
                em.op("pe", lambda e, sc=sc, kb=kb: e.matmul(
                    sc[:, 0:384], WA[0:70, KS_OFF + kh * 4096 + kb * 128: KS_OFF + kh * 4096 + (kb + 1) * 128], qa,
                    start=True, stop=False), reads=[bWA, bQA], writes=[bsc], signal=False)
                for hi in range(3):
                    em.op("pe", lambda e, sc=sc, kb=kb, hi=hi: e.matmul(
                        sc[:, hi * 128:(hi + 1) * 128], NMX[:, kb * 128:(kb + 1) * 128], IDB[:, :],
                        start=False, stop=(hi == 2), skip_group_check=True), reads=[bNMX, bC], writes=[bsc], signal=(hi == 2))
                pi = exp_to_pb(sc, bsc)
                if kb == tg:
                    select(pi, 0, -1, 1)
                pv(pi, os_, bPS[1], VSB[:, kb, kh * 65:(kh + 1) * 65], bVSB)
            wlo = row0 - 512
            for kb in range(max(0, tg - 4), tg + 1):
                sc, bsc = sc_next()
                c0 = kb * 128 - wlo
                em.op("pe", lambda e, sc=sc, c0=c0: e.matmul(sc[:, 0:384], KWB[0:70, kh, c0:c0 + 128], qa, start=True, stop=True),
                      reads=[bKWB, bQA], writes=[bsc])
                pi = exp_to_pb(sc, bsc)
                if kb == tg:
                    select(pi, 0, -1, 1)
                if kb == tg - 4:
                    select(pi, 0, 1, -1)
                pv(pi, ow, bPS[2], VWB[:, c0 // 128, kh * 65:(kh + 1) * 65], bVWB)
            den_recip(os_, bPS[1], 1)
            den_recip(ow, bPS[2], 2)
            em.op("dve", lambda e: e.reciprocal(out=R9[:, :], in_=R9[:, :]), reads=[bR9], writes=[bR9])
            em.op("dve", lambda e: e.tensor_tensor(out=R9[:, :], in0=R9[:, :], in1=G36[:, bi, kh * 9:(kh + 1) * 9], op=ALU.mult),
                  reads=[bR9, bG36], writes=[bR9])
            for hi in range(3):
                h = 3 * kh + hi
                em.op("dve", lambda e, hi=hi: e.tensor_scalar(
                    out=TY[:, :], in0=oc[:, hi * 65: hi * 65 + 64], scalar1=R9[:, 3 * hi:3 * hi + 1], scalar2=None, op0=ALU.mult),
                    reads=[bPS[0], bR9], writes=[bTK])
                em.op("dve", lambda e, hi=hi: e.scalar_tensor_tensor(
                    out=TY[:, :], in0=os_[:, hi * 65: hi * 65 + 64], scalar=R9[:, 3 * hi + 1:3 * hi + 2], in1=TY[:, :],
                    op0=ALU.mult, op1=ALU.add), reads=[bPS[1], bR9, bTK], writes=[bTK])
                em.op("dve", lambda e, hi=hi, h=h: e.scalar_tensor_tensor(
                    out=Yt[:, bi, h * 64:(h + 1) * 64], in0=ow[:, hi * 65: hi * 65 + 64], scalar=R9[:, 3 * hi + 2:3 * hi + 3],
                    in1=TY[:, :], op0=ALU.mult, op1=ALU.add), reads=[bPS[2], bR9, bTK], writes=[bYt[bi]])

        def mixer_b_tile(td, l, xslot):
            TT = td.T
            norm_T(td, xslot, 0)
            hT = HT[0]
            qaps = []
            for hp in range(2):
                ps, bps = ps_next()
                mm_group(ps[:, 0:TT], bps, [(wa(W_INB, k, 1060, 804 + hp * 128, 128), hT[:, k, 0:TT]) for k in range(8)],
                         [bHT[0], bWA])
                qap = (BA[:, 10 * T: 10 * T + TT] if hp == 0 else Q1[:, 0:TT])
                qb = bBA[10] if hp == 0 else bQ1
                em.op("act", lambda e, ps=ps, qap=qap: e.copy(out=qap, in_=ps[:, 0:TT]), reads=[bps], writes=[qb])
                qaps.append((qap, qb))
            mem_attention(td, l, [qaps[0][0], qaps[1][0]], [qaps[0][1], qaps[1][1]])
            if td.sample:
                for j in range(6):
                    em.op("pool", lambda e, j=j: e.memset(BA[:, j * T: j * T + TT], 0.0), writes=[bBA[j]])
            else:
                row0 = td.row0
                for bi, (r0, nr) in enumerate(td.blocks):
                    ps, bps = ps_next()
                    mm_group(ps[0:nr, 0:36], bps, [(hT[:, k, r0:r0 + nr], wa(W_INB, k, 1060, 768, 36)) for k in range(8)],
                             [bHT[0], bWA])
                    em.op("act", lambda e, ps=ps, bi=bi, nr=nr: e.activation(out=G36[0:nr, bi, :], in_=ps[0:nr, 0:36], func=AF.Sigmoid),
                          reads=[bps], writes=[bG36])
                em.dma("sp", lambda e: e.dma_start(out=FRC[:, :, :], in_=frc_t[row0:row0 + T, :].rearrange("(b p) j -> p b j", p=128)),
                       writes=[bFRC])
                lo = max(0, row0 - 512)
                off = lo - (row0 - 512)
                nkeys = row0 + T - lo
                em.dma("sp", lambda e: e.dma_start(
                    out=KWB[0:64, :, off:off + nkeys], in_=KW_T.rearrange("d (h s) -> d h s", h=4)[:, :, lo:lo + nkeys]),
                    reads=[bKVS], writes=[bKWB])
                for kh in range(4):
                    em.dma("pool", lambda e, kh=kh: e.dma_start(out=KWB[64:70, kh, off:off + nkeys], in_=kaug_t[:, lo:lo + nkeys]),
                           writes=[bKWB], par=True)
                em.dma("sp", lambda e: e.dma_start(
                    out=VWB[:, off // 128: off // 128 + nkeys // 128, :],
                    in_=VW_S[lo:lo + nkeys, :].rearrange("(kb p) c -> p kb c", p=128)), reads=[bKVS], writes=[bVWB])
                for kh in range(4):
                    for hi in range(3):
                        h = 3 * kh + hi
                        ps, bps = sc_next()
                        mm_group(ps[0:64, 0:TT], bps, [(wa(W_INB, k, 1060, h * 64, 64), hT[:, k, 0:TT]) for k in range(8)],
                                 [bHT[0], bWA])
                        em.op("act", lambda e, ps=ps, hi=hi: e.mul(out=QA[0:64, hi, 0:TT], in_=ps[0:64, 0:TT], mul=0.125),
                              reads=[bps], writes=[bQA])
                    em.dma("pool", lambda e, kh=kh: e.dma_start(
                        out=QA[64:70, :, 0:TT], in_=qaug_t[3 * kh:3 * kh + 3, :, row0:row0 + TT].rearrange("h r t -> r h t")),
                        writes=[bQA], par=True)
                    for bi in range(4):
                        nsa_block(td, kh, bi)
                for kk in range(3):
                    pt, bpt = pt_next()
                    for k2 in range(2):
                        j = kk * 2 + k2
                        for bi in range(4):
                            em.op("pe", lambda e, j=j, k2=k2, bi=bi, pt=pt: e.transpose(
                                pt[:, k2 * 512 + bi * 128: k2 * 512 + (bi + 1) * 128], Yt[:, bi, j * 128:(j + 1) * 128], IDB[:, :]),
                                reads=[bYt[bi], bC], writes=[bpt], signal=(k2 == 1 and bi == 3))
                    em.op("act", lambda e, kk=kk, pt=pt: e.copy(
                        out=BA[:, 2 * kk * T:(2 * kk + 2) * T].rearrange("p (a t) -> p a t", a=2),
                        in_=pt[:, :].rearrange("p (a t) -> p a t", a=2)), reads=[bpt], writes=[bBA[2 * kk], bBA[2 * kk + 1]])
            for bi, (r0, nr) in enumerate(td.blocks):
                for half in range(2):
                    ps, bps = ps_next()
                    mm_group(ps[0:nr, :], bps,
                             [(BA[:, k * T + r0: k * T + r0 + nr], wa(W_OB, k, 1024, half * 512, 512)) for k in range(8)],
                             [bBA[k] for k in range(8)] + [bWA])
                    em.op("dve", lambda e, ps=ps, bi=bi, nr=nr, half=half: e.tensor_tensor(
                        out=XT[xslot][0:nr, bi, half * 512:(half + 1) * 512], in0=ps[0:nr, :],
                        in1=XT[xslot][0:nr, bi, half * 512:(half + 1) * 512], op=ALU.add),
                        reads=[bps, bXT[xslot]], writes=[bXT[xslot]])

        def phase_mixer_b(l, src, skey, dst, dkey, bCS_):
            load_gain(g_mix[l:l + 1, :])
            load_w(W_INB, w_in_b[l - 2], 8, 1060)
            load_w(W_OB, w_o[l], 8, 1024, first=False)
            em.dma("sp", lambda e: e.dma_start(
                out=WA[0:64, KS_OFF:KS_OFF + 16384].rearrange("p (h s) -> p h s", h=4),
                in_=KS_T.rearrange("d (h s) -> d h s", h=4)[:, :, 0:SEQ]), reads=[bKVS], writes=[bWA], par=True)
            for kh in range(4):
                em.dma("pool", lambda e, kh=kh: e.dma_start(
                    out=WA[64:70, KS_OFF + kh * 4096: KS_OFF + (kh + 1) * 4096], in_=kaug_t[:, :]), writes=[bWA], par=True)
            em.dma("sp", lambda e: e.dma_start(out=VSB[:, :, :], in_=VS_S[0:SEQ, :].rearrange("(kb p) c -> p kb c", p=128)),
                   reads=[bKVS], writes=[bVSB])
            em.dma("sp", lambda e: e.dma_start(out=CKB[0:64, :, :], in_=CKT_S.rearrange("d (h n) -> d h n", h=4)),
                   reads=[bCS_], writes=[bCKB])
            for kh in range(4):
                em.dma("pool", lambda e, kh=kh: e.dma_start(out=CKB[64:70, kh, :], in_=kaug_c[:, :]), writes=[bCKB], par=True)
            em.dma("sp", lambda e: e.dma_start(out=CVB[:, :, :], in_=CV_S.rearrange("(nt p) c -> p nt c", p=128)),
                   reads=[bCS_], writes=[bCKB], par=True)
            em.dma("pool", lambda e: e.dma_start(out=GM[:, :], in_=gm_t[:, :]), writes=[bC])
            load_sample_mem(l)
            for i, td in enumerate(tiles):
                if td.sample and SAMPLE_NSA:
                    continue
                load_x(src, skey, td, 0)
                mixer_b_tile(td, l, 0)
                store_x(dst, dkey, td, 0)

        def phase_mixer_b_sample(l, src, skey, dst, dkey):
            bCSs = bCSs_box[0]
            td = tiles[NT]
            TT = NS
            bQAs = Buf("QAs"); bG36s = Buf("G36s"); bFRs = Buf("FRCs"); bCKs = Buf("CKs"); bKWs = Buf("KWs")
            bKSc = [Buf("KSc0"), Buf("KSc1")]; bVSc = [Buf("VSc0"), Buf("VSc1")]
            bPBs = [Buf("PBs%d" % i) for i in range(3)]; bNX = [Buf("NX0"), Buf("NX1")]
            bTKs = Buf("tks"); bO = Buf("Osb"); bYs = Buf("Yts"); bQ1s = Buf("Q1S"); bR9s = Buf("R9s")
            st_ = {"pb": 0, "nx": 0, "kc": 0}

            def rot(k, n):
                i = st_[k]
                st_[k] = (i + 1) % n
                return i
            load_x(src, skey, td, 0)
            norm_T(td, 0, 0)
            hT = HT[0]
            em.dma("pool", lambda e: e.dma_start(out=GMs[:, :], in_=gm_t[:, :]), writes=[bTKs])
            em.dma("sp", lambda e: e.dma_start(out=FRCs[:, :], in_=frc_s[:, :]), writes=[bFRs])
            qaps = []
            for hp in range(2):
                ps, bps = ps_next()
                mm_group(ps[:, 0:TT], bps, [(wa(W_INB, k, 1060, 804 + hp * 128, 128), hT[:, k, 0:TT]) for k in range(8)],
                         [bHT[0], bWA])
                qap = (BA[:, 10 * T: 10 * T + TT] if hp == 0 else Q1S[:, 0:TT])
                qb = bBA[10] if hp == 0 else bQ1s
                em.op("act", lambda e, ps=ps, qap=qap: e.copy(out=qap, in_=ps[:, 0:TT]), reads=[bps], writes=[qb])
                qaps.append((qap, qb))
            mem_attention(td, l, [qaps[0][0], qaps[1][0]], [qaps[0][1], qaps[1][1]])
            for h in range(12):
                ps, bps = sc_next()
                mm_group(ps[0:64, 0:TT], bps, [(wa(W_INB, k, 1060, h * 64, 64), hT[:, k, 0:TT]) for k in range(8)], [bHT[0], bWA])
                em.op("act", lambda e, ps=ps, h=h: e.mul(out=QAs[0:64, h, :], in_=ps[0:64, 0:TT], mul=0.125), reads=[bps], writes=[bQAs])
            em.dma("pool", lambda e: e.dma_start(out=QAs[64:70, :, :], in_=qaug_s.rearrange("r (h t) -> r h t", h=12)),
                   writes=[bQAs], par=True)
            for s_ in range(4):
                ps, bps = ps_next()
                mm_group(ps[0:8, 0:36], bps, [(hT[:, k, s_ * 8:(s_ + 1) * 8], wa(W_INB, k, 1060, 768, 36)) for k in range(8)],
                         [bHT[0], bWA])
                em.op("act", lambda e, ps=ps, s_=s_: e.activation(out=G36s[0:8, s_, :], in_=ps[0:8, 0:36], func=AF.Sigmoid),
                      reads=[bps], writes=[bG36s])

            def exp_pb(sc, bsc):
                pi = rot("pb", 3)
                em.op("act", lambda e: e.activation(out=PBs[pi][:, :], in_=sc[:, 0:24], func=AF.Exp), reads=[bsc], writes=[bPBs[pi]])
                return pi

            def select(pi, base, cm, tstep):
                em.op("pool", lambda e: e.affine_select(
                    out=PBs[pi][:, :].rearrange("p (h t) -> p h t", h=3), in_=PBs[pi][:, :].rearrange("p (h t) -> p h t", h=3),
                    pattern=[[0, 3], [tstep, 8]], compare_op=ALU.is_ge, fill=0.0, base=base, channel_multiplier=cm),
                    reads=[bPBs[pi]], writes=[bPBs[pi]])

            def pv(pi, acc, bacc, col0, v_ap, vb):
                for hi in range(3):
                    em.op("pe", lambda e, hi=hi: e.matmul(acc[0:8, col0 + hi * 65: col0 + (hi + 1) * 65], PBs[pi][:, hi * 8:(hi + 1) * 8],
                                                          v_ap, start=False, stop=True, skip_group_check=True),
                          reads=[bPBs[pi], vb], writes=[bacc], signal=(hi == 2))

            def one_seq(s_):
                def qa(kh):
                    return QAs[0:70, 3 * kh:3 * kh + 3, s_ * 8:(s_ + 1) * 8]
                em.dma("sp", lambda e: e.dma_start(out=CKs[0:64, :, :], in_=CKS[s_].rearrange("d (h n) -> d h n", h=4)),
                       reads=[bCSs], writes=[bCKs])
                for kh in range(4):
                    em.dma("pool", lambda e, kh=kh: e.dma_start(out=CKs[64:70, kh, :], in_=kaug_cs[:, :]), writes=[bCKs], par=True)
                em.dma("sp", lambda e: e.dma_start(out=CVs[:, :, :], in_=CVS[s_].rearrange("(nt p) c -> p nt c", p=128)),
                       reads=[bCSs], writes=[bCKs], par=True)
                em.dma("sp", lambda e: e.dma_start(out=KWs[0:64, :, :], in_=KWS[s_].rearrange("d (h n) -> d h n", h=4)),
                       reads=[bSCT], writes=[bKWs])
                for kh in range(4):
                    em.dma("pool", lambda e, kh=kh: e.dma_start(out=KWs[64:70, kh, :], in_=kaug_ws[:, :]), writes=[bKWs], par=True)
                em.dma("sp", lambda e: e.dma_start(out=VWs[:, :, :], in_=VWS[s_].rearrange("(nt p) c -> p nt c", p=128)),
                       reads=[bSCT], writes=[bKWs], par=True)
                for kh in range(4):
                    oc, psI = PS[0], PS[3]
                    em.op("dve", lambda e: e.memset(oc[0:8, 0:195], 0.0), writes=[bPS[0]])
                    em.op("dve", lambda e: e.memset(psI[0:8, 0:384], 0.0), writes=[bPS[3]])
                    for nt in range(4):
                        sc, bsc = sc_next()
                        em.op("pe", lambda e, sc=sc, nt=nt, kh=kh: e.matmul(sc[:, 0:24], CKs[0:70, kh, nt * 128:(nt + 1) * 128], qa(kh),
                                                                            start=True, stop=True), reads=[bCKs, bQAs], writes=[bsc])
                        pi = exp_pb(sc, bsc)
                        if nt == 3:
                            select(pi, 510 - 384, -1, 0)
                        pv(pi, oc, bPS[0], 0, CVs[:, nt, kh * 65:(kh + 1) * 65], bCKs)
                        for hi in range(3):
                            em.op("pe", lambda e, hi=hi, nt=nt, pi=pi: e.matmul(
                                psI[0:8, hi * 128 + nt * 32: hi * 128 + nt * 32 + 32], PBs[pi][:, hi * 8:(hi + 1) * 8], GMs[:, :],
                                start=False, stop=True, skip_group_check=True), reads=[bPBs[pi], bTKs], writes=[bPS[3]], signal=(hi == 2))
                    em.op("act", lambda e, kh=kh: e.copy(out=OCs[0:8, kh, :], in_=oc[0:8, 0:195]), reads=[bPS[0]], writes=[bO])
                    em.op("dve", lambda e, kh=kh: e.tensor_scalar(
                        out=RCs[0:8, 0:3], in0=OCs[0:8, kh, :].rearrange("p (h c) -> p h c", c=65)[:, :, 64], scalar1=1e-30, scalar2=None,
                        op0=ALU.max), reads=[bO], writes=[bTKs])
                    em.op("dve", lambda e: e.reciprocal(out=RCs[0:8, 0:3], in_=RCs[0:8, 0:3]), reads=[bTKs], writes=[bTKs])
                    em.op("dve", lambda e: e.tensor_scalar(out=IMPs[0:8, :], in0=psI[0:8, 0:128], scalar1=RCs[0:8, 0:1], scalar2=None,
                                                           op0=ALU.mult), reads=[bPS[3], bTKs], writes=[bTKs])
                    for hi in (1, 2):
                        em.op("dve", lambda e, hi=hi: e.scalar_tensor_tensor(
                            out=IMPs[0:8, :], in0=psI[0:8, hi * 128:(hi + 1) * 128], scalar=RCs[0:8, hi:hi + 1], in1=IMPs[0:8, :],
                            op0=ALU.mult, op1=ALU.add), reads=[bPS[3], bTKs], writes=[bTKs])
                    em.op("dve", lambda e: e.tensor_copy(out=SCRs[0:8, :], in_=FRCs[0:8, :]), reads=[bFRs], writes=[bTKs])
                    em.op("dve", lambda e: e.tensor_tensor(out=SCRs[0:8, 0:128], in0=IMPs[0:8, :], in1=FRCs[0:8, 0:128], op=ALU.max),
                          reads=[bTKs, bFRs], writes=[bTKs])
                    em.op("dve", lambda e: e.max(out=M8s[0:8, 0:8], in_=SCRs[0:8, :]), reads=[bTKs], writes=[bTKs])
                    em.op("dve", lambda e: e.match_replace(out=SC2s[0:8, :], in_to_replace=M8s[0:8, 0:8], in_values=SCRs[0:8, :],
                                                           imm_value=-1e30), reads=[bTKs], writes=[bTKs])
                    em.op("dve", lambda e: e.max(out=M8s[0:8, 8:16], in_=SC2s[0:8, :]), reads=[bTKs], writes=[bTKs])
                    em.op("dve", lambda e: e.tensor_reduce(out=M8s[0:8, 16:17], in_=M8s[0:8, 8:16], axis=AX.X, op=ALU.min),
                          reads=[bTKs], writes=[bTKs])
                    em.op("dve", lambda e, kh=kh: e.tensor_scalar(out=NMts[0:8, kh, :], in0=SCRs[0:8, :], scalar1=M8s[0:8, 16:17],
                                                                  scalar2=None, op0=ALU.is_ge), reads=[bTKs], writes=[bTKs])
                    em.op("dve", lambda e, kh=kh: e.tensor_scalar(out=NMts[0:8, kh, :], in0=NMts[0:8, kh, :], scalar1=30000.0,
                                                                  scalar2=-30000.0, op0=ALU.mult, op1=ALU.add), reads=[bTKs], writes=[bTKs])
                em.op("dve", lambda e: e.memset(PS[0][0:8, 0:390], 0.0), writes=[bPS[0]])
                em.op("dve", lambda e: e.memset(PS[1][0:8, 0:390], 0.0), writes=[bPS[1]])
                for ch in range(17):
                    nk = 512 if ch < 16 else 128
                    ci = rot("kc", 2)
                    em.dma("sp", lambda e, ch=ch, nk=nk, ci=ci: e.dma_start(
                        out=KSc[ci][0:64, :, 0:nk], in_=KSS[s_].rearrange("d (h n) -> d h n", h=4)[:, :, ch * 512: ch * 512 + nk]),
                        reads=[bSCT], writes=[bKSc[ci]])
                    for kh in range(4):
                        em.dma("pool", lambda e, kh=kh, ch=ch, nk=nk, ci=ci: e.dma_start(
                            out=KSc[ci][64:70, kh, 0:nk], in_=kaug_s[:, ch * 512: ch * 512 + nk]), writes=[bKSc[ci]], par=True)
                    em.dma("sp", lambda e, ch=ch, nk=nk, ci=ci: e.dma_start(
                        out=VSc[ci][:, 0:nk // 128, :], in_=VSS[s_][ch * 512: ch * 512 + nk, :].rearrange("(kb p) c -> p kb c", p=128)),
                        reads=[bSCT], writes=[bVSc[ci]])
                    for kbl in range(nk // 128):
                        kb = ch * 4 + kbl
                        for kh in range(4):
                            xi = rot("nx", 2)
                            em.op("dve", lambda e, xi=xi, kh=kh, kb=kb: e.tensor_copy(
                                out=NMXs[xi][0:8, :].rearrange("p (j r) -> p j r", r=64),
                                in_=NMts[0:8, kh, 2 * kb:2 * kb + 2].unsqueeze(2).to_broadcast([8, 2, 64])),
                                reads=[bTKs], writes=[bNX[xi]])
                            sc, bsc = sc_next()
                            em.op("pe", lambda e, sc=sc, kh=kh, kbl=kbl, ci=ci: e.matmul(
                                sc[:, 0:24], KSc[ci][0:70, kh, kbl * 128:(kbl + 1) * 128], qa(kh), start=True, stop=False),
                                reads=[bKSc[ci], bQAs], writes=[bsc], signal=False)
                            for hi in range(3):
                                em.op("pe", lambda e, sc=sc, hi=hi, xi=xi: e.matmul(
                                    sc[:, hi * 8:(hi + 1) * 8], NMXs[xi][0:8, :], IDB[0:8, 0:8], start=False, stop=(hi == 2),
                                    skip_group_check=True), reads=[bNX[xi], bC], writes=[bsc], signal=(hi == 2))
                            pi = exp_pb(sc, bsc)
                            if kb == 64:
                                select(pi, 0, -1, 1)
                            acc, bacc = (PS[0], bPS[0]) if kh < 2 else (PS[1], bPS[1])
                            pv(pi, acc, bacc, (kh % 2) * 195, VSc[ci][:, kbl, kh * 65:(kh + 1) * 65], bVSc[ci])
                em.op("act", lambda e: e.copy(out=OSs[0:8, 0:2, :], in_=PS[0][0:8, 0:390].rearrange("p (k c) -> p k c", k=2)),
                      reads=[bPS[0]], writes=[bO])
                em.op("act", lambda e: e.copy(out=OSs[0:8, 2:4, :], in_=PS[1][0:8, 0:390].rearrange("p (k c) -> p k c", k=2)),
                      reads=[bPS[1]], writes=[bO])
                em.op("dve", lambda e: e.memset(PS[2][0:8, 0:390], 0.0), writes=[bPS[2]])
                em.op("dve", lambda e: e.memset(PS[3][0:8, 0:390], 0.0), writes=[bPS[3]])
                for kh in range(4):
                    acc, bacc = (PS[2], bPS[2]) if kh < 2 else (PS[3], bPS[3])
                    for kb in range(5):
                        sc, bsc = sc_next()
                        em.op("pe", lambda e, sc=sc, kh=kh, kb=kb: e.matmul(sc[:, 0:24], KWs[0:70, kh, kb * 128:(kb + 1) * 128], qa(kh),
                                                                            start=True, stop=True), reads=[bKWs, bQAs], writes=[bsc])
                        pi = exp_pb(sc, bsc)
                        if kb == 0:
                            select(pi, 0, 1, -1)
                        if kb == 4:
                            select(pi, 0, -1, 1)
                        pv(pi, acc, bacc, (kh % 2) * 195, VWs[:, kb, kh * 65:(kh + 1) * 65], bKWs)
                em.op("act", lambda e: e.copy(out=OWs[0:8, 0:2, :], in_=PS[2][0:8, 0:390].rearrange("p (k c) -> p k c", k=2)),
                      reads=[bPS[2]], writes=[bO])
                em.op("act", lambda e: e.copy(out=OWs[0:8, 2:4, :], in_=PS[3][0:8, 0:390].rearrange("p (k c) -> p k c", k=2)),
                      reads=[bPS[3]], writes=[bO])
                for kh in range(4):
                    for br, Ob in enumerate((OCs, OSs, OWs)):
                        em.op("dve", lambda e, br=br, Ob=Ob, kh=kh: e.tensor_scalar(
                            out=R9s[0:8, br:9:3], in0=Ob[0:8, kh, :].rearrange("p (h c) -> p h c", c=65)[:, :, 64], scalar1=1e-30,
                            scalar2=None, op0=ALU.max), reads=[bO], writes=[bR9s])
                    em.op("dve", lambda e: e.reciprocal(out=R9s[0:8, :], in_=R9s[0:8, :]), reads=[bR9s], writes=[bR9s])
                    em.op("dve", lambda e, kh=kh: e.tensor_tensor(out=R9s[0:8, :], in0=R9s[0:8, :], in1=G36s[0:8, s_, kh * 9:(kh + 1) * 9],
                                                                  op=ALU.mult), reads=[bR9s, bG36s], writes=[bR9s])
                    for hi in range(3):
                        h = 3 * kh + hi
                        em.op("dve", lambda e, hi=hi, kh=kh: e.tensor_scalar(
                            out=TYs[0:8, :], in0=OCs[0:8, kh, hi * 65: hi * 65 + 64], scalar1=R9s[0:8, 3 * hi:3 * hi + 1], scalar2=None,
                            op0=ALU.mult), reads=[bO, bR9s], writes=[bTKs])
                        em.op("dve", lambda e, hi=hi, kh=kh: e.scalar_tensor_tensor(
                            out=TYs[0:8, :], in0=OSs[0:8, kh, hi * 65: hi * 65 + 64], scalar=R9s[0:8, 3 * hi + 1:3 * hi + 2],
                            in1=TYs[0:8, :], op0=ALU.mult, op1=ALU.add), reads=[bO, bR9s, bTKs], writes=[bTKs])
                        em.op("dve", lambda e, hi=hi, kh=kh, h=h: e.scalar_tensor_tensor(
                            out=Yts[0:8, s_, h * 64:(h + 1) * 64], in0=OWs[0:8, kh, hi * 65: hi * 65 + 64],
                            scalar=R9s[0:8, 3 * hi + 2:3 * hi + 3], in1=TYs[0:8, :], op0=ALU.mult, op1=ALU.add),
                            reads=[bO, bR9s, bTKs], writes=[bYs])
                pt, bpt = pt_next()
                for j in range(6):
                    em.op("pe", lambda e, j=j: e.transpose(pt[:, j * 8:(j + 1) * 8], Yts[0:8, s_, j * 128:(j + 1) * 128], IDB[0:8, 0:8]),
                          reads=[bYs, bC], writes=[bpt], signal=(j == 5))
                for j in range(6):
                    em.op("act", lambda e, j=j: e.copy(out=BA[:, j * T + s_ * 8: j * T + (s_ + 1) * 8], in_=pt[:, j * 8:(j + 1) * 8]),
                          reads=[bpt], writes=[bBA[j]])
            for s_ in range(4):
                one_seq(s_)
            for bi, (r0, nr) in enumerate(td.blocks):
                for half in range(2):
                    ps, bps = ps_next()
                    mm_group(ps[0:nr, :], bps,
                             [(BA[:, k * T + r0: k * T + r0 + nr], wa(W_OB, k, 1024, half * 512, 512)) for k in range(8)],
                             [bBA[k] for k in range(8)] + [bWA])
                    em.op("dve", lambda e, ps=ps, bi=bi, nr=nr, half=half: e.tensor_tensor(
                        out=XT[0][0:nr, bi, half * 512:(half + 1) * 512], in0=ps[0:nr, :],
                        in1=XT[0][0:nr, bi, half * 512:(half + 1) * 512], op=ALU.add),
                        reads=[bps, bXT[0]], writes=[bXT[0]])
            store_x(dst, dkey, td, 0)

        def phase_final(src, skey):
            load_gain(g_final[0:1, :])
            load_x(src, skey, tiles[0], 0)
            for i, td in enumerate(tiles):
                slot = i % 2
                if i + 1 < len(tiles):
                    load_x(src, skey, tiles[i + 1], (i + 1) % 2)
                xt = XT[slot]
                for bi, (r0, nr) in enumerate(td.blocks):
                    si = nxt("ss", 8)
                    em.op("pool", lambda e, si=si: e.memset(SS[:, si:si + 1], 0.0), writes=[bSS[si]])
                    em.op("act", lambda e, bi=bi, nr=nr, si=si, xt=xt: e.activation(
                        out=JK[0:nr, :], in_=xt[0:nr, bi, :], func=AF.Square, accum_out=SS[0:nr, si:si + 1]),
                        reads=[bXT[slot]], writes=[bJK, bSS[si]])
                    em.op("dve", lambda e, nr=nr, si=si: e.tensor_scalar(
                        out=SS[0:nr, si:si + 1], in0=SS[0:nr, si:si + 1], scalar1=1.0 / D, scalar2=EPS,
                        op0=ALU.mult, op1=ALU.add), reads=[bSS[si]], writes=[bSS[si]])
                    em.op("act", lambda e, nr=nr, si=si: e.sqrt(out=SS[0:nr, si:si + 1], in_=SS[0:nr, si:si + 1]),
                          reads=[bSS[si]], writes=[bSS[si]])
                    em.op("dve", lambda e, nr=nr, si=si: e.reciprocal(out=SS[0:nr, si:si + 1], in_=SS[0:nr, si:si + 1]),
                          reads=[bSS[si]], writes=[bSS[si]])
                    em.op("dve", lambda e, bi=bi, nr=nr, si=si, xt=xt: e.scalar_tensor_tensor(
                        out=xt[0:nr, bi, :], in0=xt[0:nr, bi, :], scalar=SS[0:nr, si:si + 1], in1=GB[0:nr, :],
                        op0=ALU.mult, op1=ALU.mult), reads=[bXT[slot], bSS[si], bGB], writes=[bXT[slot]])
                store_x(y_out, "y", td, slot)

        def run_all():
            nonlocal UT, ST8, SO8, CW, KB16, TS, VA, G, ZR, PTB, PTF, IDX, IOPF
            nonlocal W1, W1B, W2, PEV, BIAS, XCc, ACTK, ACTV, CKTt, CVt
            nonlocal QAs, Q1S, GMs, FRCs, G36s, CKs, CVs, KWs, VWs, KSc, VSc, PBs, NMXs, NMts, OCs, OSs, OWs, Yts, RCs, IMPs, SCRs, SC2s, M8s, R9s, TYs
            nonlocal VSB, KWB, VWB, CKB, CVB, QA, PB, Yt, NMX, G36, FRC, IMP, SCR, SC2, NMt, M8, R9, RC, TY, GM, Q1
            if stop_after == "consts":
                return
            with Scope() as sc:
                UT = sc.sb("UT", [128, 6, T + 2], F32)
                ST8 = sc.sb("ST8", [8, CONV], F32)
                SO8 = sc.sb("SO8", [8, CONV], F32)
                CW = sc.sb("CW", [128, 2, 6, 3], F32)
                KB16 = sc.sb("KB16", [128, 512], BF16)
                TS = sc.sb("TS", [64, 8, 128], BF16)
                VA = [sc.sb("VA%d" % i, [128, 4, 65], BF16) for i in range(2)]
                for l in range(2):
                    for k in range(3):
                        em.dma("sp", lambda e, l=l, k=k: e.dma_start(
                            out=CW[:, l, :, k], in_=conv_w[l, k, :].rearrange("(j p) -> p j", p=128),
                            allow_slow_non_contiguous=True), writes=[bC])
                phase_mem()
                if stop_after == "mem":
                    return
                cur, ckey = x_in, "in"
                for l in range(2):
                    phase_mixer_a(l, cur, ckey, XS[0], "s0")
                    if stop_after == "mixer%d" % l:
                        return
                    phase_ffn(l, 0, XS[0], "s0", None, None, XS[1], "s1")
                    phase_ffn(l, 1, None, None, XS[1], "s1", XS[2], "s2")
                    if stop_after == "ffnb%d" % l:
                        return
                    cur, ckey = XS[2], "s2"
                phase_kv(cur, ckey)
            if stop_after == "kv":
                return
            if SAMPLE_NSA:
                with Scope() as sc:
                    sc.sb("UTd", [128, 6, T + 2], F32)
                    sc.sb("ST8d", [8, CONV], F32)
                    sc.sb("SO8d", [8, CONV], F32)
                    sc.sb("CWd", [128, 2, 6, 3], F32)
                    a_old = [nc.lookup_mloc(KB16).addr, nc.lookup_mloc(TS).addr, nc.lookup_mloc(VA[0]).addr, nc.lookup_mloc(VA[1]).addr]
                    KB16 = sc.sb("KB16", [128, 512], BF16)
                    TS = sc.sb("TS", [64, 8, 128], BF16)
                    VA = [sc.sb("VA%d" % i, [128, 4, 65], BF16) for i in range(2)]
                    assert a_old == [nc.lookup_mloc(KB16).addr, nc.lookup_mloc(TS).addr, nc.lookup_mloc(VA[0]).addr, nc.lookup_mloc(VA[1]).addr]
                    G = [sc.sb("G%d" % i, [128, 512], F32) for i in range(2)]
                    ZR = sc.sb("ZR", [128, 260], BF16)
                    PTB = sc.sb("PTB", [128, 64], I32)
                    PTF = sc.sb("PTF", [128, 64], F32)
                    IDX = sc.sb("IDX", [128, 64], I32)
                    IOPF = sc.sb("IOPF", [128, 1], F32)
                    phase_sample_ctx()
                if stop_after == "sctx":
                    return
            with Scope(xt1=False) as sc:
                W1 = sc.sb("W1", [64, 2, 32, 128], BF16)
                W1B = sc.sb("W1B", [128, 2, 16, 128], BF16)
                W2 = sc.sb("W2", [128, 2, 64], BF16)
                PEV = sc.sb("PEV", [128, 2, 16], BF16)
                BIAS = sc.sb("BIAS", [128, 2], F32)
                XCc = sc.sb("XCc", [64, 8, 1040], BF16)
                ACTK = sc.sb("ACTK", [128, 4, 64], BF16)
                ACTV = sc.sb("ACTV", [128, 4, 128], BF16)
                CKTt = sc.sb("CKTt", [64, 4, 64], BF16)
                CVt = sc.sb("CVt", [128, 4, 65], BF16)
                jobs = [(XC_T, 4, CKT_S, 256, CV_S, bKVS, False)]
                if SAMPLE_NSA:
                    jobs += [(XCS[s_], 8, CKS[s_], 512, CVS[s_], bSCT, True) for s_ in range(4)]
                bCS_ = phase_compress(jobs)
                bCSs_box[0] = bCS_
            if stop_after == "cmp":
                return
            for l in (2, 3):
                with Scope(xt1=False) as sc:
                    VSB = sc.sb("VSB", [128, 32, 260], BF16)
                    KWB = sc.sb("KWB", [70, 4, 1024], BF16)
                    VWB = sc.sb("VWB", [128, 8, 260], BF16)
                    CKB = sc.sb("CKB", [70, 4, 256], BF16)
                    CVB = sc.sb("CVB", [128, 2, 260], BF16)
                    QA = sc.sb("QA", [70, 3, T], BF16)
                    PB = [sc.sb("PB%d" % i, [128, 384], BF16) for i in range(3)]
                    Yt = sc.sb("Yt", [128, 4, 768], BF16)
                    NMX = sc.sb("NMX", [128, 4096], BF16)
                    G36 = sc.sb("G36", [128, 4, 36], F32)
                    FRC = sc.sb("FRC", [128, 4, 64], F32)
                    IMP = sc.sb("IMP", [128, 64], F32)
                    SCR = sc.sb("SCR", [128, 64], F32)
                    SC2 = sc.sb("SC2", [128, 64], F32)
                    NMt = sc.sb("NMt", [128, 64], F32)
                    M8 = sc.sb("M8", [128, 24], F32)
                    R9 = sc.sb("R9", [128, 9], F32)
                    RC = sc.sb("RC", [128, 3], F32)
                    TY = sc.sb("TY", [128, 64], F32)
                    GM = sc.sb("GM", [128, 32], BF16)
                    Q1 = sc.sb("Q1", [128, T], BF16)
                    phase_mixer_b(l, cur, ckey, XS[0], "s0", bCS_)
                if stop_after == "mixer%d" % l:
                    return
                if SAMPLE_NSA:
                    with Scope(xt1=False) as sc:
                        QAs = sc.sb("QAs", [70, 12, NS], BF16)
                        Q1S = sc.sb("Q1S", [128, NS], BF16)
                        GMs = sc.sb("GMs", [128, 32], BF16)
                        FRCs = sc.sb("FRCs", [8, 130], F32)
                        G36s = sc.sb("G36s", [8, 4, 36], F32)
                        CKs = sc.sb("CKs", [70, 4, 512], BF16)
                        CVs = sc.sb("CVs", [128, 4, 260], BF16)
                        KWs = sc.sb("KWs", [70, 4, 640], BF16)
                        VWs = sc.sb("VWs", [128, 5, 260], BF16)
                        KSc = [sc.sb("KSc%d" % i, [70, 4, 512], BF16) for i in range(2)]
                        VSc = [sc.sb("VSc%d" % i, [128, 4, 260], BF16) for i in range(2)]
                        PBs = [sc.sb("PBs%d" % i, [128, 24], BF16) for i in range(3)]
                        NMXs = [sc.sb("NMXs%d" % i, [8, 128], BF16) for i in range(2)]
                        NMts = sc.sb("NMts", [8, 4, 130], F32)
                        OCs = sc.sb("OCs", [8, 4, 195], F32)
                        OSs = sc.sb("OSs", [8, 4, 195], F32)
                        OWs = sc.sb("OWs", [8, 4, 195], F32)
                        Yts = sc.sb("Yts", [8, 4, 768], BF16)
                        RCs = sc.sb("RCs", [8, 3], F32)
                        IMPs = sc.sb("IMPs", [8, 128], F32)
                        SCRs = sc.sb("SCRs", [8, 130], F32)
                        SC2s = sc.sb("SC2s", [8, 130], F32)
                        M8s = sc.sb("M8s", [8, 24], F32)
                        R9s = sc.sb("R9s", [8, 9], F32)
                        TYs = sc.sb("TYs", [8, 64], F32)
                        phase_mixer_b_sample(l, cur, ckey, XS[0], "s0")
                if stop_after == "smixer%d" % l:
                    return
                with Scope() as sc:
                    phase_ffn(l, 0, XS[0], "s0", None, None, XS[1], "s1")
                    phase_ffn(l, 1, None, None, XS[1], "s1", XS[2], "s2")
                cur, ckey = XS[2], "s2"
            with Scope(ht1=False) as sc:
                phase_final(cur, ckey)
        UT = ST8 = SO8 = CW = KB16 = TS = VA = G = ZR = PTB = PTF = IDX = IOPF = None
        W1 = W1B = W2 = PEV = BIAS = XCc = ACTK = ACTV = CKTt = CVt = None
        QAs = Q1S = GMs = FRCs = G36s = CKs = CVs = KWs = VWs = KSc = VSc = PBs = NMXs = NMts = OCs = OSs = OWs = Yts = None
        RCs = IMPs = SCRs = SC2s = M8s = R9s = TYs = None
        VSB = KWB = VWB = CKB = CVB = QA = PB = Yt = NMX = G36 = FRC = IMP = SCR = SC2 = NMt = M8 = R9 = RC = TY = GM = Q1 = None
        run_all()
        em.finish()
        em.emit()
    return nc


_CACHE = {}


def _trunc_bf16(x):
    x = np.ascontiguousarray(x, dtype=np.float32)
    return (x.view(np.uint32) & np.uint32(0xFFFF0000)).view(np.float32)


def _alibi_slopes(n):
    def pow2(m):
        start = 2.0 ** (-8.0 / m)
        return [start ** (i + 1) for i in range(m)]
    if n & (n - 1) == 0:
        return pow2(n)
    c = 2 ** int(np.floor(np.log2(n)))
    return pow2(c) + _alibi_slopes(2 * c)[0::2][: n - c]


def _const_tables():
    def key_rows(pos):
        a = (pos // 64) * 64
        b = pos % 64
        one = np.ones_like(pos)
        return np.stack([a, a, b, b, one, one]).astype(np.float32)
    slopes = np.array(_alibi_slopes(12), dtype=np.float32)
    sh = _trunc_bf16(slopes)
    sl = _trunc_bf16(slopes - sh)
    t = np.arange(4096, dtype=np.float64)
    qaug = np.zeros((12, 6, 4096), dtype=np.float32)
    for h in range(12):
        v = (np.float64(slopes[h]) * t).astype(np.float32)
        hi = _trunc_bf16(v)
        lo = _trunc_bf16(v - hi)
        qaug[h, 0], qaug[h, 1], qaug[h, 2], qaug[h, 3] = sh[h], sl[h], sh[h], sl[h]
        qaug[h, 4], qaug[h, 5] = -hi, -lo
    tt = np.arange(4096)
    cur = tt // 64
    frc = -np.ones((4096, 64), dtype=np.float32)
    frc[tt[cur >= 1], cur[cur >= 1] - 1] = 1.0e4
    frc[tt, cur] = 2.0e4
    frc[:, 0] = 3.0e4
    gm = (np.arange(128)[:, None] // 4 == np.arange(32)[None, :]).astype(np.float32)
    qaug_s = np.zeros((6, 12, 32), dtype=np.float32)
    for h in range(12):
        v = (np.float64(slopes[h]) * (8192.0 + np.arange(8))).astype(np.float32)
        hi = _trunc_bf16(v)
        lo = _trunc_bf16(v - hi)
        qaug_s[0, h], qaug_s[1, h], qaug_s[2, h], qaug_s[3, h] = sh[h], sl[h], sh[h], sl[h]
        qaug_s[4, h] = np.tile(-hi, 4)
        qaug_s[5, h] = np.tile(-lo, 4)
    frc_s = -np.ones((8, 130), dtype=np.float32)
    frc_s[:, 0], frc_s[:, 128], frc_s[:, 127], frc_s[:, 129] = 3.0e4, 2.0e4, 1.0e4, -1.0e30
    return {"kaug_t": key_rows(np.arange(4096)), "kaug_c": key_rows(16 * np.arange(256) + 31),
            "qaug_t": qaug, "frc_t": frc, "gm_t": gm,
            "kaug_s": key_rows(np.arange(8320)), "kaug_cs": key_rows(16 * np.arange(512) + 31),
            "kaug_ws": key_rows(7680 + np.arange(640)), "qaug_s": np.ascontiguousarray(qaug_s.reshape(6, 384)),
            "frc_s": frc_s}


def kernel(**inputs):
    f = lambda a: np.ascontiguousarray(np.asarray(a, dtype=np.float32))
    if "nc" not in _CACHE:
        _CACHE["nc"] = build_program(_CACHE.get("stop"))
    nc = _CACHE["nc"]
    xp = f(inputs["x_prompt"])
    xs = f(inputs["x_sample"])
    stc = f(inputs["state_conv"])
    cmemkv = f(inputs["cache_mem_kv"])
    win = f(inputs["state_win_kv"])
    shared = {
        "g_mix": f(inputs["g_mix"]), "w_in_a": f(inputs["w_in_a"]), "conv_w": f(inputs["conv_w"]),
        "w_o": f(inputs["w_o"]), "w_mkv": f(inputs["w_mkv"]), "g_mem": f(inputs["g_mem"]).reshape(1, D),
        "g_kv": f(inputs["g_kv"]).reshape(1, D), "w_kv": f(inputs["w_kv"]), "g_ffn": f(inputs["g_ffn"]),
        "w_gu": f(inputs["w_gu"]), "w_dn": f(inputs["w_dn"]), "g_final": f(inputs["g_final"]).reshape(1, D),
    }
    shared.update({k: f(inputs[k]) for k in ("w_in_b", "w1_ck", "w1_cv", "w2_ck", "w2_cv", "pe_ck", "pe_cv")})
    shared.update(_const_tables())
    shared["pool_c"] = f(inputs["cache_cmp_kv"]).reshape(2560 * 128, 512)
    shared["pool_s"] = f(inputs["cache_slc_kv"]).reshape(2560 * 128, 512)
    ptab = np.ascontiguousarray(np.asarray(inputs["page_table"], dtype=np.int32))
    in_maps = []
    for c in range(8):
        b = c % 4
        m = dict(shared)
        m["x_in"] = np.ascontiguousarray(np.concatenate([xp[b], xs[4 * c:4 * c + 4].reshape(NS, D)], axis=0))
        m["stc"] = np.ascontiguousarray(stc[:, 4 * c:4 * c + 4].reshape(2, 8, CONV))
        m["cmem"] = np.ascontiguousarray(cmemkv[:, 4 * c:4 * c + 4].reshape(4, 4, 256, 512))
        m["memp"] = f(inputs["mem_prompt"])[b]
        m["win_in"] = np.ascontiguousarray(win[4 * c:4 * c + 4].reshape(4, 512, 512))
        m["ptab"] = np.ascontiguousarray(ptab[4 * c:4 * c + 4])
        in_maps.append(m)
    res = run_bass_kernel_spmd(nc, in_maps, core_ids=list(range(8)))
    R = res.results
    y_prompt = np.stack([R[b]["y_out"][:SEQ] for b in range(4)])
    y_sample = np.concatenate([R[c]["y_out"][SEQ:].reshape(4, 8, D) for c in range(8)])
    conv_p = np.stack([R[b]["cs_p"] for b in range(4)], axis=1)
    conv_s = np.concatenate([R[c]["cs_s"].reshape(2, 4, 2, CONV) for c in range(8)], axis=1)
    mem_kv_p = np.stack([R[b]["mkv_p"] for b in range(4)], axis=1).reshape(4, 4, 256, 2, 4, 64)
    cmp_p = np.stack([R[b]["cmp_o"][:SEQ] for b in range(4)]).reshape(4, SEQ, 2, 4, 64)
    slc_p = np.stack([R[b]["slc_o"][:SEQ] for b in range(4)]).reshape(4, SEQ, 2, 4, 64)
    win_p = np.stack([R[b]["winp_o"] for b in range(4)]).reshape(4, 512, 2, 4, 64)
    cmp_s = np.concatenate([R[c]["cmp_o"][SEQ:].reshape(4, 8, 2, 4, 64) for c in range(8)])
    slc_s = np.concatenate([R[c]["slc_o"][SEQ:].reshape(4, 8, 2, 4, 64) for c in range(8)])
    win_s = np.concatenate([R[c]["wins_o"].reshape(4, 512, 2, 4, 64) for c in range(8)])
    return (y_prompt, y_sample, conv_p, conv_s, mem_kv_p, cmp_p, slc_p, win_p, cmp_s, slc_s, win_s)
```

```python
import contextlib
import numpy as np
import concourse.bass as bass
import concourse.mybir as mybir
from concourse.bass_utils import run_bass_kernel_spmd

F32 = mybir.dt.float32
BF16 = mybir.dt.bfloat16
I32 = mybir.dt.int32
AF = mybir.ActivationFunctionType
ALU = mybir.AluOpType
AX = mybir.AxisListType

ENGS = ("pe", "act", "dve", "pool", "sp")
DMA_K = 8
EPOCH = 8192

D = 1024
SEQ = 4096
NS = 32
NTOK = SEQ + NS
DFF = 2816
CONV = 768
T = 512
NT = SEQ // T
EPS = 1e-6
import os
DBG = int(os.environ.get('DBG', '9'))
DBGOUT = int(os.environ.get('DBGOUT', '0'))
SAMPLE_NSA = int(os.environ.get('SAMPLE_NSA', '1'))


class Buf:
    __slots__ = ("name", "w", "r", "rp", "excl")

    def __init__(self, name="", excl=False):
        self.name = name
        self.w = {}
        self.r = {}
        self.rp = {}
        self.excl = excl


class Emitter:
    def __init__(self, nc):
        self.nc = nc
        self.q = {e: [] for e in ENGS}
        self.cnt = {e: 0 for e in ENGS}
        self.waited = {e: {} for e in ENGS}
        self.dma_n = {e: 0 for e in ENGS}
        self.semkeys = set()

    def _need(self, eng, ev, waits):
        key, val = ev
        if key == ("e", "pe") and eng == "pe":
            return
        if self.waited[eng].get(key, 0) >= val:
            return
        waits[key] = max(waits.get(key, 0), val)

    def _deps(self, eng, reads, writes, par=False):
        waits = {}
        for b in reads:
            for k, v in b.w.items():
                self._need(eng, (k, v), waits)
        for b in writes:
            if not par:
                for k, v in b.w.items():
                    self._need(eng, (k, v), waits)
            else:
                for k, v in b.rp.items():
                    self._need(eng, (k, v), waits)
            for k, v in b.r.items():
                self._need(eng, (k, v), waits)
        for k, v in waits.items():
            self.waited[eng][k] = v
        return list(waits.items())

    def _mark(self, ev, reads, writes, par=False):
        k, v = ev
        for b in reads:
            if b.r.get(k, 0) < v:
                b.r[k] = v
        for b in writes:
            if par:
                b.w[k] = max(b.w.get(k, 0), v)
                for kk, vv in b.r.items():
                    b.rp[kk] = max(b.rp.get(kk, 0), vv)
            else:
                b.w = {k: v}
                b.rp = dict(b.r)
            b.r = {}

    def op(self, eng, fn, reads=(), writes=(), signal=True):
        writes = list(writes) + [b for b in reads if b.excl]
        reads = [b for b in reads if not b.excl]
        waits = self._deps(eng, reads, writes)
        if signal:
            self.cnt[eng] += 1
            ev = (("e", eng), self.cnt[eng])
            inc = (ev[0], ev[1], 1)
        else:
            ev = (("e", eng), self.cnt[eng] + 1)
            inc = None
        self.semkeys.add(ev[0])
        self.q[eng].append((waits, fn, inc))
        self._mark(ev, reads, writes)
        return ev

    def dma(self, eng, fn, reads=(), writes=(), par=False):
        n = self.dma_n[eng]
        self.dma_n[eng] += 1
        j, m = n % DMA_K, n // DMA_K
        key = ("d", eng, j)
        self.semkeys.add(key)
        waits = dict(self._deps(eng, reads, writes, par))
        if m > 0 and self.waited[eng].get(key, 0) < 16 * m:
            waits[key] = max(waits.get(key, 0), 16 * m)
            self.waited[eng][key] = waits[key]
        ev = (key, 16 * (m + 1))
        self.q[eng].append((list(waits.items()), fn, (key, ev[1], 16)))
        self._mark(ev, reads, writes, par)
        return ev

    def barrier(self):
        evs = []
        for eng in ENGS:
            n = self.dma_n[eng]
            for j in range(min(n, DMA_K)):
                cnt = (n - j + DMA_K - 1) // DMA_K
                evs.append((("d", eng, j), 16 * cnt))
            if self.cnt[eng] > 0:
                evs.append((("e", eng), self.cnt[eng]))
        for eng in ENGS:
            waits = {}
            for k, v in evs:
                if k == ("e", eng):
                    continue
                if self.waited[eng].get(k, 0) < v:
                    waits[k] = v
                    self.waited[eng][k] = v
            if waits:
                self.q[eng].append((list(waits.items()), None, None))

    def finish(self):
        waits = []
        for eng in ENGS:
            n = self.dma_n[eng]
            for j in range(min(n, DMA_K)):
                cnt = (n - j + DMA_K - 1) // DMA_K
                waits.append((("d", eng, j), 16 * cnt))
        for eng in ENGS:
            if eng != "sp" and self.cnt[eng] > 0:
                waits.append((("e", eng), self.cnt[eng]))
        self.q["sp"].append((waits, None, None))

    def emit(self):
        nc = self.nc
        with contextlib.ExitStack() as st:
            sems = {}
            for k in sorted(self.semkeys, key=str):
                if k[0] == "e":
                    for ep in range((self.cnt[k[1]] + EPOCH - 1) // EPOCH):
                        sems[(k, ep)] = st.enter_context(nc.semaphore("s_e_%s_%d" % (k[1], ep)))
                else:
                    sems[k] = st.enter_context(nc.semaphore("s_" + "_".join(str(x) for x in k)))

            def semval(k, v):
                if k[0] == "e":
                    ep = (v - 1) // EPOCH
                    return sems[(k, ep)], v - ep * EPOCH
                return sems[k], v

            block = st.enter_context(nc.Block())

            def runner(ops):
                def run(e):
                    for waits, fn, inc in ops:
                        for k, v in waits:
                            sm, vv = semval(k, v)
                            e.wait_ge(sm, vv)
                        if fn is not None:
                            ins = fn(e)
                            if inc is not None:
                                k, v, amt = inc
                                ins.then_inc(semval(k, v)[0], amt)
                return run

            if self.q["pe"]:
                block.tensor(runner(self.q["pe"]))
            if self.q["act"]:
                block.scalar(runner(self.q["act"]))
            if self.q["dve"]:
                block.vector(runner(self.q["dve"]))
            if self.q["pool"]:
                block.gpsimd(runner(self.q["pool"]))
            if self.q["sp"]:
                block.sync(runner(self.q["sp"]))


class TileDesc:
    def __init__(self, idx):
        self.idx = idx
        self.sample = idx == NT
        if self.sample:
            self.row0, self.T, self.blocks = SEQ, NS, [(0, NS)]
        else:
            self.row0, self.T, self.blocks = idx * T, T, [(i * 128, 128) for i in range(4)]


def build_program(stop_after=None):
    nc = bass.Bass("TRN2", target_bir_lowering=False)
    em = Emitter(nc)

    def din(name, shape, dt=F32):
        return nc.dram_tensor(name, list(shape), dt, kind="ExternalInput").ap()

    def dout(name, shape, dt=F32):
        return nc.dram_tensor(name, list(shape), dt, kind="ExternalOutput").ap()

    def dscr(name, shape, dt=F32):
        return nc.dram_tensor(name, list(shape), dt, kind=("ExternalOutput" if DBGOUT else "Internal")).ap()

    x_in = din("x_in", [NTOK, D])
    stc = din("stc", [2, 8, CONV])
    cmem = din("cmem", [4, 4, 256, 512])
    memp = din("memp", [256, D])
    g_mix = din("g_mix", [4, D])
    w_in_a = din("w_in_a", [2, D, 2560])
    conv_w = din("conv_w", [2, 3, CONV])
    w_o = din("w_o", [4, D, D])
    w_mkv = din("w_mkv", [4, D, 512])
    g_mem = din("g_mem", [1, D])
    g_kv = din("g_kv", [1, D])
    w_kv = din("w_kv", [D, 1536])
    g_ffn = din("g_ffn", [4, D])
    w_gu = din("w_gu", [4, D, 2 * DFF])
    w_dn = din("w_dn", [4, DFF, D])
    g_final = din("g_final", [1, D])
    win_in = din("win_in", [4, 512, 512])

    y_out = dout("y_out", [NTOK, D])
    cs_p = dout("cs_p", [2, 2, CONV])
    cs_s = dout("cs_s", [2, 8, CONV])
    mkv_p = dout("mkv_p", [4, 256, 512])
    cmp_o = dout("cmp_o", [NTOK, 512])
    slc_o = dout("slc_o", [NTOK, 512])
    winp_o = dout("winp_o", [512, 512])
    wins_o = dout("wins_o", [4, 512, 512])

    w_in_b = din("w_in_b", [2, D, 1060])
    w1_c = [din("w1_ck", [32, 64, 128]), din("w1_cv", [32, 64, 128])]
    w2_c = [din("w2_ck", [128, 64]), din("w2_cv", [128, 64])]
    pe_c = [din("pe_ck", [32, 64]), din("pe_cv", [32, 64])]
    kaug_t = din("kaug_t", [6, 4096])
    kaug_c = din("kaug_c", [6, 256])
    qaug_t = din("qaug_t", [12, 6, 4096])
    frc_t = din("frc_t", [4096, 64])
    gm_t = din("gm_t", [128, 32])
    npool_rows = 2560 * 128 if SAMPLE_NSA else 128
    pool_c = din("pool_c", [npool_rows, 512])
    pool_s = din("pool_s", [npool_rows, 512])
    ptab = din("ptab", [4, 64], I32)
    kaug_s = din("kaug_s", [6, 8320])
    kaug_cs = din("kaug_cs", [6, 512])
    kaug_ws = din("kaug_ws", [6, 640])
    qaug_s = din("qaug_s", [6, 384])
    frc_s = din("frc_s", [8, 130])
    LS, LW, LC = 8320, 640, 8208
    XCS = dscr("xcs", [4, 64, 8 * LC], BF16)
    KSS = dscr("kss", [4, 64, 4 * LS], BF16)
    VSS = dscr("vss", [4, LS, 260], BF16)
    KWS = dscr("kws", [4, 64, 4 * LW], BF16)
    VWS = dscr("vws", [4, LW, 260], BF16)
    CKS = dscr("cks", [4, 64, 4 * 512], BF16)
    CVS = dscr("cvs", [4, 512, 260], BF16)
    KS_T = dscr("ks_t", [64, 4 * NTOK], BF16)
    KW_T = dscr("kw_t", [64, 4 * NTOK], BF16)
    XC_T = dscr("xc_t", [64, 8 * NTOK], BF16)
    VS_S = dscr("vs_s", [NTOK, 260], BF16)
    VW_S = dscr("vw_s", [NTOK, 260], BF16)
    CKT_S = dscr("ckt_s", [64, 4 * 256], BF16)
    CV_S = dscr("cv_s", [256, 260], BF16)
    XS = [dscr("xs%d" % i, [NTOK, D]) for i in range(3)]
    HTS = dscr("hts", [NT + 1, 128, 8 * T], BF16)
    WINP = dscr("winp_s", [NTOK, 512])
    xbufs = {}

    def xb(key, ti):
        return xbufs.setdefault((key, ti), Buf("x%s_%d" % (key, ti)))

    tiles = [TileDesc(i) for i in range(NT + 1)]

    with contextlib.ExitStack() as st:
        def sb(name, shape, dt):
            return st.enter_context(nc.sbuf_tensor(name, list(shape), dt))

        WA = sb("WA", [128, 33792], BF16)
        XT = [sb("XT0", [128, 4, D], F32), None]
        HT = [sb("HT0", [128, 8, T], BF16), None]
        Hh = sb("Hh", [128, 4, D], BF16)
        BA = sb("BA", [128, 11 * T], BF16)
        CS = [sb("CS%d" % i, [128, T], F32) for i in range(2)]
        TM = [sb("TM%d" % i, [128, T], F32) for i in range(2)]
        STG = [sb("STG%d" % i, [128, 512], F32) for i in range(2)]
        GB = sb("GB", [128, D], F32)
        IDB = sb("IDB", [128, 128], BF16)
        IDB3 = sb("IDB3", [128, 384], BF16)
        IDF = sb("IDF", [128, 128], F32)
        IOT = sb("IOT", [128, 128], I32)
        ONE = sb("ONE", [128, 128], BF16)
        MKT = sb("MKT", [128, 4, 2, 256], BF16)
        VP = sb("VP", [128, 2, 4, 256], BF16)
        SKT = sb("SKT", [128, 4, 2, 256], BF16)
        SKS = sb("SKS", [128, 2, 4, 256], BF16)
        SV = sb("SV", [128, 2, 4, 256], BF16)
        SS = sb("SS", [128, 8], F32)
        JK = sb("JK", [128, D], BF16)
        scope_n = [0]
        addr_chk = {}

        class Scope:
            def __init__(self, xt1=True, ht1=True):
                self.xt1, self.ht1 = xt1, ht1

            def __enter__(self):
                self.st = contextlib.ExitStack()
                self.st.__enter__()
                scope_n[0] += 1
                if self.xt1:
                    XT[1] = self.sb("XT1", [128, 4, D], F32)
                    a = nc.lookup_mloc(XT[1]).addr
                    assert addr_chk.setdefault("xt1", a) == a
                    if self.ht1:
                        HT[1] = self.sb("HT1", [128, 8, T], BF16)
                        a = nc.lookup_mloc(HT[1]).addr
                        assert addr_chk.setdefault("ht1", a) == a
                return self

            def sb(self, name, shape, dt):
                return self.st.enter_context(nc.sbuf_tensor("%s_s%d" % (name, scope_n[0]), list(shape), dt))

            def __exit__(self, *a):
                em.barrier()
                self.st.__exit__(None, None, None)
                return False

        PT = [st.enter_context(nc.psum_tensor("PT%d" % i, [128, 1024], BF16)) for i in range(2)]
        PS = [st.enter_context(nc.psum_tensor("PS%d" % i, [128, 512], F32)) for i in range(6)]
        bPT = [Buf("PT%d" % i, True) for i in range(2)]
        bPS = [Buf("PS%d" % i, True) for i in range(6)]
        rr = {"ps": 0, "pt": 0, "cs": 0, "tm": 0, "stg": 0, "ss": 0}

        def nxt(kind, n):
            i = rr[kind]
            rr[kind] = (i + 1) % n
            return i

        def ps_next():
            i = nxt("ps", 6)
            return PS[i], bPS[i]

        def pt_next():
            i = nxt("pt", 2)
            return PT[i], bPT[i]

        bWA = Buf("WA")
        bXT = [Buf("XT0"), Buf("XT1")]
        bHT = [Buf("HT0"), Buf("HT1")]
        bHh = [Buf("Hh%d" % i) for i in range(4)]
        bBA = [Buf("BA%d" % i) for i in range(11)]
        bUT = [Buf("UT%d" % i) for i in range(6)]
        bCS = [Buf("CS0"), Buf("CS1")]
        bTM = [Buf("TM0"), Buf("TM1")]
        bSTG = [Buf("STG0"), Buf("STG1")]
        bGB = Buf("GB")
        bC = Buf("consts")
        bMKT = Buf("MKT")
        bVP = Buf("VP")
        bSKT = Buf("SKT")
        bSKS = Buf("SKS")
        bSV = Buf("SV")
        bSS = [Buf("SS%d" % i) for i in range(8)]
        bJK = Buf("JK")
        bST8 = Buf("ST8")
        bSO8 = Buf("SO8")
        bOUT = Buf("outs")

        em.op("pool", lambda e: e.iota(IOT[:, :], [[-1, 128]], base=0, channel_multiplier=1), writes=[bC])
        em.op("dve", lambda e: e.tensor_single_scalar(out=IDF[:, :], in_=IOT[:, :], scalar=0, op=ALU.is_equal),
              reads=[bC], writes=[bC])
        em.op("dve", lambda e: e.tensor_copy(out=IDB[:, :], in_=IDF[:, :]), reads=[bC], writes=[bC])
        for hi in range(3):
            em.op("dve", lambda e, hi=hi: e.tensor_copy(out=IDB3[:, hi * 128:(hi + 1) * 128], in_=IDF[:, :]), reads=[bC], writes=[bC])
        em.op("pool", lambda e: e.memset(ONE[:, :], 1.0), writes=[bC])
        def load_gain(g_ap_row):
            em.dma("sp", lambda e: e.dma_start(out=GB[:, :], in_=g_ap_row.partition_broadcast(128)), writes=[bGB])

        def load_w(dst_off, src, kch, ncols, col0=0, first=True):
            step = 2 if kch % 2 == 0 else 1
            for k0 in range(0, kch, step):
                def f(e, k0=k0):
                    dst = WA[:, dst_off + k0 * ncols: dst_off + (k0 + step) * ncols].rearrange(
                        "p (k c) -> p k c", k=step)
                    s_ = src[k0 * 128:(k0 + step) * 128, col0:col0 + ncols].rearrange("(k p) c -> p k c", p=128)
                    return e.dma_start(out=dst, in_=s_)
                em.dma("pool", f, writes=[bWA], par=not (first and k0 == 0))

        def wa(off, k, ncols, c0, n):
            return WA[:, off + k * ncols + c0: off + k * ncols + c0 + n]

        def load_x(src, key, td, slot):
            if td.sample:
                em.dma("sp", lambda e: e.dma_start(out=XT[slot][0:NS, 0, :], in_=src[SEQ:SEQ + NS, :]),
                       reads=[xb(key, td.idx)], writes=[bXT[slot]])
            else:
                em.dma("sp", lambda e: e.dma_start(
                    out=XT[slot][:, :, :], in_=src[td.row0:td.row0 + T, :].rearrange("(b p) d -> p b d", p=128)),
                    reads=[xb(key, td.idx)], writes=[bXT[slot]])

        def store_x(dst, key, td, slot):
            if td.sample:
                em.dma("sp", lambda e: e.dma_start(out=dst[SEQ:SEQ + NS, :], in_=XT[slot][0:NS, 0, :]),
                       reads=[bXT[slot]], writes=[xb(key, td.idx)])
            else:
                em.dma("sp", lambda e: e.dma_start(
                    out=dst[td.row0:td.row0 + T, :].rearrange("(b p) d -> p b d", p=128), in_=XT[slot][:, :, :]),
                    reads=[bXT[slot]], writes=[xb(key, td.idx)])

        def norm_T(td, xslot, hslot, src_t=None, src_b=None):
            xt = XT[xslot] if src_t is None else src_t
            bx = bXT[xslot] if src_b is None else src_b
            for bi, (r0, nr) in enumerate(td.blocks):
                si = nxt("ss", 8)
                em.op("pool", lambda e, si=si: e.memset(SS[:, si:si + 1], 0.0), writes=[bSS[si]])
                em.op("act", lambda e, bi=bi, nr=nr, si=si: e.activation(
                    out=JK[0:nr, :], in_=xt[0:nr, bi, :], func=AF.Square, accum_out=SS[0:nr, si:si + 1]),
                    reads=[bx], writes=[bJK, bSS[si]])
                em.op("dve", lambda e, nr=nr, si=si: e.tensor_scalar(
                    out=SS[0:nr, si:si + 1], in0=SS[0:nr, si:si + 1], scalar1=1.0 / D, scalar2=EPS,
                    op0=ALU.mult, op1=ALU.add), reads=[bSS[si]], writes=[bSS[si]])
                em.op("act", lambda e, nr=nr, si=si: e.sqrt(out=SS[0:nr, si:si + 1], in_=SS[0:nr, si:si + 1]),
                      reads=[bSS[si]], writes=[bSS[si]])
                em.op("dve", lambda e, nr=nr, si=si: e.reciprocal(out=SS[0:nr, si:si + 1], in_=SS[0:nr, si:si + 1]),
                      reads=[bSS[si]], writes=[bSS[si]])
                em.op("dve", lambda e, bi=bi, nr=nr, si=si: e.scalar_tensor_tensor(
                    out=Hh[0:nr, bi, :], in0=xt[0:nr, bi, :], scalar=SS[0:nr, si:si + 1], in1=GB[0:nr, :],
                    op0=ALU.mult, op1=ALU.mult), reads=[bx, bSS[si], bGB], writes=[bHh[bi]])
            TT = td.T
            for kk in range(4):
                pt, bpt = pt_next()
                nb = len(td.blocks)
                for k2 in range(2):
                    k = kk * 2 + k2
                    for bi, (r0, nr) in enumerate(td.blocks):
                        last = (k2 == 1 and bi == nb - 1)
                        em.op("pe", lambda e, k=k, k2=k2, bi=bi, r0=r0, nr=nr, pt=pt: e.transpose(
                            pt[:, k2 * 512 + r0: k2 * 512 + r0 + nr], Hh[0:nr, bi, k * 128:(k + 1) * 128],
                            IDB[0:nr, 0:nr]), reads=[bHh[bi], bC], writes=[bpt], signal=last)
                eng = "act" if kk % 2 == 0 else "dve"
                if eng == "act":
                    em.op("act", lambda e, kk=kk, pt=pt: e.copy(
                        out=HT[hslot][:, 2 * kk:2 * kk + 2, 0:TT],
                        in_=pt[:, :].rearrange("p (a t) -> p a t", a=2)[:, :, 0:TT]),
                        reads=[bpt], writes=[bHT[hslot]])
                else:
                    em.op("dve", lambda e, kk=kk, pt=pt: e.tensor_copy(
                        out=HT[hslot][:, 2 * kk:2 * kk + 2, 0:TT],
                        in_=pt[:, :].rearrange("p (a t) -> p a t", a=2)[:, :, 0:TT]),
                        reads=[bpt], writes=[bHT[hslot]])

        def mm_group(out_ap, bout, pairs, extra_reads):
            n = len(pairs)
            for i, (l_, r_) in enumerate(pairs):
                em.op("pe", lambda e, l_=l_, r_=r_, i=i: e.matmul(out_ap, l_, r_, start=(i == 0), stop=(i == n - 1)),
                      reads=extra_reads, writes=[bout], signal=(i == n - 1))

        def phase_mem():
            load_gain(g_mem[0:1, :])
            for l in range(4):
                load_w(l * 4096, w_mkv[l], 8, 512, first=(l == 0))
            em.dma("sp", lambda e: e.dma_start(out=XT[0][:, 0:2, :], in_=memp.rearrange("(b p) d -> p b d", p=128)),
                   writes=[bXT[0]])
            td = TileDesc(0)
            td.T, td.blocks = 256, [(0, 128), (128, 128)]
            if DBG >= 1:
                norm_T(td, 0, 0)
            for l in range(4 if DBG >= 2 else 0):
                for blk in range(2):
                    ps, bps = ps_next()
                    mm_group(ps[:, :], bps, [(HT[0][:, k, blk * 128:(blk + 1) * 128], wa(l * 4096, k, 512, 0, 512))
                                             for k in range(8)], [bHT[0], bWA])
                    si = nxt("stg", 2)
                    em.op("act", lambda e, ps=ps, si=si: e.copy(out=STG[si][:, :], in_=ps[:, :]),
                          reads=[bps], writes=[bSTG[si]])
                    em.op("dve", lambda e, ps=ps, blk=blk, l=l: e.tensor_copy(out=VP[:, blk, l, :], in_=ps[:, 256:512]),
                          reads=[bps], writes=[bVP])
                    em.dma("sp", lambda e, si=si, l=l, blk=blk: e.dma_start(
                        out=mkv_p[l, blk * 128:(blk + 1) * 128, :], in_=STG[si][:, :]), reads=[bSTG[si]], writes=[])
                for hp in range(2 if DBG >= 3 else 0):
                    ps, bps = ps_next()
                    mm_group(ps[:, 0:256], bps, [(wa(l * 4096, k, 512, hp * 128, 128), HT[0][:, k, 0:256])
                                                 for k in range(8)], [bHT[0], bWA])
                    em.op("act", lambda e, ps=ps, l=l, hp=hp: e.copy(out=MKT[:, l, hp, :], in_=ps[:, 0:256]),
                          reads=[bps], writes=[bMKT])

        def load_sample_mem(l):
            for s in range(4):
                em.dma("pool", lambda e, s=s: e.dma_start(
                    out=SV[:, :, s, :], in_=cmem[l, s, :, 256:512].rearrange("(c p) f -> p c f", p=128)), writes=[bSV])
                em.dma("pool", lambda e, s=s: e.dma_start(
                    out=SKS[:, :, s, :], in_=cmem[l, s, :, 0:256].rearrange("(c p) f -> p c f", p=128)), writes=[bSKS])
            for s in range(4):
                pt, bpt = pt_next()
                for hp in range(2):
                    for c in range(2):
                        em.op("pe", lambda e, s=s, hp=hp, c=c, pt=pt: e.transpose(
                            pt[:, hp * 256 + c * 128: hp * 256 + (c + 1) * 128],
                            SKS[:, c, s, hp * 128:(hp + 1) * 128], IDB[:, :]),
                            reads=[bSKS, bC], writes=[bpt], signal=(hp == 1 and c == 1))
                em.op("dve", lambda e, s=s, pt=pt: e.tensor_copy(
                    out=SKT[:, s, :, :], in_=pt[:, 0:512].rearrange("p (h m) -> p h m", h=2)),
                    reads=[bpt], writes=[bSKT])

        def mem_attention(td, l, q_ps, q_bufs):
            TT = td.T
            groups = [(0, TT, None)] if not td.sample else [(s * 8, 8, s) for s in range(4)]
            for hp in range(2):
                for (c0, n, s) in groups:
                    for half in range(2):
                        hs = slice(half * 64, half * 64 + 64)
                        for mc in range(2):
                            ps, bps = ps_next()
                            if s is None:
                                kT = MKT[hs, l, hp, mc * 128:(mc + 1) * 128]
                                kb = bMKT
                            else:
                                kT = SKT[hs, s, hp, mc * 128:(mc + 1) * 128]
                                kb = bSKT
                            mm_group(ps[:, 0:n], bps, [(kT, q_ps[hp][hs, c0:c0 + n])], [kb, q_bufs[hp]])
                            em.op("act", lambda e, ps=ps, mc=mc, n=n: e.activation(
                                out=BA[:, 8 * T + mc * T: 8 * T + mc * T + n], in_=ps[:, 0:n], func=AF.Exp, scale=0.125),
                                reads=[bps], writes=[bBA[8 + mc]])
                        psn, bpsn = ps_next()
                        psd, bpsd = ps_next()
                        if s is None:
                            vv = [VP[:, mc, l, hp * 128:(hp + 1) * 128] for mc in range(2)]
                            vb = bVP
                        else:
                            vv = [SV[:, mc, s, hp * 128:(hp + 1) * 128] for mc in range(2)]
                            vb = bSV
                        pp = [BA[:, 8 * T + mc * T: 8 * T + mc * T + n] for mc in range(2)]
                        mm_group(psn[:, 0:n], bpsn, [(vv[mc], pp[mc]) for mc in range(2)], [vb, bBA[8], bBA[9]])
                        mm_group(psd[:, 0:n], bpsd, [(ONE[:, :], pp[mc]) for mc in range(2)], [bC, bBA[8], bBA[9]])
                        ti = nxt("tm", 2)
                        em.op("dve", lambda e, psd=psd, ti=ti, n=n, hs=hs: e.reciprocal(out=TM[ti][hs, 0:n], in_=psd[hs, 0:n]),
                              reads=[bpsd], writes=[bTM[ti]])
                        em.op("dve", lambda e, psn=psn, ti=ti, n=n, hs=hs, hp=hp, c0=c0: e.tensor_tensor(
                            out=BA[hs, (6 + hp) * T + c0:(6 + hp) * T + c0 + n], in0=psn[hs, 0:n], in1=TM[ti][hs, 0:n],
                            op=ALU.mult), reads=[bpsn, bTM[ti]], writes=[bBA[6 + hp]])

        W_IN, W_O = 0, 8 * 2560

        def uwin(td, j, k):
            if td.sample:
                return UT[:, j, 0:40].rearrange("p (s t) -> p s t", t=10)[:, :, k:k + 8]
            return UT[:, j, k:k + T]

        def tv(td, ap):
            if td.sample:
                return ap.rearrange("p (s t) -> p s t", t=8)
            return ap

        def mixer_a_tile(td, l, xslot):
            TT = td.T
            norm_T(td, xslot, 0)
            hT = HT[0]
            qaps = []
            for hp in range(2):
                ps, bps = ps_next()
                mm_group(ps[:, 0:TT], bps, [(wa(W_IN, k, 2560, 2304 + hp * 128, 128), hT[:, k, 0:TT]) for k in range(8)],
                         [bHT[0], bWA])
                qaps.append(None)
                qap = (BA[:, 10 * T: 10 * T + TT] if hp == 0 else HT[1][:, 0, 0:TT])
                qb = bBA[10] if hp == 0 else bHT[1]
                em.op("act", lambda e, ps=ps, qap=qap: e.copy(out=qap, in_=ps[:, 0:TT]), reads=[bps], writes=[qb])
                qaps[hp] = (qap, qb)
            mem_attention(td, l, [qaps[0][0], qaps[1][0]], [qaps[0][1], qaps[1][1]])
            for j in range(6):
                psC, bC_ = ps_next()
                mm_group(psC[:, 0:TT], bC_, [(wa(W_IN, k, 2560, CONV + j * 128, 128), hT[:, k, 0:TT]) for k in range(8)],
                         [bHT[0], bWA])
                psH, bH_ = ps_next()
                mm_group(psH[:, 0:TT], bH_, [(wa(W_IN, k, 2560, 2 * CONV + j * 128, 128), hT[:, k, 0:TT]) for k in range(8)],
                         [bHT[0], bWA])
                psB, bB_ = ps_next()
                mm_group(psB[:, 0:TT], bB_, [(wa(W_IN, k, 2560, j * 128, 128), hT[:, k, 0:TT]) for k in range(8)],
                         [bHT[0], bWA])
                ci = nxt("cs", 2)
                em.op("act", lambda e, psC=psC, ci=ci: e.copy(out=CS[ci][:, 0:TT], in_=psC[:, 0:TT]),
                      reads=[bC_], writes=[bCS[ci]])
                em.op("dve", lambda e, psH=psH, ci=ci, j=j: e.tensor_tensor(
                    out=uwin(td, j, 2), in0=tv(td, psH[:, 0:TT]), in1=tv(td, CS[ci][:, 0:TT]), op=ALU.mult),
                    reads=[bH_, bCS[ci]], writes=[bUT[j]])
                ti = nxt("tm", 2)
                em.op("pool", lambda e, ti=ti, j=j: e.tensor_scalar(
                    out=tv(td, TM[ti][:, 0:TT]), in0=uwin(td, j, 0), scalar1=CW[:, l, j, 0:1], scalar2=None,
                    op0=ALU.mult), reads=[bUT[j], bC], writes=[bTM[ti]])
                for k in (1, 2):
                    em.op("dve", lambda e, ti=ti, j=j, k=k: e.scalar_tensor_tensor(
                        out=tv(td, TM[ti][:, 0:TT]), in0=uwin(td, j, k), scalar=CW[:, l, j, k:k + 1],
                        in1=tv(td, TM[ti][:, 0:TT]), op0=ALU.mult, op1=ALU.add),
                        reads=[bUT[j], bC, bTM[ti]], writes=[bTM[ti]])
                em.op("dve", lambda e, psB=psB, ti=ti, j=j: e.tensor_tensor(
                    out=BA[:, j * T: j * T + TT], in0=psB[:, 0:TT], in1=TM[ti][:, 0:TT], op=ALU.mult),
                    reads=[bB_, bTM[ti]], writes=[bBA[j]])
            for bi, (r0, nr) in enumerate(td.blocks):
                for half in range(2):
                    ps, bps = ps_next()
                    mm_group(ps[0:nr, :], bps,
                             [(BA[:, k * T + r0: k * T + r0 + nr], wa(W_O, k, 1024, half * 512, 512)) for k in range(8)],
                             [bBA[k] for k in range(8)] + [bWA])
                    em.op("dve", lambda e, ps=ps, bi=bi, nr=nr, half=half: e.tensor_tensor(
                        out=XT[xslot][0:nr, bi, half * 512:(half + 1) * 512], in0=ps[0:nr, :],
                        in1=XT[xslot][0:nr, bi, half * 512:(half + 1) * 512], op=ALU.add),
                        reads=[bps, bXT[xslot]], writes=[bXT[xslot]])

        def conv_state_out(td, l):
            if td.sample:
                nrow = 8
                def src(j):
                    return UT[:, j, 0:40].rearrange("p (s t) -> p s t", t=10)[:, :, 8:10]
                dst = cs_s[l, :, :]
            else:
                nrow = 2
                def src(j):
                    return UT[:, j, T:T + 2]
                dst = cs_p[l, :, :]
            for j in range(6):
                ti = nxt("tm", 2)
                em.op("dve", lambda e, j=j, ti=ti: e.tensor_copy(
                    out=(TM[ti][:, 0:nrow].rearrange("p (s t) -> p s t", t=2) if td.sample else TM[ti][:, 0:nrow]),
                    in_=src(j)), reads=[bUT[j]], writes=[bTM[ti]])
                ps, bps = ps_next()
                em.op("pe", lambda e, ps=ps, ti=ti: e.transpose(ps[0:nrow, 0:128], TM[ti][:, 0:nrow], IDF[:, :]),
                      reads=[bTM[ti], bC], writes=[bps])
                em.op("act", lambda e, ps=ps, j=j: e.copy(out=SO8[0:nrow, j * 128:(j + 1) * 128], in_=ps[0:nrow, 0:128]),
                      reads=[bps], writes=[bSO8])
            em.dma("sp", lambda e: e.dma_start(out=dst, in_=SO8[0:nrow, :]), reads=[bSO8], writes=[])

        def phase_mixer_a(l, src, skey, dst, dkey):
            load_gain(g_mix[l:l + 1, :])
            load_w(W_IN, w_in_a[l], 8, 2560)
            load_w(W_O, w_o[l], 8, 1024, first=False)
            load_sample_mem(l)
            em.op("pool", lambda e: e.memset(UT[:, :, 0:2], 0.0), reads=bUT, writes=bUT)
            load_x(src, skey, tiles[0], 0)
            for i, td in enumerate(tiles):
                slot = i % 2
                if i + 1 < len(tiles):
                    load_x(src, skey, tiles[i + 1], (i + 1) % 2)
                if td.sample:
                    em.dma("sp", lambda e: e.dma_start(out=ST8[:, :], in_=stc[l, :, :]), writes=[bST8])
                    for j in range(6):
                        ps, bps = ps_next()
                        em.op("pe", lambda e, ps=ps, j=j: e.transpose(ps[:, 0:8], ST8[0:8, j * 128:(j + 1) * 128], IDF[0:8, 0:8]),
                              reads=[bST8, bC], writes=[bps])
                        em.op("act", lambda e, ps=ps, j=j: e.copy(
                            out=UT[:, j, 0:40].rearrange("p (s t) -> p s t", t=10)[:, :, 0:2],
                            in_=ps[:, 0:8].rearrange("p (s t) -> p s t", t=2)), reads=[bps], writes=[bUT[j]])
                mixer_a_tile(td, l, slot)
                store_x(dst, dkey, td, slot)
                if td.idx == NT - 1 or td.sample:
                    conv_state_out(td, l)
                elif not td.sample:
                    em.op("pool", lambda e: e.tensor_copy(out=UT[:, :, 0:2], in_=UT[:, :, T:T + 2]), reads=bUT, writes=bUT)

        G_OFF, D_OFF = 0, 8 * 2816

        def phase_ffn(l, half, src, skey, acc, akey, dst, dkey):
            c0 = half * 1408
            if half == 0:
                load_gain(g_ffn[l:l + 1, :])
            for k0 in range(0, 8, 2):
                for part in range(2):
                    def f(e, k0=k0, part=part):
                        dst_ = WA[:, k0 * 2816:(k0 + 2) * 2816].rearrange("p (k c) -> p k c", k=2)[:, :, part * 1408:(part + 1) * 1408]
                        s_ = w_gu[l, k0 * 128:(k0 + 2) * 128, part * DFF + c0: part * DFF + c0 + 1408].rearrange(
                            "(k p) c -> p k c", p=128)
                        return e.dma_start(out=dst_, in_=s_)
                    em.dma("pool", f, writes=[bWA], par=not (k0 == 0 and part == 0))
            for k0 in range(0, 11):
                em.dma("pool", lambda e, k0=k0: e.dma_start(
                    out=WA[:, D_OFF + k0 * 1024: D_OFF + (k0 + 1) * 1024],
                    in_=w_dn[l, c0 + k0 * 128: c0 + (k0 + 1) * 128, :]), writes=[bWA], par=True)

            def loads(i):
                td = tiles[i]
                slot = i % 2
                if half == 0:
                    load_x(src, skey, td, slot)
                else:
                    load_x(acc, akey, td, slot)
                    em.dma("sp", lambda e: e.dma_start(out=HT[slot][:, :, :], in_=HTS[td.idx].rearrange("p (k t) -> p k t", k=8)),
                           reads=[xb("hts", td.idx)], writes=[bHT[slot]])

            def ffn_tile(td, slot):
                TT = td.T
                if half == 0:
                    norm_T(td, slot, slot)
                    em.dma("sp", lambda e: e.dma_start(
                        out=HTS[td.idx].rearrange("p (k t) -> p k t", k=8), in_=HT[slot][:, :, :]),
                        reads=[bHT[slot]], writes=[xb("hts", td.idx)])
                hT = HT[slot]
                for c in range(11):
                    psG, bG_ = ps_next()
                    mm_group(psG[:, 0:TT], bG_, [(wa(G_OFF, k, 2816, c * 128, 128), hT[:, k, 0:TT]) for k in range(8)],
                             [bHT[slot], bWA])
                    psU, bU_ = ps_next()
                    mm_group(psU[:, 0:TT], bU_, [(wa(G_OFF, k, 2816, 1408 + c * 128, 128), hT[:, k, 0:TT]) for k in range(8)],
                             [bHT[slot], bWA])
                    ci = nxt("cs", 2)
                    em.op("act", lambda e, psG=psG, ci=ci: e.activation(out=CS[ci][:, 0:TT], in_=psG[:, 0:TT], func=AF.Silu),
                          reads=[bG_], writes=[bCS[ci]])
                    em.op("dve", lambda e, psU=psU, ci=ci, c=c: e.tensor_tensor(
                        out=BA[:, c * T: c * T + TT], in0=psU[:, 0:TT], in1=CS[ci][:, 0:TT], op=ALU.mult),
                        reads=[bU_, bCS[ci]], writes=[bBA[c]])
                for bi, (r0, nr) in enumerate(td.blocks):
                    for hf in range(2):
                        ps, bps = ps_next()
                        mm_group(ps[0:nr, :], bps,
                                 [(BA[:, c * T + r0: c * T + r0 + nr], wa(D_OFF, c, 1024, hf * 512, 512)) for c in range(11)],
                                 bBA + [bWA])
                        em.op("dve", lambda e, ps=ps, bi=bi, nr=nr, hf=hf: e.tensor_tensor(
                            out=XT[slot][0:nr, bi, hf * 512:(hf + 1) * 512], in0=ps[0:nr, :],
                            in1=XT[slot][0:nr, bi, hf * 512:(hf + 1) * 512], op=ALU.add),
                            reads=[bps, bXT[slot]], writes=[bXT[slot]])
                store_x(dst, dkey, td, slot)

            loads(0)
            for i, td in enumerate(tiles):
                if i + 1 < len(tiles):
                    loads(i + 1)
                ffn_tile(td, i % 2)

        bKB = Buf("KB16")
        bTS = Buf("TS")
        bVA = [Buf("VA0"), Buf("VA1")]
        bKVS = Buf("kvscratch")
        rr["va"] = 0

        def kv_extra(ps, bps, br, row, nr, dstT=None, C=None, L=None, dstV=None, bdst=None):
            if dstT is None:
                dstT, C, L = (XC_T, KS_T, KW_T)[br], (8 if br == 0 else 4), NTOK
                dstV = None if br == 0 else (VS_S if br == 1 else VW_S)
                bdst = bKVS
            ncol = 512 if br == 0 else 256
            em.op("dve", lambda e: e.tensor_copy(out=KB16[0:nr, 0:ncol], in_=ps[0:nr, 0:ncol]), reads=[bps], writes=[bKB])
            nch = ncol // 64
            pt, bpt = pt_next()
            for c in range(nch):
                em.op("pe", lambda e, c=c: e.transpose(pt[0:64, c * 128: c * 128 + nr], KB16[0:nr, c * 64:(c + 1) * 64],
                                                       IDB[0:nr, 0:nr]), reads=[bKB, bC], writes=[bpt], signal=(c == nch - 1))
            em.op("act", lambda e: e.copy(out=TS[:, 0:nch, 0:nr],
                                          in_=pt[0:64, 0:nch * 128].rearrange("p (c t) -> p c t", c=nch)[:, :, 0:nr]),
                  reads=[bpt], writes=[bTS])
            em.dma("sp", lambda e: e.dma_start(
                out=dstT.rearrange("d (c s) -> d c s", c=C)[:, :, row:row + nr], in_=TS[:, 0:nch, 0:nr]),
                reads=[bTS], writes=[bdst], par=True)
            if dstV is not None:
                vi = nxt("va", 2)
                em.op("dve", lambda e: e.tensor_copy(out=VA[vi][0:nr, :, 0:64],
                                                     in_=ps[0:nr, 256:512].rearrange("p (h d) -> p h d", h=4)),
                      reads=[bps], writes=[bVA[vi]])
                em.dma("sp", lambda e: e.dma_start(out=dstV[row:row + nr, :], in_=VA[vi][0:nr, :, :].rearrange("p h d -> p (h d)")),
                       reads=[bVA[vi]], writes=[bdst], par=True)

        bSCT = Buf("samplectx")
        bCSs_box = [None]

        def phase_sample_ctx():
            bG = [Buf("G0"), Buf("G1")]
            bIDX = Buf("IDX")
            rr["g"] = 0
            for vi in range(2):
                em.op("pool", lambda e, vi=vi: e.memset(VA[vi][:, :, :], 1.0), writes=[bVA[vi]])
            em.op("pool", lambda e: e.memset(ZR[:, :], 0.0), writes=[bIDX])
            em.op("dve", lambda e: e.tensor_copy(out=IOPF[:, :], in_=IOT[:, 0:1]), reads=[bC], writes=[bIDX])

            def one_seq(s):
                em.dma("sp", lambda e: e.dma_start(out=PTB[:, :], in_=ptab[s:s + 1, :].partition_broadcast(128)), writes=[bIDX])
                em.op("dve", lambda e: e.tensor_copy(out=PTF[:, :], in_=PTB[:, :]), reads=[bIDX], writes=[bIDX])
                em.op("dve", lambda e: e.tensor_scalar(out=PTF[:, :], in0=PTF[:, :], scalar1=128.0, scalar2=IOPF[:, 0:1],
                                                       op0=ALU.mult, op1=ALU.add), reads=[bIDX], writes=[bIDX])
                em.op("dve", lambda e: e.tensor_copy(out=IDX[:, :], in_=PTF[:, :]), reads=[bIDX], writes=[bIDX])
                for br, pool_ap in ((0, pool_c), (1, pool_s)):
                    for j in range(64):
                        gi = nxt("g", 2)
                        em.dma("pool", lambda e, gi=gi, j=j, pool_ap=pool_ap: e.indirect_dma_start(
                            out=G[gi][:, :], out_offset=None, in_=pool_ap[:, :],
                            in_offset=bass.IndirectOffsetOnAxis(ap=IDX[:, j:j + 1], axis=0)), reads=[bIDX], writes=[bG[gi]])
                        if br == 0:
                            kv_extra(G[gi], bG[gi], 0, j * 128, 128, dstT=XCS[s], C=8, L=LC, dstV=None, bdst=bSCT)
                        else:
                            kv_extra(G[gi], bG[gi], 1, j * 128, 128, dstT=KSS[s], C=4, L=LS, dstV=VSS[s], bdst=bSCT)
                for j in range(4):
                    gi = nxt("g", 2)
                    em.dma("sp", lambda e, gi=gi, j=j: e.dma_start(out=G[gi][:, :], in_=win_in[s, j * 128:(j + 1) * 128, :]),
                           writes=[bG[gi]])
                    kv_extra(G[gi], bG[gi], 2, j * 128, 128, dstT=KWS[s], C=4, L=LW, dstV=VWS[s], bdst=bSCT)
                r0 = SEQ + 8 * s
                for (srcT, C, dT, npad, pos) in ((XC_T, 8, XCS[s], 8, 8192), (KS_T, 4, KSS[s], 120, 8192), (KW_T, 4, KWS[s], 120, 512)):
                    em.dma("sp", lambda e, srcT=srcT, C=C, dT=dT, pos=pos: e.dma_start(
                        out=dT.rearrange("d (c s) -> d c s", c=C)[:, :, pos:pos + 8],
                        in_=srcT.rearrange("d (c s) -> d c s", c=C)[:, :, r0:r0 + 8]), reads=[bKVS], writes=[bSCT], par=True)
                    for c in range(C):
                        em.dma("sp", lambda e, dT=dT, C=C, pos=pos, c=c, npad=npad: e.dma_start(
                            out=dT.rearrange("d (c s) -> d c s", c=C)[:, c, pos + 8:pos + 8 + npad], in_=ZR[0:64, 0:npad]),
                            reads=[bIDX], writes=[bSCT], par=True)
                for (sV, dV, pos) in ((VS_S, VSS[s], 8192), (VW_S, VWS[s], 512)):
                    em.dma("sp", lambda e, sV=sV, dV=dV, pos=pos: e.dma_start(out=dV[pos:pos + 8, :], in_=sV[r0:r0 + 8, :]),
                           reads=[bKVS], writes=[bSCT], par=True)
                    em.dma("sp", lambda e, dV=dV, pos=pos: e.dma_start(out=dV[pos + 8:pos + 128, :], in_=ZR[0:120, 0:260]),
                           reads=[bIDX], writes=[bSCT], par=True)
            for s in range(4):
                one_seq(s)

        def phase_kv(src, skey):
            for vi in range(2):
                em.op("pool", lambda e, vi=vi: e.memset(VA[vi][:, :, :], 1.0), writes=[bVA[vi]])
            load_gain(g_kv[0:1, :])
            load_w(0, w_kv, 8, 1536)
            load_x(src, skey, tiles[0], 0)
            for i, td in enumerate(tiles):
                slot = i % 2
                if i + 1 < len(tiles):
                    load_x(src, skey, tiles[i + 1], (i + 1) % 2)
                norm_T(td, slot, 0)
                for bi, (r0, nr) in enumerate(td.blocks):
                    for br in range(3):
                        ps, bps = ps_next()
                        mm_group(ps[0:nr, :], bps,
                                 [(HT[0][:, k, r0:r0 + nr], wa(0, k, 1536, br * 512, 512)) for k in range(8)],
                                 [bHT[0], bWA])
                        si = nxt("stg", 2)
                        em.op("act" if br != 1 else "dve",
                              (lambda e, ps=ps, si=si, nr=nr: e.copy(out=STG[si][0:nr, :], in_=ps[0:nr, :])) if br != 1 else
                              (lambda e, ps=ps, si=si, nr=nr: e.tensor_copy(out=STG[si][0:nr, :], in_=ps[0:nr, :])),
                              reads=[bps], writes=[bSTG[si]])
                        row = td.row0 + r0
                        kv_extra(ps, bps, br, row, nr)
                        dsts = []
                        if br == 0:
                            dsts.append(cmp_o[row:row + nr, :])
                        elif br == 1:
                            dsts.append(slc_o[row:row + nr, :])
                        else:
                            dsts.append(WINP[row:row + nr, :])
                            if (not td.sample) and row >= SEQ - 512:
                                dsts.append(winp_o[row - (SEQ - 512): row - (SEQ - 512) + nr, :])
                            if td.sample:
                                for s in range(4):
                                    dsts.append((wins_o[s, 504:512, :], s))
                        for dd in dsts:
                            if isinstance(dd, tuple):
                                d_, s = dd
                                em.dma("sp", lambda e, d_=d_, s=s, si=si: e.dma_start(out=d_, in_=STG[si][s * 8:(s + 1) * 8, :]),
                                       reads=[bSTG[si]], writes=[])
                            else:
                                em.dma("sp", lambda e, dd=dd, si=si, nr=nr: e.dma_start(out=dd, in_=STG[si][0:nr, :]),
                                       reads=[bSTG[si]], writes=[])
            for s in range(4):
                em.dma("sp", lambda e, s=s: e.dma_start(out=wins_o[s, 0:504, :], in_=win_in[s, 8:512, :]), writes=[])

        def phase_compress(jobs):
            bW1 = Buf("W1"); bXC = Buf("XCc"); bAK = Buf("ACTK"); bAV = Buf("ACTV"); bBI = Buf("BIAS")
            bCK = Buf("CKTt"); bCV = Buf("CVt"); bCS_ = Buf("cmpscratch")
            for kv in range(2):
                em.dma("pool", lambda e, kv=kv: e.dma_start(out=W1[:, kv, :, :], in_=w1_c[kv].rearrange("j d e -> d j e")),
                       writes=[bW1], par=(kv > 0))
                for two in range(2):
                    em.dma("pool", lambda e, kv=kv, two=two: e.dma_start(
                        out=W1B[two * 64:(two + 1) * 64, kv, :, :],
                        in_=w1_c[kv].rearrange("(c two) d e -> two d c e", two=2)[two]), writes=[bW1], par=True)
                    em.dma("pool", lambda e, kv=kv, two=two: e.dma_start(
                        out=PEV[two * 64:(two + 1) * 64, kv, :],
                        in_=pe_c[kv].rearrange("(c two) d -> two d c", two=2)[two], allow_slow_non_contiguous=True),
                        writes=[bW1], par=True)
                em.dma("pool", lambda e, kv=kv: e.dma_start(out=W2[:, kv, :], in_=w2_c[kv]), writes=[bW1], par=True)
            em.op("pool", lambda e: e.memset(ACTK[:, :, :], 0.0), writes=[bAK])
            em.op("pool", lambda e: e.memset(ACTV[:, :, :], 0.0), writes=[bAV])
            em.op("pool", lambda e: e.memset(CVt[:, :, :], 1.0), writes=[bCV])
            for kv in range(2):
                ps, bps = ps_next()
                mm_group(ps[:, 0:1], bps, [(W1B[:, kv, c, :], PEV[:, kv, c:c + 1]) for c in range(16)], [bW1])
                em.op("act", lambda e, ps=ps, kv=kv: e.copy(out=BIAS[:, kv:kv + 1], in_=ps[:, 0:1]), reads=[bps], writes=[bBI])

            def chunk(q, srcT, nq, dstK, nK, dstV, bsrc, all_full):
                nb = 64 if (all_full or q < nq - 1) else 63
                ncol = 1040 if (all_full or q < nq - 1) else 1024
                em.dma("sp", lambda e: e.dma_start(
                    out=XCc[:, :, 0:ncol], in_=srcT.rearrange("d (c s) -> d c s", c=8)[:, :, 1024 * q: 1024 * q + ncol]),
                    reads=[bsrc], writes=[bXC])
                for kv in range(2):
                    ps, bps = ps_next()
                    for kh in range(4):
                        mm_group(ps[:, kh * 64: kh * 64 + nb], bps,
                                 [(W1[:, kv, j, :], XCc[:, kv * 4 + kh, j: j + 16 * (nb - 1) + 1: 16]) for j in range(32)],
                                 [bW1, bXC])
                    hq = q % 2
                    if kv == 0:
                        em.op("act", lambda e, ps=ps: e.activation(
                            out=ACTK[:, :, 0:nb], in_=ps[:, 0:256].rearrange("p (h n) -> p h n", h=4)[:, :, 0:nb],
                            func=AF.Silu, bias=BIAS[:, 0:1]), reads=[bps, bBI], writes=[bAK])
                        ps2, bps2 = ps_next()
                        mm_group(ps2[0:64, 0:256], bps2, [(W2[:, 0, :], ACTK[:, :, :].rearrange("p h n -> p (h n)"))], [bW1, bAK])
                        em.op("dve", lambda e, ps2=ps2: e.tensor_copy(
                            out=CKTt[:, :, :], in_=ps2[0:64, 0:256].rearrange("p (h n) -> p h n", h=4)), reads=[bps2], writes=[bCK])
                        em.dma("sp", lambda e: e.dma_start(
                            out=dstK.rearrange("d (h n) -> d h n", h=4)[:, :, q * 64:(q + 1) * 64], in_=CKTt[:, :, :]),
                            reads=[bCK], writes=[bCS_], par=True)
                    else:
                        em.op("act", lambda e, ps=ps: e.activation(
                            out=ACTV[:, :, hq * 64: hq * 64 + nb], in_=ps[:, 0:256].rearrange("p (h n) -> p h n", h=4)[:, :, 0:nb],
                            func=AF.Silu, bias=BIAS[:, 1:2]), reads=[bps, bBI], writes=[bAV])
                        ps3, bps3 = ps_next()
                        for kh in range(4):
                            mm_group(ps3[:, kh * 64:(kh + 1) * 64], bps3, [(ACTV[:, kh, :], W2[:, 1, :])], [bW1, bAV])
                        em.op("dve", lambda e, ps3=ps3: e.tensor_copy(
                            out=CVt[hq * 64:(hq + 1) * 64, :, 0:64],
                            in_=ps3[hq * 64:(hq + 1) * 64, 0:256].rearrange("p (h d) -> p h d", h=4)), reads=[bps3], writes=[bCV])
                        em.dma("sp", lambda e: e.dma_start(
                            out=dstV[q * 64:(q + 1) * 64, :], in_=CVt[hq * 64:(hq + 1) * 64, :, :].rearrange("p h d -> p (h d)")),
                            reads=[bCV], writes=[bCS_], par=True)
            for (srcT, nq, dstK, nK, dstV, bsrc, all_full) in jobs:
                for q in range(nq):
                    chunk(q, srcT, nq, dstK, nK, dstV, bsrc, all_full)
            return bCS_

        W_INB, W_OB, KS_OFF = 0, 8 * 1060, 17408
        bVSB = Buf("VSB"); bKWB = Buf("KWB"); bVWB = Buf("VWB"); bCKB = Buf("CKB"); bQA = Buf("QA")
        bPB = [Buf("PB%d" % i) for i in range(3)]
        bYt = [Buf("Yt%d" % i) for i in range(4)]
        bNMX = Buf("NMX"); bG36 = Buf("G36"); bFRC = Buf("FRC"); bTK = Buf("topk"); bR9 = Buf("R9"); bQ1 = Buf("Q1")
        rr["pb"] = 0
        rr["sc"] = 0

        def sc_next():
            i = 4 + nxt("sc", 2)
            return PS[i], bPS[i]

        def nsa_block(td, kh, bi):
            tg = td.idx * 4 + bi
            t0 = tg * 128
            row0 = td.row0
            oc, os_, ow = PS[0], PS[1], PS[2]
            psI = PS[3]
            for pz, bz in ((oc, bPS[0]), (os_, bPS[1]), (ow, bPS[2]), (psI, bPS[3])):
                em.op("dve", lambda e, pz=pz: e.memset(pz[:, 0:195], 0.0), writes=[bz])
            qa = QA[0:70, :, bi * 128:(bi + 1) * 128]

            def exp_to_pb(sc, bsc):
                pi = nxt("pb", 3)
                em.op("act", lambda e: e.activation(out=PB[pi][:, :], in_=sc[:, 0:384], func=AF.Exp),
                      reads=[bsc], writes=[bPB[pi]])
                return pi

            def select(pi, base, cm, tstep):
                em.op("pool", lambda e: e.affine_select(
                    out=PB[pi][:, :].rearrange("p (h t) -> p h t", h=3), in_=PB[pi][:, :].rearrange("p (h t) -> p h t", h=3),
                    pattern=[[0, 3], [tstep, 128]], compare_op=ALU.is_ge, fill=0.0, base=base, channel_multiplier=cm),
                    reads=[bPB[pi]], writes=[bPB[pi]])

            def pv(pi, acc, bacc, v_ap, vb):
                for hi in range(3):
                    em.op("pe", lambda e, hi=hi: e.matmul(acc[:, hi * 65:(hi + 1) * 65], PB[pi][:, hi * 128:(hi + 1) * 128], v_ap,
                                                          start=False, stop=True, skip_group_check=True),
                          reads=[bPB[pi], vb], writes=[bacc], signal=(hi == 2))

            def pipeline(items, stage_a, stage_b):
                prev = None
                for it in items:
                    pi = stage_a(it)
                    if prev is not None:
                        stage_b(*prev)
                    prev = (it, pi)
                if prev is not None:
                    stage_b(*prev)

            def cmp_a(nt):
                sc, bsc = sc_next()
                em.op("pe", lambda e: e.matmul(sc[:, 0:384], CKB[0:70, kh, nt * 128:(nt + 1) * 128], qa,
                                               start=True, stop=True), reads=[bCKB, bQA], writes=[bsc])
                pi = exp_to_pb(sc, bsc)
                select(pi, t0 - 16 * nt * 128 - 31, -16, 1)
                return pi

            def cmp_b(nt, pi):
                pv(pi, oc, bPS[0], CVB[:, nt, kh * 65:(kh + 1) * 65], bCKB)
                for hi in range(3):
                    em.op("pe", lambda e, hi=hi: e.matmul(
                        psI[:, hi * 64 + nt * 32: hi * 64 + nt * 32 + 32], PB[pi][:, hi * 128:(hi + 1) * 128], GM[:, :],
                        start=False, stop=True, skip_group_check=True), reads=[bPB[pi], bC], writes=[bPS[3]], signal=(hi == 2))
            pipeline([nt for nt in range(2) if 16 * (nt * 128) + 31 <= t0 + 127], cmp_a, cmp_b)
            def den_recip(acc, bacc, br):
                em.op("dve", lambda e: e.tensor_scalar(
                    out=R9[:, br:9:3], in0=acc[:, 0:195].rearrange("p (h c) -> p h c", c=65)[:, :, 64],
                    scalar1=1e-30, scalar2=None, op0=ALU.max), reads=[bacc], writes=[bR9])
            den_recip(oc, bPS[0], 0)
            em.op("dve", lambda e: e.reciprocal(out=RC[:, 0:3], in_=R9[:, 0:9:3]), reads=[bR9], writes=[bTK])
            em.op("dve", lambda e: e.tensor_scalar(out=IMP[:, :], in0=psI[:, 0:64], scalar1=RC[:, 0:1], scalar2=None, op0=ALU.mult),
                  reads=[bPS[3], bTK], writes=[bTK])
            for hi in (1, 2):
                em.op("dve", lambda e, hi=hi: e.scalar_tensor_tensor(
                    out=IMP[:, :], in0=psI[:, hi * 64:(hi + 1) * 64], scalar=RC[:, hi:hi + 1], in1=IMP[:, :],
                    op0=ALU.mult, op1=ALU.add), reads=[bPS[3], bTK], writes=[bTK])
            em.op("dve", lambda e: e.tensor_tensor(out=SCR[:, :], in0=IMP[:, :], in1=FRC[:, bi, :], op=ALU.max),
                  reads=[bTK, bFRC], writes=[bTK])
            em.op("dve", lambda e: e.max(out=M8[:, 0:8], in_=SCR[:, :]), reads=[bTK], writes=[bTK])
            em.op("dve", lambda e: e.match_replace(out=SC2[:, :], in_to_replace=M8[:, 0:8], in_values=SCR[:, :], imm_value=-1e30),
                  reads=[bTK], writes=[bTK])
            em.op("dve", lambda e: e.max(out=M8[:, 8:16], in_=SC2[:, :]), reads=[bTK], writes=[bTK])
            em.op("dve", lambda e: e.tensor_reduce(out=M8[:, 16:17], in_=M8[:, 8:16], axis=AX.X, op=ALU.min), reads=[bTK], writes=[bTK])
            em.op("dve", lambda e: e.tensor_scalar(out=NMt[:, :], in0=SCR[:, :], scalar1=M8[:, 16:17], scalar2=None, op0=ALU.is_ge),
                  reads=[bTK], writes=[bTK])
            em.op("dve", lambda e: e.tensor_scalar(out=NMt[:, :], in0=NMt[:, :], scalar1=30000.0, scalar2=-30000.0,
                                                   op0=ALU.mult, op1=ALU.add), reads=[bTK], writes=[bTK])
            nblk = 2 * (tg + 1)
            em.op("dve", lambda e: e.tensor_copy(
                out=NMX[:, 0:nblk * 64].rearrange("p (j r) -> p j r", r=64),
                in_=NMt[:, 0:nblk].unsqueeze(2).to_broadcast([128, nblk, 64])), reads=[bTK], writes=[bNMX])
            wlo = row0 - 512

            def sw_a(it):
                br, kb = it
                sc, bsc = sc_next()
                if br == "w":
                    c0 = kb * 128 - wlo
                    em.op("pe", lambda e: e.matmul(sc[:, 0:384], KWB[0:70, kh, c0:c0 + 128], qa, start=True, stop=True),
                          reads=[bKWB, bQA], writes=[bsc])
                    pi = exp_to_pb(sc, bsc)
                    if kb == tg:
                        select(pi, 0, -1, 1)
                    if kb == tg - 4:
                        select(pi, 0, 1, -1)
                else:
                    em.op("pe", lambda e: e.matmul(
                        sc[:, 0:384], WA[0:70, KS_OFF + kh * 4096 + kb * 128: KS_OFF + kh * 4096 + (kb + 1) * 128], qa,
                        start=True, stop=False), reads=[bWA, bQA], writes=[bsc], signal=False)
                    em.op("pe", lambda e: e.matmul(
                        sc[:, 0:384], NMX[:, kb * 128:(kb + 1) * 128], IDB3[:, :],
                        start=False, stop=True, skip_group_check=True), reads=[bNMX, bC], writes=[bsc], signal=True)
                    pi = exp_to_pb(sc, bsc)
                    if kb == tg:
                        select(pi, 0, -1, 1)
                return pi

            def sw_b(it, pi):
                br, kb = it
                if br == "w":
                    c0 = kb * 128 - wlo
                    pv(pi, ow, bPS[2], VWB[:, c0 // 128, kh * 65:(kh + 1) * 65], bVWB)
                else:
                    pv(pi, os_, bPS[1], VSB[:, kb, kh * 65:(kh + 1) * 65], bVSB)
            pipeline([("w", kb) for kb in range(max(0, tg - 4), tg + 1)] + [("s", kb) for kb in range(tg + 1)], sw_a, sw_b)
            den_recip(os_, bPS[1], 1)
            den_recip(ow, bPS[2], 2)
            em.op("dve", lambda e: e.reciprocal(out=R9[:, :], in_=R9[:, :]), reads=[bR9], writes=[bR9])
            em.op("dve", lambda e: e.tensor_tensor(out=R9[:, :], in0=R9[:, :], in1=G36[:, bi, kh * 9:(kh + 1) * 9], op=ALU.mult),
                  reads=[bR9, bG36], writes=[bR9])
            for hi in range(3):
                h = 3 * kh + hi
                em.op("dve", lambda e, hi=hi: e.tensor_scalar(
                    out=TY[:, :], in0=oc[:, hi * 65: hi * 65 + 64], scalar1=R9[:, 3 * hi:3 * hi + 1], scalar2=None, op0=ALU.mult),
                    reads=[bPS[0], bR9], writes=[bTK])
                em.op("dve", lambda e, hi=hi: e.scalar_tensor_tensor(
                    out=TY[:, :], in0=os_[:, hi * 65: hi * 65 + 64], scalar=R9[:, 3 * hi + 1:3 * hi + 2], in1=TY[:, :],
                    op0=ALU.mult, op1=ALU.add), reads=[bPS[1], bR9, bTK], writes=[bTK])
                em.op("dve", lambda e, hi=hi, h=h: e.scalar_tensor_tensor(
                    out=Yt[:, bi, h * 64:(h + 1) * 64], in0=ow[:, hi * 65: hi * 65 + 64], scalar=R9[:, 3 * hi + 2:3 * hi + 3],
                    in1=TY[:, :], op0=ALU.mult, op1=ALU.add), reads=[bPS[2], bR9, bTK], writes=[bYt[bi]])

        def mixer_b_tile(td, l, xslot):
            TT = td.T
            norm_T(td, xslot, 0)
            hT = HT[0]
            qaps = []
            for hp in range(2):
                ps, bps = ps_next()
                mm_group(ps[:, 0:TT], bps, [(wa(W_INB, k, 1060, 804 + hp * 128, 128), hT[:, k, 0:TT]) for k in range(8)],
                         [bHT[0], bWA])
                qap = (BA[:, 10 * T: 10 * T + TT] if hp == 0 else Q1[:, 0:TT])
                qb = bBA[10] if hp == 0 else bQ1
                em.op("act", lambda e, ps=ps, qap=qap: e.copy(out=qap, in_=ps[:, 0:TT]), reads=[bps], writes=[qb])
                qaps.append((qap, qb))
            mem_attention(td, l, [qaps[0][0], qaps[1][0]], [qaps[0][1], qaps[1][1]])
            if td.sample:
                for j in range(6):
                    em.op("pool", lambda e, j=j: e.memset(BA[:, j * T: j * T + TT], 0.0), writes=[bBA[j]])
            else:
                row0 = td.row0
                for bi, (r0, nr) in enumerate(td.blocks):
                    ps, bps = ps_next()
                    mm_group(ps[0:nr, 0:36], bps, [(hT[:, k, r0:r0 + nr], wa(W_INB, k, 1060, 768, 36)) for k in range(8)],
                             [bHT[0], bWA])
                    em.op("act", lambda e, ps=ps, bi=bi, nr=nr: e.activation(out=G36[0:nr, bi, :], in_=ps[0:nr, 0:36], func=AF.Sigmoid),
                          reads=[bps], writes=[bG36])
                em.dma("sp", lambda e: e.dma_start(out=FRC[:, :, :], in_=frc_t[row0:row0 + T, :].rearrange("(b p) j -> p b j", p=128)),
                       writes=[bFRC])
                lo = max(0, row0 - 512)
                off = lo - (row0 - 512)
                nkeys = row0 + T - lo
                em.dma("sp", lambda e: e.dma_start(
                    out=KWB[0:64, :, off:off + nkeys], in_=KW_T.rearrange("d (h s) -> d h s", h=4)[:, :, lo:lo + nkeys]),
                    reads=[bKVS], writes=[bKWB])
                for kh in range(4):
                    em.dma("pool", lambda e, kh=kh: e.dma_start(out=KWB[64:70, kh, off:off + nkeys], in_=kaug_t[:, lo:lo + nkeys]),
                           writes=[bKWB], par=True)
                em.dma("sp", lambda e: e.dma_start(
                    out=VWB[:, off // 128: off // 128 + nkeys // 128, :],
                    in_=VW_S[lo:lo + nkeys, :].rearrange("(kb p) c -> p kb c", p=128)), reads=[bKVS], writes=[bVWB])
                for kh in range(4):
                    for hi in range(3):
                        h = 3 * kh + hi
                        ps, bps = sc_next()
                        mm_group(ps[0:64, 0:TT], bps, [(wa(W_INB, k, 1060, h * 64, 64), hT[:, k, 0:TT]) for k in range(8)],
                                 [bHT[0], bWA])
                        em.op("act", lambda e, ps=ps, hi=hi: e.mul(out=QA[0:64, hi, 0:TT], in_=ps[0:64, 0:TT], mul=0.125),
                              reads=[bps], writes=[bQA])
                    em.dma("pool", lambda e, kh=kh: e.dma_start(
                        out=QA[64:70, :, 0:TT], in_=qaug_t[3 * kh:3 * kh + 3, :, row0:row0 + TT].rearrange("h r t -> r h t")),
                        writes=[bQA], par=True)
                    for bi in range(4):
                        nsa_block(td, kh, bi)
                for kk in range(3):
                    pt, bpt = pt_next()
                    for k2 in range(2):
                        j = kk * 2 + k2
                        for bi in range(4):
                            em.op("pe", lambda e, j=j, k2=k2, bi=bi, pt=pt: e.transpose(
                                pt[:, k2 * 512 + bi * 128: k2 * 512 + (bi + 1) * 128], Yt[:, bi, j * 128:(j + 1) * 128], IDB[:, :]),
                                reads=[bYt[bi], bC], writes=[bpt], signal=(k2 == 1 and bi == 3))
                    em.op("act", lambda e, kk=kk, pt=pt: e.copy(
                        out=BA[:, 2 * kk * T:(2 * kk + 2) * T].rearrange("p (a t) -> p a t", a=2),
                        in_=pt[:, :].rearrange("p (a t) -> p a t", a=2)), reads=[bpt], writes=[bBA[2 * kk], bBA[2 * kk + 1]])
            for bi, (r0, nr) in enumerate(td.blocks):
                for half in range(2):
                    ps, bps = ps_next()
                    mm_group(ps[0:nr, :], bps,
                             [(BA[:, k * T + r0: k * T + r0 + nr], wa(W_OB, k, 1024, half * 512, 512)) for k in range(8)],
                             [bBA[k] for k in range(8)] + [bWA])
                    em.op("dve", lambda e, ps=ps, bi=bi, nr=nr, half=half: e.tensor_tensor(
                        out=XT[xslot][0:nr, bi, half * 512:(half + 1) * 512], in0=ps[0:nr, :],
                        in1=XT[xslot][0:nr, bi, half * 512:(half + 1) * 512], op=ALU.add),
                        reads=[bps, bXT[xslot]], writes=[bXT[xslot]])

        def phase_mixer_b(l, src, skey, dst, dkey, bCS_):
            load_gain(g_mix[l:l + 1, :])
            load_w(W_INB, w_in_b[l - 2], 8, 1060)
            load_w(W_OB, w_o[l], 8, 1024, first=False)
            em.dma("sp", lambda e: e.dma_start(
                out=WA[0:64, KS_OFF:KS_OFF + 16384].rearrange("p (h s) -> p h s", h=4),
                in_=KS_T.rearrange("d (h s) -> d h s", h=4)[:, :, 0:SEQ]), reads=[bKVS], writes=[bWA], par=True)
            for kh in range(4):
                em.dma("pool", lambda e, kh=kh: e.dma_start(
                    out=WA[64:70, KS_OFF + kh * 4096: KS_OFF + (kh + 1) * 4096], in_=kaug_t[:, :]), writes=[bWA], par=True)
            em.dma("sp", lambda e: e.dma_start(out=VSB[:, :, :], in_=VS_S[0:SEQ, :].rearrange("(kb p) c -> p kb c", p=128)),
                   reads=[bKVS], writes=[bVSB])
            em.dma("sp", lambda e: e.dma_start(out=CKB[0:64, :, :], in_=CKT_S.rearrange("d (h n) -> d h n", h=4)),
                   reads=[bCS_], writes=[bCKB])
            for kh in range(4):
                em.dma("pool", lambda e, kh=kh: e.dma_start(out=CKB[64:70, kh, :], in_=kaug_c[:, :]), writes=[bCKB], par=True)
            em.dma("sp", lambda e: e.dma_start(out=CVB[:, :, :], in_=CV_S.rearrange("(nt p) c -> p nt c", p=128)),
                   reads=[bCS_], writes=[bCKB], par=True)
            em.dma("pool", lambda e: e.dma_start(out=GM[:, :], in_=gm_t[:, :]), writes=[bC])
            load_sample_mem(l)
            for i, td in enumerate(tiles):
                if td.sample and SAMPLE_NSA:
                    continue
                load_x(src, skey, td, 0)
                mixer_b_tile(td, l, 0)
                store_x(dst, dkey, td, 0)

        def phase_mixer_b_sample(l, src, skey, dst, dkey):
            bCSs = bCSs_box[0]
            td = tiles[NT]
            TT = NS
            bQAs = Buf("QAs"); bG36s = Buf("G36s"); bFRs = Buf("FRCs"); bCKs = Buf("CKs"); bKWs = Buf("KWs")
            bKSc = [Buf("KSc0"), Buf("KSc1")]; bVSc = [Buf("VSc0"), Buf("VSc1")]
            bPBs = [Buf("PBs%d" % i) for i in range(3)]; bNX = [Buf("NX0"), Buf("NX1")]
            bTKs = Buf("tks"); bO = Buf("Osb"); bYs = Buf("Yts"); bQ1s = Buf("Q1S"); bR9s = Buf("R9s")
            st_ = {"pb": 0, "nx": 0, "kc": 0}

            def rot(k, n):
                i = st_[k]
                st_[k] = (i + 1) % n
                return i
            load_x(src, skey, td, 0)
            norm_T(td, 0, 0)
            hT = HT[0]
            em.dma("pool", lambda e: e.dma_start(out=GMs[:, :], in_=gm_t[:, :]), writes=[bTKs])
            em.dma("sp", lambda e: e.dma_start(out=FRCs[:, :], in_=frc_s[:, :]), writes=[bFRs])
            qaps = []
            for hp in range(2):
                ps, bps = ps_next()
                mm_group(ps[:, 0:TT], bps, [(wa(W_INB, k, 1060, 804 + hp * 128, 128), hT[:, k, 0:TT]) for k in range(8)],
                         [bHT[0], bWA])
                qap = (BA[:, 10 * T: 10 * T + TT] if hp == 0 else Q1S[:, 0:TT])
                qb = bBA[10] if hp == 0 else bQ1s
                em.op("act", lambda e, ps=ps, qap=qap: e.copy(out=qap, in_=ps[:, 0:TT]), reads=[bps], writes=[qb])
                qaps.append((qap, qb))
            mem_attention(td, l, [qaps[0][0], qaps[1][0]], [qaps[0][1], qaps[1][1]])
            for h in range(12):
                ps, bps = sc_next()
                mm_group(ps[0:64, 0:TT], bps, [(wa(W_INB, k, 1060, h * 64, 64), hT[:, k, 0:TT]) for k in range(8)], [bHT[0], bWA])
                em.op("act", lambda e, ps=ps, h=h: e.mul(out=QAs[0:64, h, :], in_=ps[0:64, 0:TT], mul=0.125), reads=[bps], writes=[bQAs])
            em.dma("pool", lambda e: e.dma_start(out=QAs[64:70, :, :], in_=qaug_s.rearrange("r (h t) -> r h t", h=12)),
                   writes=[bQAs], par=True)
            for s_ in range(4):
                ps, bps = ps_next()
                mm_group(ps[0:8, 0:36], bps, [(hT[:, k, s_ * 8:(s_ + 1) * 8], wa(W_INB, k, 1060, 768, 36)) for k in range(8)],
                         [bHT[0], bWA])
                em.op("act", lambda e, ps=ps, s_=s_: e.activation(out=G36s[0:8, s_, :], in_=ps[0:8, 0:36], func=AF.Sigmoid),
                      reads=[bps], writes=[bG36s])

            def pipeline(items, stage_a, stage_b):
                prev = None
                for it in items:
                    pi = stage_a(it)
                    if prev is not None:
                        stage_b(*prev)
                    prev = (it, pi)
                if prev is not None:
                    stage_b(*prev)

            def exp_pb(sc, bsc):
                pi = rot("pb", 3)
                em.op("act", lambda e: e.activation(out=PBs[pi][:, :], in_=sc[:, 0:24], func=AF.Exp), reads=[bsc], writes=[bPBs[pi]])
                return pi

            def select(pi, base, cm, tstep):
                em.op("pool", lambda e: e.affine_select(
                    out=PBs[pi][:, :].rearrange("p (h t) -> p h t", h=3), in_=PBs[pi][:, :].rearrange("p (h t) -> p h t", h=3),
                    pattern=[[0, 3], [tstep, 8]], compare_op=ALU.is_ge, fill=0.0, base=base, channel_multiplier=cm),
                    reads=[bPBs[pi]], writes=[bPBs[pi]])

            def pv(pi, acc, bacc, col0, v_ap, vb):
                for hi in range(3):
                    em.op("pe", lambda e, hi=hi: e.matmul(acc[0:8, col0 + hi * 65: col0 + (hi + 1) * 65], PBs[pi][:, hi * 8:(hi + 1) * 8],
                                                          v_ap, start=False, stop=True, skip_group_check=True),
                          reads=[bPBs[pi], vb], writes=[bacc], signal=(hi == 2))

            def one_seq(s_):
                def qa(kh):
                    return QAs[0:70, 3 * kh:3 * kh + 3, s_ * 8:(s_ + 1) * 8]
                em.dma("sp", lambda e: e.dma_start(out=CKs[0:64, :, :], in_=CKS[s_].rearrange("d (h n) -> d h n", h=4)),
                       reads=[bCSs], writes=[bCKs])
                for kh in range(4):
                    em.dma("pool", lambda e, kh=kh: e.dma_start(out=CKs[64:70, kh, :], in_=kaug_cs[:, :]), writes=[bCKs], par=True)
                em.dma("sp", lambda e: e.dma_start(out=CVs[:, :, :], in_=CVS[s_].rearrange("(nt p) c -> p nt c", p=128)),
                       reads=[bCSs], writes=[bCKs], par=True)
                em.dma("sp", lambda e: e.dma_start(out=KWs[0:64, :, :], in_=KWS[s_].rearrange("d (h n) -> d h n", h=4)),
                       reads=[bSCT], writes=[bKWs])
                for kh in range(4):
                    em.dma("pool", lambda e, kh=kh: e.dma_start(out=KWs[64:70, kh, :], in_=kaug_ws[:, :]), writes=[bKWs], par=True)
                em.dma("sp", lambda e: e.dma_start(out=VWs[:, :, :], in_=VWS[s_].rearrange("(nt p) c -> p nt c", p=128)),
                       reads=[bSCT], writes=[bKWs], par=True)
                for kh in range(4):
                    oc, psI = PS[0], PS[3]
                    em.op("dve", lambda e: e.memset(oc[0:8, 0:195], 0.0), writes=[bPS[0]])
                    em.op("dve", lambda e: e.memset(psI[0:8, 0:384], 0.0), writes=[bPS[3]])
                    def c_a(nt, kh=kh):
                        sc, bsc = sc_next()
                        em.op("pe", lambda e: e.matmul(sc[:, 0:24], CKs[0:70, kh, nt * 128:(nt + 1) * 128], qa(kh),
                                                       start=True, stop=True), reads=[bCKs, bQAs], writes=[bsc])
                        pi = exp_pb(sc, bsc)
                        if nt == 3:
                            select(pi, 510 - 384, -1, 0)
                        return pi

                    def c_b(nt, pi, kh=kh, oc=oc, psI=psI):
                        pv(pi, oc, bPS[0], 0, CVs[:, nt, kh * 65:(kh + 1) * 65], bCKs)
                        for hi in range(3):
                            em.op("pe", lambda e, hi=hi: e.matmul(
                                psI[0:8, hi * 128 + nt * 32: hi * 128 + nt * 32 + 32], PBs[pi][:, hi * 8:(hi + 1) * 8], GMs[:, :],
                                start=False, stop=True, skip_group_check=True), reads=[bPBs[pi], bTKs], writes=[bPS[3]], signal=(hi == 2))
                    pipeline(list(range(4)), c_a, c_b)
                    em.op("act", lambda e, kh=kh: e.copy(out=OCs[0:8, kh, :], in_=oc[0:8, 0:195]), reads=[bPS[0]], writes=[bO])
                    em.op("dve", lambda e, kh=kh: e.tensor_scalar(
                        out=RCs[0:8, 0:3], in0=OCs[0:8, kh, :].rearrange("p (h c) -> p h c", c=65)[:, :, 64], scalar1=1e-30, scalar2=None,
                        op0=ALU.max), reads=[bO], writes=[bTKs])
                    em.op("dve", lambda e: e.reciprocal(out=RCs[0:8, 0:3], in_=RCs[0:8, 0:3]), reads=[bTKs], writes=[bTKs])
                    em.op("dve", lambda e: e.tensor_scalar(out=IMPs[0:8, :], in0=psI[0:8, 0:128], scalar1=RCs[0:8, 0:1], scalar2=None,
                                                           op0=ALU.mult), reads=[bPS[3], bTKs], writes=[bTKs])
                    for hi in (1, 2):
                        em.op("dve", lambda e, hi=hi: e.scalar_tensor_tensor(
                            out=IMPs[0:8, :], in0=psI[0:8, hi * 128:(hi + 1) * 128], scalar=RCs[0:8, hi:hi + 1], in1=IMPs[0:8, :],
                            op0=ALU.mult, op1=ALU.add), reads=[bPS[3], bTKs], writes=[bTKs])
                    em.op("dve", lambda e: e.tensor_copy(out=SCRs[0:8, :], in_=FRCs[0:8, :]), reads=[bFRs], writes=[bTKs])
                    em.op("dve", lambda e: e.tensor_tensor(out=SCRs[0:8, 0:128], in0=IMPs[0:8, :], in1=FRCs[0:8, 0:128], op=ALU.max),
                          reads=[bTKs, bFRs], writes=[bTKs])
                    em.op("dve", lambda e: e.max(out=M8s[0:8, 0:8], in_=SCRs[0:8, :]), reads=[bTKs], writes=[bTKs])
                    em.op("dve", lambda e: e.match_replace(out=SC2s[0:8, :], in_to_replace=M8s[0:8, 0:8], in_values=SCRs[0:8, :],
                                                           imm_value=-1e30), reads=[bTKs], writes=[bTKs])
                    em.op("dve", lambda e: e.max(out=M8s[0:8, 8:16], in_=SC2s[0:8, :]), reads=[bTKs], writes=[bTKs])
                    em.op("dve", lambda e: e.tensor_reduce(out=M8s[0:8, 16:17], in_=M8s[0:8, 8:16], axis=AX.X, op=ALU.min),
                          reads=[bTKs], writes=[bTKs])
                    em.op("dve", lambda e, kh=kh: e.tensor_scalar(out=NMts[0:8, kh, :], in0=SCRs[0:8, :], scalar1=M8s[0:8, 16:17],
                                                                  scalar2=None, op0=ALU.is_ge), reads=[bTKs], writes=[bTKs])
                    em.op("dve", lambda e, kh=kh: e.tensor_scalar(out=NMts[0:8, kh, :], in0=NMts[0:8, kh, :], scalar1=30000.0,
                                                                  scalar2=-30000.0, op0=ALU.mult, op1=ALU.add), reads=[bTKs], writes=[bTKs])
                em.op("dve", lambda e: e.memset(PS[0][0:8, 0:390], 0.0), writes=[bPS[0]])
                em.op("dve", lambda e: e.memset(PS[1][0:8, 0:390], 0.0), writes=[bPS[1]])
                def s_loads(ch):
                    nk = 512 if ch < 16 else 128
                    ci = ch % 2
                    em.dma("sp", lambda e: e.dma_start(
                        out=KSc[ci][0:64, :, 0:nk], in_=KSS[s_].rearrange("d (h n) -> d h n", h=4)[:, :, ch * 512: ch * 512 + nk]),
                        reads=[bSCT], writes=[bKSc[ci]])
                    for kh in range(4):
                        em.dma("pool", lambda e, kh=kh: e.dma_start(
                            out=KSc[ci][64:70, kh, 0:nk], in_=kaug_s[:, ch * 512: ch * 512 + nk]), writes=[bKSc[ci]], par=True)
                    em.dma("sp", lambda e: e.dma_start(
                        out=VSc[ci][:, 0:nk // 128, :], in_=VSS[s_][ch * 512: ch * 512 + nk, :].rearrange("(kb p) c -> p kb c", p=128)),
                        reads=[bSCT], writes=[bVSc[ci]])

                def s_a(it):
                    kb, kbl, kh, ci = it
                    xi = rot("nx", 2)
                    em.op("dve", lambda e: e.tensor_copy(
                        out=NMXs[xi][0:8, :].rearrange("p (j r) -> p j r", r=64),
                        in_=NMts[0:8, kh, 2 * kb:2 * kb + 2].unsqueeze(2).to_broadcast([8, 2, 64])),
                        reads=[bTKs], writes=[bNX[xi]])
                    sc, bsc = sc_next()
                    em.op("pe", lambda e: e.matmul(
                        sc[:, 0:24], KSc[ci][0:70, kh, kbl * 128:(kbl + 1) * 128], qa(kh), start=True, stop=False),
                        reads=[bKSc[ci], bQAs], writes=[bsc], signal=False)
                    for hi in range(3):
                        em.op("pe", lambda e, hi=hi: e.matmul(
                            sc[:, hi * 8:(hi + 1) * 8], NMXs[xi][0:8, :], IDB[0:8, 0:8], start=False, stop=(hi == 2),
                            skip_group_check=True), reads=[bNX[xi], bC], writes=[bsc], signal=(hi == 2))
                    pi = exp_pb(sc, bsc)
                    if kb == 64:
                        select(pi, 0, -1, 1)
                    return pi

                def s_b(it, pi):
                    kb, kbl, kh, ci = it
                    acc, bacc = (PS[0], bPS[0]) if kh < 2 else (PS[1], bPS[1])
                    pv(pi, acc, bacc, (kh % 2) * 195, VSc[ci][:, kbl, kh * 65:(kh + 1) * 65], bVSc[ci])
                prev = None
                s_loads(0)
                for ch in range(17):
                    if prev is not None:
                        s_b(*prev)
                        prev = None
                    if ch + 1 < 17:
                        s_loads(ch + 1)
                    nk = 512 if ch < 16 else 128
                    for kbl in range(nk // 128):
                        for kh in range(4):
                            it = (ch * 4 + kbl, kbl, kh, ch % 2)
                            pi = s_a(it)
                            if prev is not None:
                                s_b(*prev)
                            prev = (it, pi)
                if prev is not None:
                    s_b(*prev)
                em.op("act", lambda e: e.copy(out=OSs[0:8, 0:2, :], in_=PS[0][0:8, 0:390].rearrange("p (k c) -> p k c", k=2)),
                      reads=[bPS[0]], writes=[bO])
                em.op("act", lambda e: e.copy(out=OSs[0:8, 2:4, :], in_=PS[1][0:8, 0:390].rearrange("p (k c) -> p k c", k=2)),
                      reads=[bPS[1]], writes=[bO])
                em.op("dve", lambda e: e.memset(PS[2][0:8, 0:390], 0.0), writes=[bPS[2]])
                em.op("dve", lambda e: e.memset(PS[3][0:8, 0:390], 0.0), writes=[bPS[3]])
                def w_a(it):
                    kh, kb = it
                    sc, bsc = sc_next()
                    em.op("pe", lambda e: e.matmul(sc[:, 0:24], KWs[0:70, kh, kb * 128:(kb + 1) * 128], qa(kh),
                                                   start=True, stop=True), reads=[bKWs, bQAs], writes=[bsc])
                    pi = exp_pb(sc, bsc)
                    if kb == 0:
                        select(pi, 0, 1, -1)
                    if kb == 4:
                        select(pi, 0, -1, 1)
                    return pi

                def w_b(it, pi):
                    kh, kb = it
                    acc, bacc = (PS[2], bPS[2]) if kh < 2 else (PS[3], bPS[3])
                    pv(pi, acc, bacc, (kh % 2) * 195, VWs[:, kb, kh * 65:(kh + 1) * 65], bKWs)
                pipeline([(kh, kb) for kh in range(4) for kb in range(5)], w_a, w_b)
                em.op("act", lambda e: e.copy(out=OWs[0:8, 0:2, :], in_=PS[2][0:8, 0:390].rearrange("p (k c) -> p k c", k=2)),
                      reads=[bPS[2]], writes=[bO])
                em.op("act", lambda e: e.copy(out=OWs[0:8, 2:4, :], in_=PS[3][0:8, 0:390].rearrange("p (k c) -> p k c", k=2)),
                      reads=[bPS[3]], writes=[bO])
                for kh in range(4):
                    for br, Ob in enumerate((OCs, OSs, OWs)):
                        em.op("dve", lambda e, br=br, Ob=Ob, kh=kh: e.tensor_scalar(
                            out=R9s[0:8, br:9:3], in0=Ob[0:8, kh, :].rearrange("p (h c) -> p h c", c=65)[:, :, 64], scalar1=1e-30,
                            scalar2=None, op0=ALU.max), reads=[bO], writes=[bR9s])
                    em.op("dve", lambda e: e.reciprocal(out=R9s[0:8, :], in_=R9s[0:8, :]), reads=[bR9s], writes=[bR9s])
                    em.op("dve", lambda e, kh=kh: e.tensor_tensor(out=R9s[0:8, :], in0=R9s[0:8, :], in1=G36s[0:8, s_, kh * 9:(kh + 1) * 9],
                                                                  op=ALU.mult), reads=[bR9s, bG36s], writes=[bR9s])
                    for hi in range(3):
                        h = 3 * kh + hi
                        em.op("dve", lambda e, hi=hi, kh=kh: e.tensor_scalar(
                            out=TYs[0:8, :], in0=OCs[0:8, kh, hi * 65: hi * 65 + 64], scalar1=R9s[0:8, 3 * hi:3 * hi + 1], scalar2=None,
                            op0=ALU.mult), reads=[bO, bR9s], writes=[bTKs])
                        em.op("dve", lambda e, hi=hi, kh=kh: e.scalar_tensor_tensor(
                            out=TYs[0:8, :], in0=OSs[0:8, kh, hi * 65: hi * 65 + 64], scalar=R9s[0:8, 3 * hi + 1:3 * hi + 2],
                            in1=TYs[0:8, :], op0=ALU.mult, op1=ALU.add), reads=[bO, bR9s, bTKs], writes=[bTKs])
                        em.op("dve", lambda e, hi=hi, kh=kh, h=h: e.scalar_tensor_tensor(
                            out=Yts[0:8, s_, h * 64:(h + 1) * 64], in0=OWs[0:8, kh, hi * 65: hi * 65 + 64],
                            scalar=R9s[0:8, 3 * hi + 2:3 * hi + 3], in1=TYs[0:8, :], op0=ALU.mult, op1=ALU.add),
                            reads=[bO, bR9s, bTKs], writes=[bYs])
                pt, bpt = pt_next()
                for j in range(6):
                    em.op("pe", lambda e, j=j: e.transpose(pt[:, j * 8:(j + 1) * 8], Yts[0:8, s_, j * 128:(j + 1) * 128], IDB[0:8, 0:8]),
                          reads=[bYs, bC], writes=[bpt], signal=(j == 5))
                for j in range(6):
                    em.op("act", lambda e, j=j: e.copy(out=BA[:, j * T + s_ * 8: j * T + (s_ + 1) * 8], in_=pt[:, j * 8:(j + 1) * 8]),
                          reads=[bpt], writes=[bBA[j]])
            for s_ in range(4):
                one_seq(s_)
            for bi, (r0, nr) in enumerate(td.blocks):
                for half in range(2):
                    ps, bps = ps_next()
                    mm_group(ps[0:nr, :], bps,
                             [(BA[:, k * T + r0: k * T + r0 + nr], wa(W_OB, k, 1024, half * 512, 512)) for k in range(8)],
                             [bBA[k] for k in range(8)] + [bWA])
                    em.op("dve", lambda e, ps=ps, bi=bi, nr=nr, half=half: e.tensor_tensor(
                        out=XT[0][0:nr, bi, half * 512:(half + 1) * 512], in0=ps[0:nr, :],
                        in1=XT[0][0:nr, bi, half * 512:(half + 1) * 512], op=ALU.add),
                        reads=[bps, bXT[0]], writes=[bXT[0]])
            store_x(dst, dkey, td, 0)

        def phase_final(src, skey):
            load_gain(g_final[0:1, :])
            load_x(src, skey, tiles[0], 0)
            for i, td in enumerate(tiles):
                slot = i % 2
                if i + 1 < len(tiles):
                    load_x(src, skey, tiles[i + 1], (i + 1) % 2)
                xt = XT[slot]
                for bi, (r0, nr) in enumerate(td.blocks):
                    si = nxt("ss", 8)
                    em.op("pool", lambda e, si=si: e.memset(SS[:, si:si + 1], 0.0), writes=[bSS[si]])
                    em.op("act", lambda e, bi=bi, nr=nr, si=si, xt=xt: e.activation(
                        out=JK[0:nr, :], in_=xt[0:nr, bi, :], func=AF.Square, accum_out=SS[0:nr, si:si + 1]),
                        reads=[bXT[slot]], writes=[bJK, bSS[si]])
                    em.op("dve", lambda e, nr=nr, si=si: e.tensor_scalar(
                        out=SS[0:nr, si:si + 1], in0=SS[0:nr, si:si + 1], scalar1=1.0 / D, scalar2=EPS,
                        op0=ALU.mult, op1=ALU.add), reads=[bSS[si]], writes=[bSS[si]])
                    em.op("act", lambda e, nr=nr, si=si: e.sqrt(out=SS[0:nr, si:si + 1], in_=SS[0:nr, si:si + 1]),
                          reads=[bSS[si]], writes=[bSS[si]])
                    em.op("dve", lambda e, nr=nr, si=si: e.reciprocal(out=SS[0:nr, si:si + 1], in_=SS[0:nr, si:si + 1]),
                          reads=[bSS[si]], writes=[bSS[si]])
                    em.op("dve", lambda e, bi=bi, nr=nr, si=si, xt=xt: e.scalar_tensor_tensor(
                        out=xt[0:nr, bi, :], in0=xt[0:nr, bi, :], scalar=SS[0:nr, si:si + 1], in1=GB[0:nr, :],
                        op0=ALU.mult, op1=ALU.mult), reads=[bXT[slot], bSS[si], bGB], writes=[bXT[slot]])
                store_x(y_out, "y", td, slot)

        def run_all():
            nonlocal UT, ST8, SO8, CW, KB16, TS, VA, G, ZR, PTB, PTF, IDX, IOPF
            nonlocal W1, W1B, W2, PEV, BIAS, XCc, ACTK, ACTV, CKTt, CVt
            nonlocal QAs, Q1S, GMs, FRCs, G36s, CKs, CVs, KWs, VWs, KSc, VSc, PBs, NMXs, NMts, OCs, OSs, OWs, Yts, RCs, IMPs, SCRs, SC2s, M8s, R9s, TYs
            nonlocal VSB, KWB, VWB, CKB, CVB, QA, PB, Yt, NMX, G36, FRC, IMP, SCR, SC2, NMt, M8, R9, RC, TY, GM, Q1
            if stop_after == "consts":
                return
            with Scope() as sc:
                UT = sc.sb("UT", [128, 6, T + 2], F32)
                ST8 = sc.sb("ST8", [8, CONV], F32)
                SO8 = sc.sb("SO8", [8, CONV], F32)
                CW = sc.sb("CW", [128, 2, 6, 3], F32)
                KB16 = sc.sb("KB16", [128, 512], BF16)
                TS = sc.sb("TS", [64, 8, 128], BF16)
                VA = [sc.sb("VA%d" % i, [128, 4, 65], BF16) for i in range(2)]
                for l in range(2):
                    for k in range(3):
                        em.dma("sp", lambda e, l=l, k=k: e.dma_start(
                            out=CW[:, l, :, k], in_=conv_w[l, k, :].rearrange("(j p) -> p j", p=128),
                            allow_slow_non_contiguous=True), writes=[bC])
                phase_mem()
                if stop_after == "mem":
                    return
                cur, ckey = x_in, "in"
                for l in range(2):
                    phase_mixer_a(l, cur, ckey, XS[0], "s0")
                    if stop_after == "mixer%d" % l:
                        return
                    phase_ffn(l, 0, XS[0], "s0", None, None, XS[1], "s1")
                    phase_ffn(l, 1, None, None, XS[1], "s1", XS[2], "s2")
                    if stop_after == "ffnb%d" % l:
                        return
                    cur, ckey = XS[2], "s2"
                phase_kv(cur, ckey)
            if stop_after == "kv":
                return
            if SAMPLE_NSA:
                with Scope() as sc:
                    sc.sb("UTd", [128, 6, T + 2], F32)
                    sc.sb("ST8d", [8, CONV], F32)
                    sc.sb("SO8d", [8, CONV], F32)
                    sc.sb("CWd", [128, 2, 6, 3], F32)
                    a_old = [nc.lookup_mloc(KB16).addr, nc.lookup_mloc(TS).addr, nc.lookup_mloc(VA[0]).addr, nc.lookup_mloc(VA[1]).addr]
                    KB16 = sc.sb("KB16", [128, 512], BF16)
                    TS = sc.sb("TS", [64, 8, 128], BF16)
                    VA = [sc.sb("VA%d" % i, [128, 4, 65], BF16) for i in range(2)]
                    assert a_old == [nc.lookup_mloc(KB16).addr, nc.lookup_mloc(TS).addr, nc.lookup_mloc(VA[0]).addr, nc.lookup_mloc(VA[1]).addr]
                    G = [sc.sb("G%d" % i, [128, 512], F32) for i in range(2)]
                    ZR = sc.sb("ZR", [128, 260], BF16)
                    PTB = sc.sb("PTB", [128, 64], I32)
                    PTF = sc.sb("PTF", [128, 64], F32)
                    IDX = sc.sb("IDX", [128, 64], I32)
                    IOPF = sc.sb("IOPF", [128, 1], F32)
                    phase_sample_ctx()
                if stop_after == "sctx":
                    return
            with Scope(xt1=False) as sc:
                W1 = sc.sb("W1", [64, 2, 32, 128], BF16)
                W1B = sc.sb("W1B", [128, 2, 16, 128], BF16)
                W2 = sc.sb("W2", [128, 2, 64], BF16)
                PEV = sc.sb("PEV", [128, 2, 16], BF16)
                BIAS = sc.sb("BIAS", [128, 2], F32)
                XCc = sc.sb("XCc", [64, 8, 1040], BF16)
                ACTK = sc.sb("ACTK", [128, 4, 64], BF16)
                ACTV = sc.sb("ACTV", [128, 4, 128], BF16)
                CKTt = sc.sb("CKTt", [64, 4, 64], BF16)
                CVt = sc.sb("CVt", [128, 4, 65], BF16)
                jobs = [(XC_T, 4, CKT_S, 256, CV_S, bKVS, False)]
                if SAMPLE_NSA:
                    jobs += [(XCS[s_], 8, CKS[s_], 512, CVS[s_], bSCT, True) for s_ in range(4)]
                bCS_ = phase_compress(jobs)
                bCSs_box[0] = bCS_
            if stop_after == "cmp":
                return
            for l in (2, 3):
                with Scope(xt1=False) as sc:
                    VSB = sc.sb("VSB", [128, 32, 260], BF16)
                    KWB = sc.sb("KWB", [70, 4, 1024], BF16)
                    VWB = sc.sb("VWB", [128, 8, 260], BF16)
                    CKB = sc.sb("CKB", [70, 4, 256], BF16)
                    CVB = sc.sb("CVB", [128, 2, 260], BF16)
                    QA = sc.sb("QA", [70, 3, T], BF16)
                    PB = [sc.sb("PB%d" % i, [128, 384], BF16) for i in range(3)]
                    Yt = sc.sb("Yt", [128, 4, 768], BF16)
                    NMX = sc.sb("NMX", [128, 4096], BF16)
                    G36 = sc.sb("G36", [128, 4, 36], F32)
                    FRC = sc.sb("FRC", [128, 4, 64], F32)
                    IMP = sc.sb("IMP", [128, 64], F32)
                    SCR = sc.sb("SCR", [128, 64], F32)
                    SC2 = sc.sb("SC2", [128, 64], F32)
                    NMt = sc.sb("NMt", [128, 64], F32)
                    M8 = sc.sb("M8", [128, 24], F32)
                    R9 = sc.sb("R9", [128, 9], F32)
                    RC = sc.sb("RC", [128, 3], F32)
                    TY = sc.sb("TY", [128, 64], F32)
                    GM = sc.sb("GM", [128, 32], BF16)
                    Q1 = sc.sb("Q1", [128, T], BF16)
                    phase_mixer_b(l, cur, ckey, XS[0], "s0", bCS_)
                if stop_after == "mixer%d" % l:
                    return
                if SAMPLE_NSA:
                    with Scope(xt1=False) as sc:
                        QAs = sc.sb("QAs", [70, 12, NS], BF16)
                        Q1S = sc.sb("Q1S", [128, NS], BF16)
                        GMs = sc.sb("GMs", [128, 32], BF16)
                        FRCs = sc.sb("FRCs", [8, 130], F32)
                        G36s = sc.sb("G36s", [8, 4, 36], F32)
                        CKs = sc.sb("CKs", [70, 4, 512], BF16)
                        CVs = sc.sb("CVs", [128, 4, 260], BF16)
                        KWs = sc.sb("KWs", [70, 4, 640], BF16)
                        VWs = sc.sb("VWs", [128, 5, 260], BF16)
                        KSc = [sc.sb("KSc%d" % i, [70, 4, 512], BF16) for i in range(2)]
                        VSc = [sc.sb("VSc%d" % i, [128, 4, 260], BF16) for i in range(2)]
                        PBs = [sc.sb("PBs%d" % i, [128, 24], BF16) for i in range(3)]
                        NMXs = [sc.sb("NMXs%d" % i, [8, 128], BF16) for i in range(2)]
                        NMts = sc.sb("NMts", [8, 4, 130], F32)
                        OCs = sc.sb("OCs", [8, 4, 195], F32)
                        OSs = sc.sb("OSs", [8, 4, 195], F32)
                        OWs = sc.sb("OWs", [8, 4, 195], F32)
                        Yts = sc.sb("Yts", [8, 4, 768], BF16)
                        RCs = sc.sb("RCs", [8, 3], F32)
                        IMPs = sc.sb("IMPs", [8, 128], F32)
                        SCRs = sc.sb("SCRs", [8, 130], F32)
                        SC2s = sc.sb("SC2s", [8, 130], F32)
                        M8s = sc.sb("M8s", [8, 24], F32)
                        R9s = sc.sb("R9s", [8, 9], F32)
                        TYs = sc.sb("TYs", [8, 64], F32)
                        phase_mixer_b_sample(l, cur, ckey, XS[0], "s0")
                if stop_after == "smixer%d" % l:
                    return
                with Scope() as sc:
                    phase_ffn(l, 0, XS[0], "s0", None, None, XS[1], "s1")
                    phase_ffn(l, 1, None, None, XS[1], "s1", XS[2], "s2")
                cur, ckey = XS[2], "s2"
            with Scope(ht1=False) as sc:
                phase_final(cur, ckey)
        UT = ST8 = SO8 = CW = KB16 = TS = VA = G = ZR = PTB = PTF = IDX = IOPF = None
        W1 = W1B = W2 = PEV = BIAS = XCc = ACTK = ACTV = CKTt = CVt = None
        QAs = Q1S = GMs = FRCs = G36s = CKs = CVs = KWs = VWs = KSc = VSc = PBs = NMXs = NMts = OCs = OSs = OWs = Yts = None
        RCs = IMPs = SCRs = SC2s = M8s = R9s = TYs = None
        VSB = KWB = VWB = CKB = CVB = QA = PB = Yt = NMX = G36 = FRC = IMP = SCR = SC2 = NMt = M8 = R9 = RC = TY = GM = Q1 = None
        run_all()
        em.finish()
        em.emit()
    return nc


_CACHE = {}


def _trunc_bf16(x):
    x = np.ascontiguousarray(x, dtype=np.float32)
    return (x.view(np.uint32) & np.uint32(0xFFFF0000)).view(np.float32)


def _alibi_slopes(n):
    def pow2(m):
        start = 2.0 ** (-8.0 / m)
        return [start ** (i + 1) for i in range(m)]
    if n & (n - 1) == 0:
        return pow2(n)
    c = 2 ** int(np.floor(np.log2(n)))
    return pow2(c) + _alibi_slopes(2 * c)[0::2][: n - c]


def _const_tables():
    def key_rows(pos):
        a = (pos // 64) * 64
        b = pos % 64
        one = np.ones_like(pos)
        return np.stack([a, a, b, b, one, one]).astype(np.float32)
    slopes = np.array(_alibi_slopes(12), dtype=np.float32)
    sh = _trunc_bf16(slopes)
    sl = _trunc_bf16(slopes - sh)
    t = np.arange(4096, dtype=np.float64)
    qaug = np.zeros((12, 6, 4096), dtype=np.float32)
    for h in range(12):
        v = (np.float64(slopes[h]) * t).astype(np.float32)
        hi = _trunc_bf16(v)
        lo = _trunc_bf16(v - hi)
        qaug[h, 0], qaug[h, 1], qaug[h, 2], qaug[h, 3] = sh[h], sl[h], sh[h], sl[h]
        qaug[h, 4], qaug[h, 5] = -hi, -lo
    tt = np.arange(4096)
    cur = tt // 64
    frc = -np.ones((4096, 64), dtype=np.float32)
    frc[tt[cur >= 1], cur[cur >= 1] - 1] = 1.0e4
    frc[tt, cur] = 2.0e4
    frc[:, 0] = 3.0e4
    gm = (np.arange(128)[:, None] // 4 == np.arange(32)[None, :]).astype(np.float32)
    qaug_s = np.zeros((6, 12, 32), dtype=np.float32)
    for h in range(12):
        v = (np.float64(slopes[h]) * (8192.0 + np.arange(8))).astype(np.float32)
        hi = _trunc_bf16(v)
        lo = _trunc_bf16(v - hi)
        qaug_s[0, h], qaug_s[1, h], qaug_s[2, h], qaug_s[3, h] = sh[h], sl[h], sh[h], sl[h]
        qaug_s[4, h] = np.tile(-hi, 4)
        qaug_s[5, h] = np.tile(-lo, 4)
    frc_s = -np.ones((8, 130), dtype=np.float32)
    frc_s[:, 0], frc_s[:, 128], frc_s[:, 127], frc_s[:, 129] = 3.0e4, 2.0e4, 1.0e4, -1.0e30
    return {"kaug_t": key_rows(np.arange(4096)), "kaug_c": key_rows(16 * np.arange(256) + 31),
            "qaug_t": qaug, "frc_t": frc, "gm_t": gm,
            "kaug_s": key_rows(np.arange(8320)), "kaug_cs": key_rows(16 * np.arange(512) + 31),
            "kaug_ws": key_rows(7680 + np.arange(640)), "qaug_s": np.ascontiguousarray(qaug_s.reshape(6, 384)),
            "frc_s": frc_s}


def kernel(**inputs):
    f = lambda a: np.ascontiguousarray(np.asarray(a, dtype=np.float32))
    if "nc" not in _CACHE:
        _CACHE["nc"] = build_program(_CACHE.get("stop"))
    nc = _CACHE["nc"]
    xp = f(inputs["x_prompt"])
    xs = f(inputs["x_sample"])
    stc = f(inputs["state_conv"])
    cmemkv = f(inputs["cache_mem_kv"])
    win = f(inputs["state_win_kv"])
    shared = {
        "g_mix": f(inputs["g_mix"]), "w_in_a": f(inputs["w_in_a"]), "conv_w": f(inputs["conv_w"]),
        "w_o": f(inputs["w_o"]), "w_mkv": f(inputs["w_mkv"]), "g_mem": f(inputs["g_mem"]).reshape(1, D),
        "g_kv": f(inputs["g_kv"]).reshape(1, D), "w_kv": f(inputs["w_kv"]), "g_ffn": f(inputs["g_ffn"]),
        "w_gu": f(inputs["w_gu"]), "w_dn": f(inputs["w_dn"]), "g_final": f(inputs["g_final"]).reshape(1, D),
    }
    shared.update({k: f(inputs[k]) for k in ("w_in_b", "w1_ck", "w1_cv", "w2_ck", "w2_cv", "pe_ck", "pe_cv")})
    shared.update(_const_tables())
    shared["pool_c"] = f(inputs["cache_cmp_kv"]).reshape(2560 * 128, 512)
    shared["pool_s"] = f(inputs["cache_slc_kv"]).reshape(2560 * 128, 512)
    ptab = np.ascontiguousarray(np.asarray(inputs["page_table"], dtype=np.int32))
    in_maps = []
    for c in range(8):
        b = c % 4
        m = dict(shared)
        m["x_in"] = np.ascontiguousarray(np.concatenate([xp[b], xs[4 * c:4 * c + 4].reshape(NS, D)], axis=0))
        m["stc"] = np.ascontiguousarray(stc[:, 4 * c:4 * c + 4].reshape(2, 8, CONV))
        m["cmem"] = np.ascontiguousarray(cmemkv[:, 4 * c:4 * c + 4].reshape(4, 4, 256, 512))
        m["memp"] = f(inputs["mem_prompt"])[b]
        m["win_in"] = np.ascontiguousarray(win[4 * c:4 * c + 4].reshape(4, 512, 512))
        m["ptab"] = np.ascontiguousarray(ptab[4 * c:4 * c + 4])
        in_maps.append(m)
    res = run_bass_kernel_spmd(nc, in_maps, core_ids=list(range(8)))
    R = res.results
    y_prompt = np.stack([R[b]["y_out"][:SEQ] for b in range(4)])
    y_sample = np.concatenate([R[c]["y_out"][SEQ:].reshape(4, 8, D) for c in range(8)])
    conv_p = np.stack([R[b]["cs_p"] for b in range(4)], axis=1)
    conv_s = np.concatenate([R[c]["cs_s"].reshape(2, 4, 2, CONV) for c in range(8)], axis=1)
    mem_kv_p = np.stack([R[b]["mkv_p"] for b in range(4)], axis=1).reshape(4, 4, 256, 2, 4, 64)
    cmp_p = np.stack([R[b]["cmp_o"][:SEQ] for b in range(4)]).reshape(4, SEQ, 2, 4, 64)
    slc_p = np.stack([R[b]["slc_o"][:SEQ] for b in range(4)]).reshape(4, SEQ, 2, 4, 64)
    win_p = np.stack([R[b]["winp_o"] for b in range(4)]).reshape(4, 512, 2, 4, 64)
    cmp_s = np.concatenate([R[c]["cmp_o"][SEQ:].reshape(4, 8, 2, 4, 64) for c in range(8)])
    slc_s = np.concatenate([R[c]["slc_o"][SEQ:].reshape(4, 8, 2, 4, 64) for c in range(8)])
    win_s = np.concatenate([R[c]["wins_o"].reshape(4, 512, 2, 4, 64) for c in range(8)])
    return (y_prompt, y_sample, conv_p, conv_s, mem_kv_p, cmp_p, slc_p, win_p, cmp_s, slc_s, win_s)
```

```python
import contextlib
import numpy as np
import concourse.bass as bass
import concourse.mybir as mybir
from concourse.bass_utils import run_bass_kernel_spmd

F32 = mybir.dt.float32
BF16 = mybir.dt.bfloat16
I32 = mybir.dt.int32
AF = mybir.ActivationFunctionType
ALU = mybir.AluOpType
AX = mybir.AxisListType

ENGS = ("pe", "act", "dve", "pool", "sp")
DMA_K = 8
EPOCH = 8192

D = 1024
SEQ = 4096
NS = 32
NTOK = SEQ + NS
DFF = 2816
CONV = 768
T = 512
NT = SEQ // T
EPS = 1e-6
import os
DBG = int(os.environ.get('DBG', '9'))
DBGOUT = int(os.environ.get('DBGOUT', '0'))
SAMPLE_NSA = int(os.environ.get('SAMPLE_NSA', '1'))


class Buf:
    __slots__ = ("name", "w", "r", "rp", "excl")

    def __init__(self, name="", excl=False):
        self.name = name
        self.w = {}
        self.r = {}
        self.rp = {}
        self.excl = excl


class Emitter:
    def __init__(self, nc):
        self.nc = nc
        self.q = {e: [] for e in ENGS}
        self.cnt = {e: 0 for e in ENGS}
        self.waited = {e: {} for e in ENGS}
        self.dma_n = {e: 0 for e in ENGS}
        self.semkeys = set()

    def _need(self, eng, ev, waits):
        key, val = ev
        if key == ("e", "pe") and eng == "pe":
            return
        if self.waited[eng].get(key, 0) >= val:
            return
        waits[key] = max(waits.get(key, 0), val)

    def _deps(self, eng, reads, writes, par=False):
        waits = {}
        for b in reads:
            for k, v in b.w.items():
                self._need(eng, (k, v), waits)
        for b in writes:
            if not par:
                for k, v in b.w.items():
                    self._need(eng, (k, v), waits)
            else:
                for k, v in b.rp.items():
                    self._need(eng, (k, v), waits)
            for k, v in b.r.items():
                self._need(eng, (k, v), waits)
        for k, v in waits.items():
            self.waited[eng][k] = v
        return list(waits.items())

    def _mark(self, ev, reads, writes, par=False):
        k, v = ev
        for b in reads:
            if b.r.get(k, 0) < v:
                b.r[k] = v
        for b in writes:
            if par:
                b.w[k] = max(b.w.get(k, 0), v)
                for kk, vv in b.r.items():
                    b.rp[kk] = max(b.rp.get(kk, 0), vv)
            else:
                b.w = {k: v}
                b.rp = dict(b.r)
            b.r = {}

    def op(self, eng, fn, reads=(), writes=(), signal=True):
        writes = list(writes) + [b for b in reads if b.excl]
        reads = [b for b in reads if not b.excl]
        waits = self._deps(eng, reads, writes)
        if signal:
            self.cnt[eng] += 1
            ev = (("e", eng), self.cnt[eng])
            inc = (ev[0], ev[1], 1)
        else:
            ev = (("e", eng), self.cnt[eng] + 1)
            inc = None
        self.semkeys.add(ev[0])
        self.q[eng].append((waits, fn, inc))
        self._mark(ev, reads, writes)
        return ev

    def dma(self, eng, fn, reads=(), writes=(), par=False):
        n = self.dma_n[eng]
        self.dma_n[eng] += 1
        j, m = n % DMA_K, n // DMA_K
        key = ("d", eng, j)
        self.semkeys.add(key)
        waits = dict(self._deps(eng, reads, writes, par))
        if m > 0 and self.waited[eng].get(key, 0) < 16 * m:
            waits[key] = max(waits.get(key, 0), 16 * m)
            self.waited[eng][key] = waits[key]
        ev = (key, 16 * (m + 1))
        self.q[eng].append((list(waits.items()), fn, (key, ev[1], 16)))
        self._mark(ev, reads, writes, par)
        return ev

    def barrier(self):
        evs = []
        for eng in ENGS:
            n = self.dma_n[eng]
            for j in range(min(n, DMA_K)):
                cnt = (n - j + DMA_K - 1) // DMA_K
                evs.append((("d", eng, j), 16 * cnt))
            if self.cnt[eng] > 0:
                evs.append((("e", eng), self.cnt[eng]))
        for eng in ENGS:
            waits = {}
            for k, v in evs:
                if k == ("e", eng):
                    continue
                if self.waited[eng].get(k, 0) < v:
                    waits[k] = v
                    self.waited[eng][k] = v
            if waits:
                self.q[eng].append((list(waits.items()), None, None))

    def finish(self):
        waits = []
        for eng in ENGS:
            n = self.dma_n[eng]
            for j in range(min(n, DMA_K)):
                cnt = (n - j + DMA_K - 1) // DMA_K
                waits.append((("d", eng, j), 16 * cnt))
        for eng in ENGS:
            if eng != "sp" and self.cnt[eng] > 0:
                waits.append((("e", eng), self.cnt[eng]))
        self.q["sp"].append((waits, None, None))

    def emit(self):
        nc = self.nc
        with contextlib.ExitStack() as st:
            sems = {}
            for k in sorted(self.semkeys, key=str):
                if k[0] == "e":
                    for ep in range((self.cnt[k[1]] + EPOCH - 1) // EPOCH):
                        sems[(k, ep)] = st.enter_context(nc.semaphore("s_e_%s_%d" % (k[1], ep)))
                else:
                    sems[k] = st.enter_context(nc.semaphore("s_" + "_".join(str(x) for x in k)))

            def semval(k, v):
                if k[0] == "e":
                    ep = (v - 1) // EPOCH
                    return sems[(k, ep)], v - ep * EPOCH
                return sems[k], v

            block = st.enter_context(nc.Block())

            def runner(ops):
                def run(e):
                    for waits, fn, inc in ops:
                        for k, v in waits:
                            sm, vv = semval(k, v)
                            e.wait_ge(sm, vv)
                        if fn is not None:
                            ins = fn(e)
                            if inc is not None:
                                k, v, amt = inc
                                ins.then_inc(semval(k, v)[0], amt)
                return run

            if self.q["pe"]:
                block.tensor(runner(self.q["pe"]))
            if self.q["act"]:
                block.scalar(runner(self.q["act"]))
            if self.q["dve"]:
                block.vector(runner(self.q["dve"]))
            if self.q["pool"]:
                block.gpsimd(runner(self.q["pool"]))
            if self.q["sp"]:
                block.sync(runner(self.q["sp"]))


class TileDesc:
    def __init__(self, idx):
        self.idx = idx
        self.sample = idx == NT
        if self.sample:
            self.row0, self.T, self.blocks = SEQ, NS, [(0, NS)]
        else:
            self.row0, self.T, self.blocks = idx * T, T, [(i * 128, 128) for i in range(4)]


def build_program(stop_after=None):
    nc = bass.Bass("TRN2", target_bir_lowering=False)
    em = Emitter(nc)

    def din(name, shape, dt=F32):
        return nc.dram_tensor(name, list(shape), dt, kind="ExternalInput").ap()

    def dout(name, shape, dt=F32):
        return nc.dram_tensor(name, list(shape), dt, kind="ExternalOutput").ap()

    def dscr(name, shape, dt=F32):
        return nc.dram_tensor(name, list(shape), dt, kind=("ExternalOutput" if DBGOUT else "Internal")).ap()

    x_in = din("x_in", [NTOK, D])
    stc = din("stc", [2, 8, CONV])
    cmem = din("cmem", [4, 4, 256, 512])
    memp = din("memp", [256, D])
    g_mix = din("g_mix", [4, D])
    w_in_a = din("w_in_a", [2, D, 2560])
    conv_w = din("conv_w", [2, 3, CONV])
    w_o = din("w_o", [4, D, D])
    w_mkv = din("w_mkv", [4, D, 512])
    g_mem = din("g_mem", [1, D])
    g_kv = din("g_kv", [1, D])
    w_kv = din("w_kv", [D, 1536])
    g_ffn = din("g_ffn", [4, D])
    w_gu = din("w_gu", [4, D, 2 * DFF])
    w_dn = din("w_dn", [4, DFF, D])
    g_final = din("g_final", [1, D])
    win_in = din("win_in", [4, 512, 512])

    y_out = dout("y_out", [NTOK, D])
    cs_p = dout("cs_p", [2, 2, CONV])
    cs_s = dout("cs_s", [2, 8, CONV])
    mkv_p = dout("mkv_p", [4, 256, 512])
    cmp_o = dout("cmp_o", [NTOK, 512])
    slc_o = dout("slc_o", [NTOK, 512])
    winp_o = dout("winp_o", [512, 512])
    wins_o = dout("wins_o", [4, 512, 512])

    w_in_b = din("w_in_b", [2, D, 1060])
    w1_c = [din("w1_ck", [32, 64, 128]), din("w1_cv", [32, 64, 128])]
    w2_c = [din("w2_ck", [128, 64]), din("w2_cv", [128, 64])]
    pe_c = [din("pe_ck", [32, 64]), din("pe_cv", [32, 64])]
    kaug_t = din("kaug_t", [6, 4096])
    kaug_c = din("kaug_c", [6, 256])
    qaug_t = din("qaug_t", [12, 6, 4096])
    frc_t = din("frc_t", [4096, 64])
    gm_t = din("gm_t", [128, 32])
    npool_rows = 2560 * 128 if SAMPLE_NSA else 128
    pool_c = din("pool_c", [npool_rows, 512])
    pool_s = din("pool_s", [npool_rows, 512])
    ptab = din("ptab", [4, 64], I32)
    kaug_s = din("kaug_s", [6, 8320])
    kaug_cs = din("kaug_cs", [6, 512])
    kaug_ws = din("kaug_ws", [6, 640])
    qaug_s = din("qaug_s", [6, 384])
    frc_s = din("frc_s", [8, 130])
    LS, LW, LC = 8320, 640, 8208
    XCS = dscr("xcs", [4, 64, 8 * LC], BF16)
    KSS = dscr("kss", [4, 64, 4 * LS], BF16)
    VSS = dscr("vss", [4, LS, 260], BF16)
    KWS = dscr("kws", [4, 64, 4 * LW], BF16)
    VWS = dscr("vws", [4, LW, 260], BF16)
    CKS = dscr("cks", [4, 64, 4 * 512], BF16)
    CVS = dscr("cvs", [4, 512, 260], BF16)
    KS_T = dscr("ks_t", [64, 4 * NTOK], BF16)
    KW_T = dscr("kw_t", [64, 4 * NTOK], BF16)
    XC_T = dscr("xc_t", [64, 8 * NTOK], BF16)
    VS_S = dscr("vs_s", [NTOK, 260], BF16)
    VW_S = dscr("vw_s", [NTOK, 260], BF16)
    CKT_S = dscr("ckt_s", [64, 4 * 256], BF16)
    CV_S = dscr("cv_s", [256, 260], BF16)
    XS = [dscr("xs%d" % i, [NTOK, D]) for i in range(3)]
    HTS = dscr("hts", [NT + 1, 128, 8 * T], BF16)
    WINP = dscr("winp_s", [NTOK, 512])
    xbufs = {}

    def xb(key, ti):
        return xbufs.setdefault((key, ti), Buf("x%s_%d" % (key, ti)))

    tiles = [TileDesc(i) for i in range(NT + 1)]

    with contextlib.ExitStack() as st:
        def sb(name, shape, dt):
            return st.enter_context(nc.sbuf_tensor(name, list(shape), dt))

        WA = sb("WA", [128, 33792], BF16)
        XT = [sb("XT0", [128, 4, D], F32), None]
        HT = [sb("HT0", [128, 8, T], BF16), None]
        Hh = sb("Hh", [128, 4, D], BF16)
        BA = sb("BA", [128, 11 * T], BF16)
        CS = [sb("CS%d" % i, [128, T], F32) for i in range(2)]
        TM = [sb("TM%d" % i, [128, T], F32) for i in range(2)]
        STG = [sb("STG%d" % i, [128, 512], F32) for i in range(2)]
        GB = sb("GB", [128, D], F32)
        IDB = sb("IDB", [128, 128], BF16)
        IDF = sb("IDF", [128, 128], F32)
        IOT = sb("IOT", [128, 128], I32)
        ONE = sb("ONE", [128, 128], BF16)
        MKT = sb("MKT", [128, 4, 2, 256], BF16)
        VP = sb("VP", [128, 2, 4, 256], BF16)
        SKT = sb("SKT", [128, 4, 2, 256], BF16)
        SKS = sb("SKS", [128, 2, 4, 256], BF16)
        SV = sb("SV", [128, 2, 4, 256], BF16)
        SS = sb("SS", [128, 8], F32)
        JK = sb("JK", [128, D], BF16)
        scope_n = [0]
        addr_chk = {}

        class Scope:
            def __init__(self, xt1=True, ht1=True):
                self.xt1, self.ht1 = xt1, ht1

            def __enter__(self):
                self.st = contextlib.ExitStack()
                self.st.__enter__()
                scope_n[0] += 1
                if self.xt1:
                    XT[1] = self.sb("XT1", [128, 4, D], F32)
                    a = nc.lookup_mloc(XT[1]).addr
                    assert addr_chk.setdefault("xt1", a) == a
                    if self.ht1:
                        HT[1] = self.sb("HT1", [128, 8, T], BF16)
                        a = nc.lookup_mloc(HT[1]).addr
                        assert addr_chk.setdefault("ht1", a) == a
                return self

            def sb(self, name, shape, dt):
                return self.st.enter_context(nc.sbuf_tensor("%s_s%d" % (name, scope_n[0]), list(shape), dt))

            def __exit__(self, *a):
                em.barrier()
                self.st.__exit__(None, None, None)
                return False

        PT = [st.enter_context(nc.psum_tensor("PT%d" % i, [128, 1024], BF16)) for i in range(2)]
        PS = [st.enter_context(nc.psum_tensor("PS%d" % i, [128, 512], F32)) for i in range(6)]
        bPT = [Buf("PT%d" % i, True) for i in range(2)]
        bPS = [Buf("PS%d" % i, True) for i in range(6)]
        rr = {"ps": 0, "pt": 0, "cs": 0, "tm": 0, "stg": 0, "ss": 0}

        def nxt(kind, n):
            i = rr[kind]
            rr[kind] = (i + 1) % n
            return i

        def ps_next():
            i = nxt("ps", 6)
            return PS[i], bPS[i]

        def pt_next():
            i = nxt("pt", 2)
            return PT[i], bPT[i]

        bWA = Buf("WA")
        bXT = [Buf("XT0"), Buf("XT1")]
        bHT = [Buf("HT0"), Buf("HT1")]
        bHh = [Buf("Hh%d" % i) for i in range(4)]
        bBA = [Buf("BA%d" % i) for i in range(11)]
        bUT = [Buf("UT%d" % i) for i in range(6)]
        bCS = [Buf("CS0"), Buf("CS1")]
        bTM = [Buf("TM0"), Buf("TM1")]
        bSTG = [Buf("STG0"), Buf("STG1")]
        bGB = Buf("GB")
        bC = Buf("consts")
        bMKT = Buf("MKT")
        bVP = Buf("VP")
        bSKT = Buf("SKT")
        bSKS = Buf("SKS")
        bSV = Buf("SV")
        bSS = [Buf("SS%d" % i) for i in range(8)]
        bJK = Buf("JK")
        bST8 = Buf("ST8")
        bSO8 = Buf("SO8")
        bOUT = Buf("outs")

        em.op("pool", lambda e: e.iota(IOT[:, :], [[-1, 128]], base=0, channel_multiplier=1), writes=[bC])
        em.op("dve", lambda e: e.tensor_single_scalar(out=IDF[:, :], in_=IOT[:, :], scalar=0, op=ALU.is_equal),
              reads=[bC], writes=[bC])
        em.op("dve", lambda e: e.tensor_copy(out=IDB[:, :], in_=IDF[:, :]), reads=[bC], writes=[bC])
        em.op("pool", lambda e: e.memset(ONE[:, :], 1.0), writes=[bC])
        def load_gain(g_ap_row):
            em.dma("sp", lambda e: e.dma_start(out=GB[:, :], in_=g_ap_row.partition_broadcast(128)), writes=[bGB])

        def load_w(dst_off, src, kch, ncols, col0=0, first=True):
            step = 2 if kch % 2 == 0 else 1
            for k0 in range(0, kch, step):
                def f(e, k0=k0):
                    dst = WA[:, dst_off + k0 * ncols: dst_off + (k0 + step) * ncols].rearrange(
                        "p (k c) -> p k c", k=step)
                    s_ = src[k0 * 128:(k0 + step) * 128, col0:col0 + ncols].rearrange("(k p) c -> p k c", p=128)
                    return e.dma_start(out=dst, in_=s_)
                em.dma("pool", f, writes=[bWA], par=not (first and k0 == 0))

        def wa(off, k, ncols, c0, n):
            return WA[:, off + k * ncols + c0: off + k * ncols + c0 + n]

        def load_x(src, key, td, slot):
            if td.sample:
                em.dma("sp", lambda e: e.dma_start(out=XT[slot][0:NS, 0, :], in_=src[SEQ:SEQ + NS, :]),
                       reads=[xb(key, td.idx)], writes=[bXT[slot]])
            else:
                em.dma("sp", lambda e: e.dma_start(
                    out=XT[slot][:, :, :], in_=src[td.row0:td.row0 + T, :].rearrange("(b p) d -> p b d", p=128)),
                    reads=[xb(key, td.idx)], writes=[bXT[slot]])

        def store_x(dst, key, td, slot):
            if td.sample:
                em.dma("sp", lambda e: e.dma_start(out=dst[SEQ:SEQ + NS, :], in_=XT[slot][0:NS, 0, :]),
                       reads=[bXT[slot]], writes=[xb(key, td.idx)])
            else:
                em.dma("sp", lambda e: e.dma_start(
                    out=dst[td.row0:td.row0 + T, :].rearrange("(b p) d -> p b d", p=128), in_=XT[slot][:, :, :]),
                    reads=[bXT[slot]], writes=[xb(key, td.idx)])

        def norm_T(td, xslot, hslot, src_t=None, src_b=None):
            xt = XT[xslot] if src_t is None else src_t
            bx = bXT[xslot] if src_b is None else src_b
            for bi, (r0, nr) in enumerate(td.blocks):
                si = nxt("ss", 8)
                em.op("pool", lambda e, si=si: e.memset(SS[:, si:si + 1], 0.0), writes=[bSS[si]])
                em.op("act", lambda e, bi=bi, nr=nr, si=si: e.activation(
                    out=JK[0:nr, :], in_=xt[0:nr, bi, :], func=AF.Square, accum_out=SS[0:nr, si:si + 1]),
                    reads=[bx], writes=[bJK, bSS[si]])
                em.op("dve", lambda e, nr=nr, si=si: e.tensor_scalar(
                    out=SS[0:nr, si:si + 1], in0=SS[0:nr, si:si + 1], scalar1=1.0 / D, scalar2=EPS,
                    op0=ALU.mult, op1=ALU.add), reads=[bSS[si]], writes=[bSS[si]])
                em.op("act", lambda e, nr=nr, si=si: e.sqrt(out=SS[0:nr, si:si + 1], in_=SS[0:nr, si:si + 1]),
                      reads=[bSS[si]], writes=[bSS[si]])
                em.op("dve", lambda e, nr=nr, si=si: e.reciprocal(out=SS[0:nr, si:si + 1], in_=SS[0:nr, si:si + 1]),
                      reads=[bSS[si]], writes=[bSS[si]])
                em.op("dve", lambda e, bi=bi, nr=nr, si=si: e.scalar_tensor_tensor(
                    out=Hh[0:nr, bi, :], in0=xt[0:nr, bi, :], scalar=SS[0:nr, si:si + 1], in1=GB[0:nr, :],
                    op0=ALU.mult, op1=ALU.mult), reads=[bx, bSS[si], bGB], writes=[bHh[bi]])
            TT = td.T
            for kk in range(4):
                pt, bpt = pt_next()
                nb = len(td.blocks)
                for k2 in range(2):
                    k = kk * 2 + k2
                    for bi, (r0, nr) in enumerate(td.blocks):
                        last = (k2 == 1 and bi == nb - 1)
                        em.op("pe", lambda e, k=k, k2=k2, bi=bi, r0=r0, nr=nr, pt=pt: e.transpose(
                            pt[:, k2 * 512 + r0: k2 * 512 + r0 + nr], Hh[0:nr, bi, k * 128:(k + 1) * 128],
                            IDB[0:nr, 0:nr]), reads=[bHh[bi], bC], writes=[bpt], signal=last)
                eng = "act" if kk % 2 == 0 else "dve"
                if eng == "act":
                    em.op("act", lambda e, kk=kk, pt=pt: e.copy(
                        out=HT[hslot][:, 2 * kk:2 * kk + 2, 0:TT],
                        in_=pt[:, :].rearrange("p (a t) -> p a t", a=2)[:, :, 0:TT]),
                        reads=[bpt], writes=[bHT[hslot]])
                else:
                    em.op("dve", lambda e, kk=kk, pt=pt: e.tensor_copy(
                        out=HT[hslot][:, 2 * kk:2 * kk + 2, 0:TT],
                        in_=pt[:, :].rearrange("p (a t) -> p a t", a=2)[:, :, 0:TT]),
                        reads=[bpt], writes=[bHT[hslot]])

        def mm_group(out_ap, bout, pairs, extra_reads):
            n = len(pairs)
            for i, (l_, r_) in enumerate(pairs):
                em.op("pe", lambda e, l_=l_, r_=r_, i=i: e.matmul(out_ap, l_, r_, start=(i == 0), stop=(i == n - 1)),
                      reads=extra_reads, writes=[bout], signal=(i == n - 1))

        def phase_mem():
            load_gain(g_mem[0:1, :])
            for l in range(4):
                load_w(l * 4096, w_mkv[l], 8, 512, first=(l == 0))
            em.dma("sp", lambda e: e.dma_start(out=XT[0][:, 0:2, :], in_=memp.rearrange("(b p) d -> p b d", p=128)),
                   writes=[bXT[0]])
            td = TileDesc(0)
            td.T, td.blocks = 256, [(0, 128), (128, 128)]
            if DBG >= 1:
                norm_T(td, 0, 0)
            for l in range(4 if DBG >= 2 else 0):
                for blk in range(2):
                    ps, bps = ps_next()
                    mm_group(ps[:, :], bps, [(HT[0][:, k, blk * 128:(blk + 1) * 128], wa(l * 4096, k, 512, 0, 512))
                                             for k in range(8)], [bHT[0], bWA])
                    si = nxt("stg", 2)
                    em.op("act", lambda e, ps=ps, si=si: e.copy(out=STG[si][:, :], in_=ps[:, :]),
                          reads=[bps], writes=[bSTG[si]])
                    em.op("dve", lambda e, ps=ps, blk=blk, l=l: e.tensor_copy(out=VP[:, blk, l, :], in_=ps[:, 256:512]),
                          reads=[bps], writes=[bVP])
                    em.dma("sp", lambda e, si=si, l=l, blk=blk: e.dma_start(
                        out=mkv_p[l, blk * 128:(blk + 1) * 128, :], in_=STG[si][:, :]), reads=[bSTG[si]], writes=[])
                for hp in range(2 if DBG >= 3 else 0):
                    ps, bps = ps_next()
                    mm_group(ps[:, 0:256], bps, [(wa(l * 4096, k, 512, hp * 128, 128), HT[0][:, k, 0:256])
                                                 for k in range(8)], [bHT[0], bWA])
                    em.op("act", lambda e, ps=ps, l=l, hp=hp: e.copy(out=MKT[:, l, hp, :], in_=ps[:, 0:256]),
                          reads=[bps], writes=[bMKT])

        def load_sample_mem(l):
            for s in range(4):
                em.dma("pool", lambda e, s=s: e.dma_start(
                    out=SV[:, :, s, :], in_=cmem[l, s, :, 256:512].rearrange("(c p) f -> p c f", p=128)), writes=[bSV])
                em.dma("pool", lambda e, s=s: e.dma_start(
                    out=SKS[:, :, s, :], in_=cmem[l, s, :, 0:256].rearrange("(c p) f -> p c f", p=128)), writes=[bSKS])
            for s in range(4):
                pt, bpt = pt_next()
                for hp in range(2):
                    for c in range(2):
                        em.op("pe", lambda e, s=s, hp=hp, c=c, pt=pt: e.transpose(
                            pt[:, hp * 256 + c * 128: hp * 256 + (c + 1) * 128],
                            SKS[:, c, s, hp * 128:(hp + 1) * 128], IDB[:, :]),
                            reads=[bSKS, bC], writes=[bpt], signal=(hp == 1 and c == 1))
                em.op("dve", lambda e, s=s, pt=pt: e.tensor_copy(
                    out=SKT[:, s, :, :], in_=pt[:, 0:512].rearrange("p (h m) -> p h m", h=2)),
                    reads=[bpt], writes=[bSKT])

        def mem_attention(td, l, q_ps, q_bufs):
            TT = td.T
            groups = [(0, TT, None)] if not td.sample else [(s * 8, 8, s) for s in range(4)]
            for hp in range(2):
                for (c0, n, s) in groups:
                    for half in range(2):
                        hs = slice(half * 64, half * 64 + 64)
                        for mc in range(2):
                            ps, bps = ps_next()
                            if s is None:
                                kT = MKT[hs, l, hp, mc * 128:(mc + 1) * 128]
                                kb = bMKT
                            else:
                                kT = SKT[hs, s, hp, mc * 128:(mc + 1) * 128]
                                kb = bSKT
                            mm_group(ps[:, 0:n], bps, [(kT, q_ps[hp][hs, c0:c0 + n])], [kb, q_bufs[hp]])
                            em.op("act", lambda e, ps=ps, mc=mc, n=n: e.activation(
                                out=BA[:, 8 * T + mc * T: 8 * T + mc * T + n], in_=ps[:, 0:n], func=AF.Exp, scale=0.125),
                                reads=[bps], writes=[bBA[8 + mc]])
                        psn, bpsn = ps_next()
                        psd, bpsd = ps_next()
                        if s is None:
                            vv = [VP[:, mc, l, hp * 128:(hp + 1) * 128] for mc in range(2)]
                            vb = bVP
                        else:
                            vv = [SV[:, mc, s, hp * 128:(hp + 1) * 128] for mc in range(2)]
                            vb = bSV
                        pp = [BA[:, 8 * T + mc * T: 8 * T + mc * T + n] for mc in range(2)]
                        mm_group(psn[:, 0:n], bpsn, [(vv[mc], pp[mc]) for mc in range(2)], [vb, bBA[8], bBA[9]])
                        mm_group(psd[:, 0:n], bpsd, [(ONE[:, :], pp[mc]) for mc in range(2)], [bC, bBA[8], bBA[9]])
                        ti = nxt("tm", 2)
                        em.op("dve", lambda e, psd=psd, ti=ti, n=n, hs=hs: e.reciprocal(out=TM[ti][hs, 0:n], in_=psd[hs, 0:n]),
                              reads=[bpsd], writes=[bTM[ti]])
                        em.op("dve", lambda e, psn=psn, ti=ti, n=n, hs=hs, hp=hp, c0=c0: e.tensor_tensor(
                            out=BA[hs, (6 + hp) * T + c0:(6 + hp) * T + c0 + n], in0=psn[hs, 0:n], in1=TM[ti][hs, 0:n],
                            op=ALU.mult), reads=[bpsn, bTM[ti]], writes=[bBA[6 + hp]])

        W_IN, W_O = 0, 8 * 2560

        def uwin(td, j, k):
            if td.sample:
                return UT[:, j, 0:40].rearrange("p (s t) -> p s t", t=10)[:, :, k:k + 8]
            return UT[:, j, k:k + T]

        def tv(td, ap):
            if td.sample:
                return ap.rearrange("p (s t) -> p s t", t=8)
            return ap

        def mixer_a_tile(td, l, xslot):
            TT = td.T
            norm_T(td, xslot, 0)
            hT = HT[0]
            qaps = []
            for hp in range(2):
                ps, bps = ps_next()
                mm_group(ps[:, 0:TT], bps, [(wa(W_IN, k, 2560, 2304 + hp * 128, 128), hT[:, k, 0:TT]) for k in range(8)],
                         [bHT[0], bWA])
                qaps.append(None)
                qap = (BA[:, 10 * T: 10 * T + TT] if hp == 0 else HT[1][:, 0, 0:TT])
                qb = bBA[10] if hp == 0 else bHT[1]
                em.op("act", lambda e, ps=ps, qap=qap: e.copy(out=qap, in_=ps[:, 0:TT]), reads=[bps], writes=[qb])
                qaps[hp] = (qap, qb)
            mem_attention(td, l, [qaps[0][0], qaps[1][0]], [qaps[0][1], qaps[1][1]])
            for j in range(6):
                psC, bC_ = ps_next()
                mm_group(psC[:, 0:TT], bC_, [(wa(W_IN, k, 2560, CONV + j * 128, 128), hT[:, k, 0:TT]) for k in range(8)],
                         [bHT[0], bWA])
                psH, bH_ = ps_next()
                mm_group(psH[:, 0:TT], bH_, [(wa(W_IN, k, 2560, 2 * CONV + j * 128, 128), hT[:, k, 0:TT]) for k in range(8)],
                         [bHT[0], bWA])
                psB, bB_ = ps_next()
                mm_group(psB[:, 0:TT], bB_, [(wa(W_IN, k, 2560, j * 128, 128), hT[:, k, 0:TT]) for k in range(8)],
                         [bHT[0], bWA])
                ci = nxt("cs", 2)
                em.op("act", lambda e, psC=psC, ci=ci: e.copy(out=CS[ci][:, 0:TT], in_=psC[:, 0:TT]),
                      reads=[bC_], writes=[bCS[ci]])
                em.op("dve", lambda e, psH=psH, ci=ci, j=j: e.tensor_tensor(
                    out=uwin(td, j, 2), in0=tv(td, psH[:, 0:TT]), in1=tv(td, CS[ci][:, 0:TT]), op=ALU.mult),
                    reads=[bH_, bCS[ci]], writes=[bUT[j]])
                ti = nxt("tm", 2)
                em.op("pool", lambda e, ti=ti, j=j: e.tensor_scalar(
                    out=tv(td, TM[ti][:, 0:TT]), in0=uwin(td, j, 0), scalar1=CW[:, l, j, 0:1], scalar2=None,
                    op0=ALU.mult), reads=[bUT[j], bC], writes=[bTM[ti]])
                for k in (1, 2):
                    em.op("dve", lambda e, ti=ti, j=j, k=k: e.scalar_tensor_tensor(
                        out=tv(td, TM[ti][:, 0:TT]), in0=uwin(td, j, k), scalar=CW[:, l, j, k:k + 1],
                        in1=tv(td, TM[ti][:, 0:TT]), op0=ALU.mult, op1=ALU.add),
                        reads=[bUT[j], bC, bTM[ti]], writes=[bTM[ti]])
                em.op("dve", lambda e, psB=psB, ti=ti, j=j: e.tensor_tensor(
                    out=BA[:, j * T: j * T + TT], in0=psB[:, 0:TT], in1=TM[ti][:, 0:TT], op=ALU.mult),
                    reads=[bB_, bTM[ti]], writes=[bBA[j]])
            for bi, (r0, nr) in enumerate(td.blocks):
                for half in range(2):
                    ps, bps = ps_next()
                    mm_group(ps[0:nr, :], bps,
                             [(BA[:, k * T + r0: k * T + r0 + nr], wa(W_O, k, 1024, half * 512, 512)) for k in range(8)],
                             [bBA[k] for k in range(8)] + [bWA])
                    em.op("dve", lambda e, ps=ps, bi=bi, nr=nr, half=half: e.tensor_tensor(
                        out=XT[xslot][0:nr, bi, half * 512:(half + 1) * 512], in0=ps[0:nr, :],
                        in1=XT[xslot][0:nr, bi, half * 512:(half + 1) * 512], op=ALU.add),
                        reads=[bps, bXT[xslot]], writes=[bXT[xslot]])

        def conv_state_out(td, l):
            if td.sample:
                nrow = 8
                def src(j):
                    return UT[:, j, 0:40].rearrange("p (s t) -> p s t", t=10)[:, :, 8:10]
                dst = cs_s[l, :, :]
            else:
                nrow = 2
                def src(j):
                    return UT[:, j, T:T + 2]
                dst = cs_p[l, :, :]
            for j in range(6):
                ti = nxt("tm", 2)
                em.op("dve", lambda e, j=j, ti=ti: e.tensor_copy(
                    out=(TM[ti][:, 0:nrow].rearrange("p (s t) -> p s t", t=2) if td.sample else TM[ti][:, 0:nrow]),
                    in_=src(j)), reads=[bUT[j]], writes=[bTM[ti]])
                ps, bps = ps_next()
                em.op("pe", lambda e, ps=ps, ti=ti: e.transpose(ps[0:nrow, 0:128], TM[ti][:, 0:nrow], IDF[:, :]),
                      reads=[bTM[ti], bC], writes=[bps])
                em.op("act", lambda e, ps=ps, j=j: e.copy(out=SO8[0:nrow, j * 128:(j + 1) * 128], in_=ps[0:nrow, 0:128]),
                      reads=[bps], writes=[bSO8])
            em.dma("sp", lambda e: e.dma_start(out=dst, in_=SO8[0:nrow, :]), reads=[bSO8], writes=[])

        def phase_mixer_a(l, src, skey, dst, dkey):
            load_gain(g_mix[l:l + 1, :])
            load_w(W_IN, w_in_a[l], 8, 2560)
            load_w(W_O, w_o[l], 8, 1024, first=False)
            load_sample_mem(l)
            em.op("pool", lambda e: e.memset(UT[:, :, 0:2], 0.0), reads=bUT, writes=bUT)
            load_x(src, skey, tiles[0], 0)
            for i, td in enumerate(tiles):
                slot = i % 2
                if i + 1 < len(tiles):
                    load_x(src, skey, tiles[i + 1], (i + 1) % 2)
                if td.sample:
                    em.dma("sp", lambda e: e.dma_start(out=ST8[:, :], in_=stc[l, :, :]), writes=[bST8])
                    for j in range(6):
                        ps, bps = ps_next()
                        em.op("pe", lambda e, ps=ps, j=j: e.transpose(ps[:, 0:8], ST8[0:8, j * 128:(j + 1) * 128], IDF[0:8, 0:8]),
                              reads=[bST8, bC], writes=[bps])
                        em.op("act", lambda e, ps=ps, j=j: e.copy(
                            out=UT[:, j, 0:40].rearrange("p (s t) -> p s t", t=10)[:, :, 0:2],
                            in_=ps[:, 0:8].rearrange("p (s t) -> p s t", t=2)), reads=[bps], writes=[bUT[j]])
                mixer_a_tile(td, l, slot)
                store_x(dst, dkey, td, slot)
                if td.idx == NT - 1 or td.sample:
                    conv_state_out(td, l)
                elif not td.sample:
                    em.op("pool", lambda e: e.tensor_copy(out=UT[:, :, 0:2], in_=UT[:, :, T:T + 2]), reads=bUT, writes=bUT)

        G_OFF, D_OFF = 0, 8 * 2816

        def phase_ffn(l, half, src, skey, acc, akey, dst, dkey):
            c0 = half * 1408
            if half == 0:
                load_gain(g_ffn[l:l + 1, :])
            for k0 in range(0, 8, 2):
                for part in range(2):
                    def f(e, k0=k0, part=part):
                        dst_ = WA[:, k0 * 2816:(k0 + 2) * 2816].rearrange("p (k c) -> p k c", k=2)[:, :, part * 1408:(part + 1) * 1408]
                        s_ = w_gu[l, k0 * 128:(k0 + 2) * 128, part * DFF + c0: part * DFF + c0 + 1408].rearrange(
                            "(k p) c -> p k c", p=128)
                        return e.dma_start(out=dst_, in_=s_)
                    em.dma("pool", f, writes=[bWA], par=not (k0 == 0 and part == 0))
            for k0 in range(0, 11):
                em.dma("pool", lambda e, k0=k0: e.dma_start(
                    out=WA[:, D_OFF + k0 * 1024: D_OFF + (k0 + 1) * 1024],
                    in_=w_dn[l, c0 + k0 * 128: c0 + (k0 + 1) * 128, :]), writes=[bWA], par=True)

            def loads(i):
                td = tiles[i]
                slot = i % 2
                if half == 0:
                    load_x(src, skey, td, slot)
                else:
                    load_x(acc, akey, td, slot)
                    em.dma("sp", lambda e: e.dma_start(out=HT[slot][:, :, :], in_=HTS[td.idx].rearrange("p (k t) -> p k t", k=8)),
                           reads=[xb("hts", td.idx)], writes=[bHT[slot]])

            def ffn_tile(td, slot):
                TT = td.T
                if half == 0:
                    norm_T(td, slot, slot)
                    em.dma("sp", lambda e: e.dma_start(
                        out=HTS[td.idx].rearrange("p (k t) -> p k t", k=8), in_=HT[slot][:, :, :]),
                        reads=[bHT[slot]], writes=[xb("hts", td.idx)])
                hT = HT[slot]
                for c in range(11):
                    psG, bG_ = ps_next()
                    mm_group(psG[:, 0:TT], bG_, [(wa(G_OFF, k, 2816, c * 128, 128), hT[:, k, 0:TT]) for k in range(8)],
                             [bHT[slot], bWA])
                    psU, bU_ = ps_next()
                    mm_group(psU[:, 0:TT], bU_, [(wa(G_OFF, k, 2816, 1408 + c * 128, 128), hT[:, k, 0:TT]) for k in range(8)],
                             [bHT[slot], bWA])
                    ci = nxt("cs", 2)
                    em.op("act", lambda e, psG=psG, ci=ci: e.activation(out=CS[ci][:, 0:TT], in_=psG[:, 0:TT], func=AF.Silu),
                          reads=[bG_], writes=[bCS[ci]])
                    em.op("dve", lambda e, psU=psU, ci=ci, c=c: e.tensor_tensor(
                        out=BA[:, c * T: c * T + TT], in0=psU[:, 0:TT], in1=CS[ci][:, 0:TT], op=ALU.mult),
                        reads=[bU_, bCS[ci]], writes=[bBA[c]])
                for bi, (r0, nr) in enumerate(td.blocks):
                    for hf in range(2):
                        ps, bps = ps_next()
                        mm_group(ps[0:nr, :], bps,
                                 [(BA[:, c * T + r0: c * T + r0 + nr], wa(D_OFF, c, 1024, hf * 512, 512)) for c in range(11)],
                                 bBA + [bWA])
                        em.op("dve", lambda e, ps=ps, bi=bi, nr=nr, hf=hf: e.tensor_tensor(
                            out=XT[slot][0:nr, bi, hf * 512:(hf + 1) * 512], in0=ps[0:nr, :],
                            in1=XT[slot][0:nr, bi, hf * 512:(hf + 1) * 512], op=ALU.add),
                            reads=[bps, bXT[slot]], writes=[bXT[slot]])
                store_x(dst, dkey, td, slot)

            loads(0)
            for i, td in enumerate(tiles):
                if i + 1 < len(tiles):
                    loads(i + 1)
                ffn_tile(td, i % 2)

        bKB = Buf("KB16")
        bTS = Buf("TS")
        bVA = [Buf("VA0"), Buf("VA1")]
        bKVS = Buf("kvscratch")
        rr["va"] = 0

        def kv_extra(ps, bps, br, row, nr, dstT=None, C=None, L=None, dstV=None, bdst=None):
            if dstT is None:
                dstT, C, L = (XC_T, KS_T, KW_T)[br], (8 if br == 0 else 4), NTOK
                dstV = None if br == 0 else (VS_S if br == 1 else VW_S)
                bdst = bKVS
            ncol = 512 if br == 0 else 256
            em.op("dve", lambda e: e.tensor_copy(out=KB16[0:nr, 0:ncol], in_=ps[0:nr, 0:ncol]), reads=[bps], writes=[bKB])
            nch = ncol // 64
            pt, bpt = pt_next()
            for c in range(nch):
                em.op("pe", lambda e, c=c: e.transpose(pt[0:64, c * 128: c * 128 + nr], KB16[0:nr, c * 64:(c + 1) * 64],
                                                       IDB[0:nr, 0:nr]), reads=[bKB, bC], writes=[bpt], signal=(c == nch - 1))
            em.op("act", lambda e: e.copy(out=TS[:, 0:nch, 0:nr],
                                          in_=pt[0:64, 0:nch * 128].rearrange("p (c t) -> p c t", c=nch)[:, :, 0:nr]),
                  reads=[bpt], writes=[bTS])
            em.dma("sp", lambda e: e.dma_start(
                out=dstT.rearrange("d (c s) -> d c s", c=C)[:, :, row:row + nr], in_=TS[:, 0:nch, 0:nr]),
                reads=[bTS], writes=[bdst], par=True)
            if dstV is not None:
                vi = nxt("va", 2)
                em.op("dve", lambda e: e.tensor_copy(out=VA[vi][0:nr, :, 0:64],
                                                     in_=ps[0:nr, 256:512].rearrange("p (h d) -> p h d", h=4)),
                      reads=[bps], writes=[bVA[vi]])
                em.dma("sp", lambda e: e.dma_start(out=dstV[row:row + nr, :], in_=VA[vi][0:nr, :, :].rearrange("p h d -> p (h d)")),
                       reads=[bVA[vi]], writes=[bdst], par=True)

        bSCT = Buf("samplectx")
        bCSs_box = [None]

        def phase_sample_ctx():
            bG = [Buf("G0"), Buf("G1")]
            bIDX = Buf("IDX")
            rr["g"] = 0
            for vi in range(2):
                em.op("pool", lambda e, vi=vi: e.memset(VA[vi][:, :, :], 1.0), writes=[bVA[vi]])
            em.op("pool", lambda e: e.memset(ZR[:, :], 0.0), writes=[bIDX])
            em.op("dve", lambda e: e.tensor_copy(out=IOPF[:, :], in_=IOT[:, 0:1]), reads=[bC], writes=[bIDX])

            def one_seq(s):
                em.dma("sp", lambda e: e.dma_start(out=PTB[:, :], in_=ptab[s:s + 1, :].partition_broadcast(128)), writes=[bIDX])
                em.op("dve", lambda e: e.tensor_copy(out=PTF[:, :], in_=PTB[:, :]), reads=[bIDX], writes=[bIDX])
                em.op("dve", lambda e: e.tensor_scalar(out=PTF[:, :], in0=PTF[:, :], scalar1=128.0, scalar2=IOPF[:, 0:1],
                                                       op0=ALU.mult, op1=ALU.add), reads=[bIDX], writes=[bIDX])
                em.op("dve", lambda e: e.tensor_copy(out=IDX[:, :], in_=PTF[:, :]), reads=[bIDX], writes=[bIDX])
                for br, pool_ap in ((0, pool_c), (1, pool_s)):
                    for j in range(64):
                        gi = nxt("g", 2)
                        em.dma("pool", lambda e, gi=gi, j=j, pool_ap=pool_ap: e.indirect_dma_start(
                            out=G[gi][:, :], out_offset=None, in_=pool_ap[:, :],
                            in_offset=bass.IndirectOffsetOnAxis(ap=IDX[:, j:j + 1], axis=0)), reads=[bIDX], writes=[bG[gi]])
                        if br == 0:
                            kv_extra(G[gi], bG[gi], 0, j * 128, 128, dstT=XCS[s], C=8, L=LC, dstV=None, bdst=bSCT)
                        else:
                            kv_extra(G[gi], bG[gi], 1, j * 128, 128, dstT=KSS[s], C=4, L=LS, dstV=VSS[s], bdst=bSCT)
                for j in range(4):
                    gi = nxt("g", 2)
                    em.dma("sp", lambda e, gi=gi, j=j: e.dma_start(out=G[gi][:, :], in_=win_in[s, j * 128:(j + 1) * 128, :]),
                           writes=[bG[gi]])
                    kv_extra(G[gi], bG[gi], 2, j * 128, 128, dstT=KWS[s], C=4, L=LW, dstV=VWS[s], bdst=bSCT)
                r0 = SEQ + 8 * s
                for (srcT, C, dT, npad, pos) in ((XC_T, 8, XCS[s], 8, 8192), (KS_T, 4, KSS[s], 120, 8192), (KW_T, 4, KWS[s], 120, 512)):
                    em.dma("sp", lambda e, srcT=srcT, C=C, dT=dT, pos=pos: e.dma_start(
                        out=dT.rearrange("d (c s) -> d c s", c=C)[:, :, pos:pos + 8],
                        in_=srcT.rearrange("d (c s) -> d c s", c=C)[:, :, r0:r0 + 8]), reads=[bKVS], writes=[bSCT], par=True)
                    for c in range(C):
                        em.dma("sp", lambda e, dT=dT, C=C, pos=pos, c=c, npad=npad: e.dma_start(
                            out=dT.rearrange("d (c s) -> d c s", c=C)[:, c, pos + 8:pos + 8 + npad], in_=ZR[0:64, 0:npad]),
                            reads=[bIDX], writes=[bSCT], par=True)
                for (sV, dV, pos) in ((VS_S, VSS[s], 8192), (VW_S, VWS[s], 512)):
                    em.dma("sp", lambda e, sV=sV, dV=dV, pos=pos: e.dma_start(out=dV[pos:pos + 8, :], in_=sV[r0:r0 + 8, :]),
                           reads=[bKVS], writes=[bSCT], par=True)
                    em.dma("sp", lambda e, dV=dV, pos=pos: e.dma_start(out=dV[pos + 8:pos + 128, :], in_=ZR[0:120, 0:260]),
                           reads=[bIDX], writes=[bSCT], par=True)
            for s in range(4):
                one_seq(s)

        def phase_kv(src, skey):
            for vi in range(2):
                em.op("pool", lambda e, vi=vi: e.memset(VA[vi][:, :, :], 1.0), writes=[bVA[vi]])
            load_gain(g_kv[0:1, :])
            load_w(0, w_kv, 8, 1536)
            load_x(src, skey, tiles[0], 0)
            for i, td in enumerate(tiles):
                slot = i % 2
                if i + 1 < len(tiles):
                    load_x(src, skey, tiles[i + 1], (i + 1) % 2)
                norm_T(td, slot, 0)
                for bi, (r0, nr) in enumerate(td.blocks):
                    for br in range(3):
                        ps, bps = ps_next()
                        mm_group(ps[0:nr, :], bps,
                                 [(HT[0][:, k, r0:r0 + nr], wa(0, k, 1536, br * 512, 512)) for k in range(8)],
                                 [bHT[0], bWA])
                        si = nxt("stg", 2)
                        em.op("act" if br != 1 else "dve",
                              (lambda e, ps=ps, si=si, nr=nr: e.copy(out=STG[si][0:nr, :], in_=ps[0:nr, :])) if br != 1 else
                              (lambda e, ps=ps, si=si, nr=nr: e.tensor_copy(out=STG[si][0:nr, :], in_=ps[0:nr, :])),
                              reads=[bps], writes=[bSTG[si]])
                        row = td.row0 + r0
                        kv_extra(ps, bps, br, row, nr)
                        dsts = []
                        if br == 0:
                            dsts.append(cmp_o[row:row + nr, :])
                        elif br == 1:
                            dsts.append(slc_o[row:row + nr, :])
                        else:
                            dsts.append(WINP[row:row + nr, :])
                            if (not td.sample) and row >= SEQ - 512:
                                dsts.append(winp_o[row - (SEQ - 512): row - (SEQ - 512) + nr, :])
                            if td.sample:
                                for s in range(4):
                                    dsts.append((wins_o[s, 504:512, :], s))
                        for dd in dsts:
                            if isinstance(dd, tuple):
                                d_, s = dd
                                em.dma("sp", lambda e, d_=d_, s=s, si=si: e.dma_start(out=d_, in_=STG[si][s * 8:(s + 1) * 8, :]),
                                       reads=[bSTG[si]], writes=[])
                            else:
                                em.dma("sp", lambda e, dd=dd, si=si, nr=nr: e.dma_start(out=dd, in_=STG[si][0:nr, :]),
                                       reads=[bSTG[si]], writes=[])
            for s in range(4):
                em.dma("sp", lambda e, s=s: e.dma_start(out=wins_o[s, 0:504, :], in_=win_in[s, 8:512, :]), writes=[])

        def phase_compress(jobs):
            bW1 = Buf("W1"); bXC = Buf("XCc"); bAK = Buf("ACTK"); bAV = Buf("ACTV"); bBI = Buf("BIAS")
            bCK = Buf("CKTt"); bCV = Buf("CVt"); bCS_ = Buf("cmpscratch")
            for kv in range(2):
                em.dma("pool", lambda e, kv=kv: e.dma_start(out=W1[:, kv, :, :], in_=w1_c[kv].rearrange("j d e -> d j e")),
                       writes=[bW1], par=(kv > 0))
                for two in range(2):
                    em.dma("pool", lambda e, kv=kv, two=two: e.dma_start(
                        out=W1B[two * 64:(two + 1) * 64, kv, :, :],
                        in_=w1_c[kv].rearrange("(c two) d e -> two d c e", two=2)[two]), writes=[bW1], par=True)
                    em.dma("pool", lambda e, kv=kv, two=two: e.dma_start(
                        out=PEV[two * 64:(two + 1) * 64, kv, :],
                        in_=pe_c[kv].rearrange("(c two) d -> two d c", two=2)[two], allow_slow_non_contiguous=True),
                        writes=[bW1], par=True)
                em.dma("pool", lambda e, kv=kv: e.dma_start(out=W2[:, kv, :], in_=w2_c[kv]), writes=[bW1], par=True)
            em.op("pool", lambda e: e.memset(ACTK[:, :, :], 0.0), writes=[bAK])
            em.op("pool", lambda e: e.memset(ACTV[:, :, :], 0.0), writes=[bAV])
            em.op("pool", lambda e: e.memset(CVt[:, :, :], 1.0), writes=[bCV])
            for kv in range(2):
                ps, bps = ps_next()
                mm_group(ps[:, 0:1], bps, [(W1B[:, kv, c, :], PEV[:, kv, c:c + 1]) for c in range(16)], [bW1])
                em.op("act", lambda e, ps=ps, kv=kv: e.copy(out=BIAS[:, kv:kv + 1], in_=ps[:, 0:1]), reads=[bps], writes=[bBI])

            def chunk(q, srcT, nq, dstK, nK, dstV, bsrc, all_full):
                nb = 64 if (all_full or q < nq - 1) else 63
                ncol = 1040 if (all_full or q < nq - 1) else 1024
                em.dma("sp", lambda e: e.dma_start(
                    out=XCc[:, :, 0:ncol], in_=srcT.rearrange("d (c s) -> d c s", c=8)[:, :, 1024 * q: 1024 * q + ncol]),
                    reads=[bsrc], writes=[bXC])
                for kv in range(2):
                    ps, bps = ps_next()
                    for kh in range(4):
                        mm_group(ps[:, kh * 64: kh * 64 + nb], bps,
                                 [(W1[:, kv, j, :], XCc[:, kv * 4 + kh, j: j + 16 * (nb - 1) + 1: 16]) for j in range(32)],
                                 [bW1, bXC])
                    hq = q % 2
                    if kv == 0:
                        em.op("act", lambda e, ps=ps: e.activation(
                            out=ACTK[:, :, 0:nb], in_=ps[:, 0:256].rearrange("p (h n) -> p h n", h=4)[:, :, 0:nb],
                            func=AF.Silu, bias=BIAS[:, 0:1]), reads=[bps, bBI], writes=[bAK])
                        ps2, bps2 = ps_next()
                        mm_group(ps2[0:64, 0:256], bps2, [(W2[:, 0, :], ACTK[:, :, :].rearrange("p h n -> p (h n)"))], [bW1, bAK])
                        em.op("dve", lambda e, ps2=ps2: e.tensor_copy(
                            out=CKTt[:, :, :], in_=ps2[0:64, 0:256].rearrange("p (h n) -> p h n", h=4)), reads=[bps2], writes=[bCK])
                        em.dma("sp", lambda e: e.dma_start(
                            out=dstK.rearrange("d (h n) -> d h n", h=4)[:, :, q * 64:(q + 1) * 64], in_=CKTt[:, :, :]),
                            reads=[bCK], writes=[bCS_], par=True)
                    else:
                        em.op("act", lambda e, ps=ps: e.activation(
                            out=ACTV[:, :, hq * 64: hq * 64 + nb], in_=ps[:, 0:256].rearrange("p (h n) -> p h n", h=4)[:, :, 0:nb],
                            func=AF.Silu, bias=BIAS[:, 1:2]), reads=[bps, bBI], writes=[bAV])
                        ps3, bps3 = ps_next()
                        for kh in range(4):
                            mm_group(ps3[:, kh * 64:(kh + 1) * 64], bps3, [(ACTV[:, kh, :], W2[:, 1, :])], [bW1, bAV])
                        em.op("dve", lambda e, ps3=ps3: e.tensor_copy(
                            out=CVt[hq * 64:(hq + 1) * 64, :, 0:64],
                            in_=ps3[hq * 64:(hq + 1) * 64, 0:256].rearrange("p (h d) -> p h d", h=4)), reads=[bps3], writes=[bCV])
                        em.dma("sp", lambda e: e.dma_start(
                            out=dstV[q * 64:(q + 1) * 64, :], in_=CVt[hq * 64:(hq + 1) * 64, :, :].rearrange("p h d -> p (h d)")),
                            reads=[bCV], writes=[bCS_], par=True)
            for (srcT, nq, dstK, nK, dstV, bsrc, all_full) in jobs:
                for q in range(nq):
                    chunk(q, srcT, nq, dstK, nK, dstV, bsrc, all_full)
            return bCS_

        W_INB, W_OB, KS_OFF = 0, 8 * 1060, 17408
        bVSB = Buf("VSB"); bKWB = Buf("KWB"); bVWB = Buf("VWB"); bCKB = Buf("CKB"); bQA = Buf("QA")
        bPB = [Buf("PB%d" % i) for i in range(3)]
        bYt = [Buf("Yt%d" % i) for i in range(4)]
        bNMX = Buf("NMX"); bG36 = Buf("G36"); bFRC = Buf("FRC"); bTK = Buf("topk"); bR9 = Buf("R9"); bQ1 = Buf("Q1")
        rr["pb"] = 0
        rr["sc"] = 0

        def sc_next():
            i = 4 + nxt("sc", 2)
            return PS[i], bPS[i]

        rr["sc3"] = 0

        def sc_next3():
            i = 3 + nxt("sc3", 3)
            return PS[i], bPS[i]

        def nsa_block(td, kh, bi):
            tg = td.idx * 4 + bi
            t0 = tg * 128
            row0 = td.row0
            oc, os_, ow = PS[0], PS[1], PS[2]
            psI = PS[3]
            for pz, bz in ((oc, bPS[0]), (os_, bPS[1]), (ow, bPS[2]), (psI, bPS[3])):
                em.op("dve", lambda e, pz=pz: e.memset(pz[:, 0:195], 0.0), writes=[bz])
            qa = QA[0:70, :, bi * 128:(bi + 1) * 128]

            def exp_to_pb(sc, bsc):
                pi = nxt("pb", 3)
                em.op("act", lambda e: e.activation(out=PB[pi][:, :], in_=sc[:, 0:384], func=AF.Exp),
                      reads=[bsc], writes=[bPB[pi]])
                return pi

            def select(pi, base, cm, tstep):
                em.op("pool", lambda e: e.affine_select(
                    out=PB[pi][:, :].rearrange("p (h t) -> p h t", h=3), in_=PB[pi][:, :].rearrange("p (h t) -> p h t", h=3),
                    pattern=[[0, 3], [tstep, 128]], compare_op=ALU.is_ge, fill=0.0, base=base, channel_multiplier=cm),
                    reads=[bPB[pi]], writes=[bPB[pi]])

            def pv(pi, acc, bacc, v_ap, vb):
                for hi in range(3):
                    em.op("pe", lambda e, hi=hi: e.matmul(acc[:, hi * 65:(hi + 1) * 65], PB[pi][:, hi * 128:(hi + 1) * 128], v_ap,
                                                          start=False, stop=True, skip_group_check=True),
                          reads=[bPB[pi], vb], writes=[bacc], signal=(hi == 2))

            def pipeline(items, stage_a, stage_b):
                prev = None
                for it in items:
                    pi = stage_a(it)
                    if prev is not None:
                        stage_b(*prev)
                    prev = (it, pi)
                if prev is not None:
                    stage_b(*prev)

            def cmp_a(nt):
                sc, bsc = sc_next()
                em.op("pe", lambda e: e.matmul(sc[:, 0:384], CKB[0:70, kh, nt * 128:(nt + 1) * 128], qa,
                                               start=True, stop=True), reads=[bCKB, bQA], writes=[bsc])
                pi = exp_to_pb(sc, bsc)
                select(pi, t0 - 16 * nt * 128 - 31, -16, 1)
                return pi

            def cmp_b(nt, pi):
                pv(pi, oc, bPS[0], CVB[:, nt, kh * 65:(kh + 1) * 65], bCKB)
                for hi in range(3):
                    em.op("pe", lambda e, hi=hi: e.matmul(
                        psI[:, hi * 64 + nt * 32: hi * 64 + nt * 32 + 32], PB[pi][:, hi * 128:(hi + 1) * 128], GM[:, :],
                        start=False, stop=True, skip_group_check=True), reads=[bPB[pi], bC], writes=[bPS[3]], signal=(hi == 2))
            pipeline([nt for nt in range(2) if 16 * (nt * 128) + 31 <= t0 + 127], cmp_a, cmp_b)
            def den_recip(acc, bacc, br):
                em.op("dve", lambda e: e.tensor_scalar(
                    out=R9[:, br:9:3], in0=acc[:, 0:195].rearrange("p (h c) -> p h c", c=65)[:, :, 64],
                    scalar1=1e-30, scalar2=None, op0=ALU.max), reads=[bacc], writes=[bR9])
            den_recip(oc, bPS[0], 0)
            em.op("dve", lambda e: e.reciprocal(out=RC[:, 0:3], in_=R9[:, 0:9:3]), reads=[bR9], writes=[bTK])
            em.op("dve", lambda e: e.tensor_scalar(out=IMP[:, :], in0=psI[:, 0:64], scalar1=RC[:, 0:1], scalar2=None, op0=ALU.mult),
                  reads=[bPS[3], bTK], writes=[bTK])
            for hi in (1, 2):
                em.op("dve", lambda e, hi=hi: e.scalar_tensor_tensor(
                    out=IMP[:, :], in0=psI[:, hi * 64:(hi + 1) * 64], scalar=RC[:, hi:hi + 1], in1=IMP[:, :],
                    op0=ALU.mult, op1=ALU.add), reads=[bPS[3], bTK], writes=[bTK])
            em.op("dve", lambda e: e.tensor_tensor(out=SCR[:, :], in0=IMP[:, :], in1=FRC[:, bi, :], op=ALU.max),
                  reads=[bTK, bFRC], writes=[bTK])
            em.op("dve", lambda e: e.max(out=M8[:, 0:8], in_=SCR[:, :]), reads=[bTK], writes=[bTK])
            em.op("dve", lambda e: e.match_replace(out=SC2[:, :], in_to_replace=M8[:, 0:8], in_values=SCR[:, :], imm_value=-1e30),
                  reads=[bTK], writes=[bTK])
            em.op("dve", lambda e: e.max(out=M8[:, 8:16], in_=SC2[:, :]), reads=[bTK], writes=[bTK])
            em.op("dve", lambda e: e.tensor_reduce(out=M8[:, 16:17], in_=M8[:, 8:16], axis=AX.X, op=ALU.min), reads=[bTK], writes=[bTK])
            em.op("dve", lambda e: e.tensor_scalar(out=NMt[:, :], in0=SCR[:, :], scalar1=M8[:, 16:17], scalar2=None, op0=ALU.is_ge),
                  reads=[bTK], writes=[bTK])
            em.op("dve", lambda e: e.tensor_scalar(out=NMt[:, :], in0=NMt[:, :], scalar1=30000.0, scalar2=-30000.0,
                                                   op0=ALU.mult, op1=ALU.add), reads=[bTK], writes=[bTK])
            nblk = 2 * (tg + 1)
            em.op("dve", lambda e: e.tensor_copy(
                out=NMX[:, 0:nblk * 64].rearrange("p (j r) -> p j r", r=64),
                in_=NMt[:, 0:nblk].unsqueeze(2).to_broadcast([128, nblk, 64])), reads=[bTK], writes=[bNMX])
            wlo = row0 - 512

            def sw_a(it):
                br, kb = it
                sc, bsc = sc_next3()
                if br == "w":
                    c0 = kb * 128 - wlo
                    em.op("pe", lambda e: e.matmul(sc[:, 0:384], KWB[0:70, kh, c0:c0 + 128], qa, start=True, stop=True),
                          reads=[bKWB, bQA], writes=[bsc])
                    pi = exp_to_pb(sc, bsc)
                    if kb == tg:
                        select(pi, 0, -1, 1)
                    if kb == tg - 4:
                        select(pi, 0, 1, -1)
                else:
                    em.op("pe", lambda e: e.matmul(
                        sc[:, 0:384], WA[0:70, KS_OFF + kh * 4096 + kb * 128: KS_OFF + kh * 4096 + (kb + 1) * 128], qa,
                        start=True, stop=False), reads=[bWA, bQA], writes=[bsc], signal=False)
                    for hi in range(3):
                        em.op("pe", lambda e, hi=hi: e.matmul(
                            sc[:, hi * 128:(hi + 1) * 128], NMX[:, kb * 128:(kb + 1) * 128], IDB[:, :],
                            start=False, stop=(hi == 2), skip_group_check=True), reads=[bNMX, bC], writes=[bsc], signal=(hi == 2))
                    pi = exp_to_pb(sc, bsc)
                    if kb == tg:
                        select(pi, 0, -1, 1)
                return pi

            def sw_b(it, pi):
                br, kb = it
                if br == "w":
                    c0 = kb * 128 - wlo
                    pv(pi, ow, bPS[2], VWB[:, c0 // 128, kh * 65:(kh + 1) * 65], bVWB)
                else:
                    pv(pi, os_, bPS[1], VSB[:, kb, kh * 65:(kh + 1) * 65], bVSB)
            def pipeline2(items, stage_a, stage_b):
                q = []
                for it in items:
                    q.append((it, stage_a(it)))
                    if len(q) > 2:
                        stage_b(*q.pop(0))
                for x in q:
                    stage_b(*x)
            pipeline2([("w", kb) for kb in range(max(0, tg - 4), tg + 1)] + [("s", kb) for kb in range(tg + 1)], sw_a, sw_b)
            den_recip(os_, bPS[1], 1)
            den_recip(ow, bPS[2], 2)
            em.op("dve", lambda e: e.reciprocal(out=R9[:, :], in_=R9[:, :]), reads=[bR9], writes=[bR9])
            em.op("dve", lambda e: e.tensor_tensor(out=R9[:, :], in0=R9[:, :], in1=G36[:, bi, kh * 9:(kh + 1) * 9], op=ALU.mult),
                  reads=[bR9, bG36], writes=[bR9])
            for hi in range(3):
                h = 3 * kh + hi
                em.op("dve", lambda e, hi=hi: e.tensor_scalar(
                    out=TY[:, :], in0=oc[:, hi * 65: hi * 65 + 64], scalar1=R9[:, 3 * hi:3 * hi + 1], scalar2=None, op0=ALU.mult),
                    reads=[bPS[0], bR9], writes=[bTK])
                em.op("dve", lambda e, hi=hi: e.scalar_tensor_tensor(
                    out=TY[:, :], in0=os_[:, hi * 65: hi * 65 + 64], scalar=R9[:, 3 * hi + 1:3 * hi + 2], in1=TY[:, :],
                    op0=ALU.mult, op1=ALU.add), reads=[bPS[1], bR9, bTK], writes=[bTK])
                em.op("dve", lambda e, hi=hi, h=h: e.scalar_tensor_tensor(
                    out=Yt[:, bi, h * 64:(h + 1) * 64], in0=ow[:, hi * 65: hi * 65 + 64], scalar=R9[:, 3 * hi + 2:3 * hi + 3],
                    in1=TY[:, :], op0=ALU.mult, op1=ALU.add), reads=[bPS[2], bR9, bTK], writes=[bYt[bi]])

        def mixer_b_tile(td, l, xslot):
            TT = td.T
            norm_T(td, xslot, 0)
            hT = HT[0]
            qaps = []
            for hp in range(2):
                ps, bps = ps_next()
                mm_group(ps[:, 0:TT], bps, [(wa(W_INB, k, 1060, 804 + hp * 128, 128), hT[:, k, 0:TT]) for k in range(8)],
                         [bHT[0], bWA])
                qap = (BA[:, 10 * T: 10 * T + TT] if hp == 0 else Q1[:, 0:TT])
                qb = bBA[10] if hp == 0 else bQ1
                em.op("act", lambda e, ps=ps, qap=qap: e.copy(out=qap, in_=ps[:, 0:TT]), reads=[bps], writes=[qb])
                qaps.append((qap, qb))
            mem_attention(td, l, [qaps[0][0], qaps[1][0]], [qaps[0][1], qaps[1][1]])
            if td.sample:
                for j in range(6):
                    em.op("pool", lambda e, j=j: e.memset(BA[:, j * T: j * T + TT], 0.0), writes=[bBA[j]])
            else:
                row0 = td.row0
                for bi, (r0, nr) in enumerate(td.blocks):
                    ps, bps = ps_next()
                    mm_group(ps[0:nr, 0:36], bps, [(hT[:, k, r0:r0 + nr], wa(W_INB, k, 1060, 768, 36)) for k in range(8)],
                             [bHT[0], bWA])
                    em.op("act", lambda e, ps=ps, bi=bi, nr=nr: e.activation(out=G36[0:nr, bi, :], in_=ps[0:nr, 0:36], func=AF.Sigmoid),
                          reads=[bps], writes=[bG36])
                em.dma("sp", lambda e: e.dma_start(out=FRC[:, :, :], in_=frc_t[row0:row0 + T, :].rearrange("(b p) j -> p b j", p=128)),
                       writes=[bFRC])
                lo = max(0, row0 - 512)
                off = lo - (row0 - 512)
                nkeys = row0 + T - lo
                em.dma("sp", lambda e: e.dma_start(
                    out=KWB[0:64, :, off:off + nkeys], in_=KW_T.rearrange("d (h s) -> d h s", h=4)[:, :, lo:lo + nkeys]),
                    reads=[bKVS], writes=[bKWB])
                for kh in range(4):
                    em.dma("pool", lambda e, kh=kh: e.dma_start(out=KWB[64:70, kh, off:off + nkeys], in_=kaug_t[:, lo:lo + nkeys]),
                           writes=[bKWB], par=True)
                em.dma("sp", lambda e: e.dma_start(
                    out=VWB[:, off // 128: off // 128 + nkeys // 128, :],
                    in_=VW_S[lo:lo + nkeys, :].rearrange("(kb p) c -> p kb c", p=128)), reads=[bKVS], writes=[bVWB])
                for kh in range(4):
                    for hi in range(3):
                        h = 3 * kh + hi
                        ps, bps = sc_next()
                        mm_group(ps[0:64, 0:TT], bps, [(wa(W_INB, k, 1060, h * 64, 64), hT[:, k, 0:TT]) for k in range(8)],
                                 [bHT[0], bWA])
                        em.op("act", lambda e, ps=ps, hi=hi: e.mul(out=QA[0:64, hi, 0:TT], in_=ps[0:64, 0:TT], mul=0.125),
                              reads=[bps], writes=[bQA])
                    em.dma("pool", lambda e, kh=kh: e.dma_start(
                        out=QA[64:70, :, 0:TT], in_=qaug_t[3 * kh:3 * kh + 3, :, row0:row0 + TT].rearrange("h r t -> r h t")),
                        writes=[bQA], par=True)
                    for bi in range(4):
                        nsa_block(td, kh, bi)
                for kk in range(3):
                    pt, bpt = pt_next()
                    for k2 in range(2):
                        j = kk * 2 + k2
                        for bi in range(4):
                            em.op("pe", lambda e, j=j, k2=k2, bi=bi, pt=pt: e.transpose(
                                pt[:, k2 * 512 + bi * 128: k2 * 512 + (bi + 1) * 128], Yt[:, bi, j * 128:(j + 1) * 128], IDB[:, :]),
                                reads=[bYt[bi], bC], writes=[bpt], signal=(k2 == 1 and bi == 3))
                    em.op("act", lambda e, kk=kk, pt=pt: e.copy(
                        out=BA[:, 2 * kk * T:(2 * kk + 2) * T].rearrange("p (a t) -> p a t", a=2),
                        in_=pt[:, :].rearrange("p (a t) -> p a t", a=2)), reads=[bpt], writes=[bBA[2 * kk], bBA[2 * kk + 1]])
            for bi, (r0, nr) in enumerate(td.blocks):
                for half in range(2):
                    ps, bps = ps_next()
                    mm_group(ps[0:nr, :], bps,
                             [(BA[:, k * T + r0: k * T + r0 + nr], wa(W_OB, k, 1024, half * 512, 512)) for k in range(8)],
                             [bBA[k] for k in range(8)] + [bWA])
                    em.op("dve", lambda e, ps=ps, bi=bi, nr=nr, half=half: e.tensor_tensor(
                        out=XT[xslot][0:nr, bi, half * 512:(half + 1) * 512], in0=ps[0:nr, :],
                        in1=XT[xslot][0:nr, bi, half * 512:(half + 1) * 512], op=ALU.add),
                        reads=[bps, bXT[xslot]], writes=[bXT[xslot]])

        def phase_mixer_b(l, src, skey, dst, dkey, bCS_):
            load_gain(g_mix[l:l + 1, :])
            load_w(W_INB, w_in_b[l - 2], 8, 1060)
            load_w(W_OB, w_o[l], 8, 1024, first=False)
            em.dma("sp", lambda e: e.dma_start(
                out=WA[0:64, KS_OFF:KS_OFF + 16384].rearrange("p (h s) -> p h s", h=4),
                in_=KS_T.rearrange("d (h s) -> d h s", h=4)[:, :, 0:SEQ]), reads=[bKVS], writes=[bWA], par=True)
            for kh in range(4):
                em.dma("pool", lambda e, kh=kh: e.dma_start(
                    out=WA[64:70, KS_OFF + kh * 4096: KS_OFF + (kh + 1) * 4096], in_=kaug_t[:, :]), writes=[bWA], par=True)
            em.dma("sp", lambda e: e.dma_start(out=VSB[:, :, :], in_=VS_S[0:SEQ, :].rearrange("(kb p) c -> p kb c", p=128)),
                   reads=[bKVS], writes=[bVSB])
            em.dma("sp", lambda e: e.dma_start(out=CKB[0:64, :, :], in_=CKT_S.rearrange("d (h n) -> d h n", h=4)),
                   reads=[bCS_], writes=[bCKB])
            for kh in range(4):
                em.dma("pool", lambda e, kh=kh: e.dma_start(out=CKB[64:70, kh, :], in_=kaug_c[:, :]), writes=[bCKB], par=True)
            em.dma("sp", lambda e: e.dma_start(out=CVB[:, :, :], in_=CV_S.rearrange("(nt p) c -> p nt c", p=128)),
                   reads=[bCS_], writes=[bCKB], par=True)
            em.dma("pool", lambda e: e.dma_start(out=GM[:, :], in_=gm_t[:, :]), writes=[bC])
            load_sample_mem(l)
            for i, td in enumerate(tiles):
                if td.sample and SAMPLE_NSA:
                    continue
                load_x(src, skey, td, 0)
                mixer_b_tile(td, l, 0)
                store_x(dst, dkey, td, 0)

        def phase_mixer_b_sample(l, src, skey, dst, dkey):
            bCSs = bCSs_box[0]
            td = tiles[NT]
            TT = NS
            bQAs = Buf("QAs"); bG36s = Buf("G36s"); bFRs = Buf("FRCs"); bCKs = Buf("CKs"); bKWs = Buf("KWs")
            bKSc = [Buf("KSc0"), Buf("KSc1")]; bVSc = [Buf("VSc0"), Buf("VSc1")]
            bPBs = [Buf("PBs%d" % i) for i in range(3)]; bNX = [Buf("NX0"), Buf("NX1")]
            bTKs = Buf("tks"); bO = Buf("Osb"); bYs = Buf("Yts"); bQ1s = Buf("Q1S"); bR9s = Buf("R9s")
            st_ = {"pb": 0, "nx": 0, "kc": 0}

            def rot(k, n):
                i = st_[k]
                st_[k] = (i + 1) % n
                return i
            load_x(src, skey, td, 0)
            norm_T(td, 0, 0)
            hT = HT[0]
            em.dma("pool", lambda e: e.dma_start(out=GMs[:, :], in_=gm_t[:, :]), writes=[bTKs])
            em.dma("sp", lambda e: e.dma_start(out=FRCs[:, :], in_=frc_s[:, :]), writes=[bFRs])
            qaps = []
            for hp in range(2):
                ps, bps = ps_next()
                mm_group(ps[:, 0:TT], bps, [(wa(W_INB, k, 1060, 804 + hp * 128, 128), hT[:, k, 0:TT]) for k in range(8)],
                         [bHT[0], bWA])
                qap = (BA[:, 10 * T: 10 * T + TT] if hp == 0 else Q1S[:, 0:TT])
                qb = bBA[10] if hp == 0 else bQ1s
                em.op("act", lambda e, ps=ps, qap=qap: e.copy(out=qap, in_=ps[:, 0:TT]), reads=[bps], writes=[qb])
                qaps.append((qap, qb))
            mem_attention(td, l, [qaps[0][0], qaps[1][0]], [qaps[0][1], qaps[1][1]])
            for h in range(12):
                ps, bps = sc_next()
                mm_group(ps[0:64, 0:TT], bps, [(wa(W_INB, k, 1060, h * 64, 64), hT[:, k, 0:TT]) for k in range(8)], [bHT[0], bWA])
                em.op("act", lambda e, ps=ps, h=h: e.mul(out=QAs[0:64, h, :], in_=ps[0:64, 0:TT], mul=0.125), reads=[bps], writes=[bQAs])
            em.dma("pool", lambda e: e.dma_start(out=QAs[64:70, :, :], in_=qaug_s.rearrange("r (h t) -> r h t", h=12)),
                   writes=[bQAs], par=True)
            for s_ in range(4):
                ps, bps = ps_next()
                mm_group(ps[0:8, 0:36], bps, [(hT[:, k, s_ * 8:(s_ + 1) * 8], wa(W_INB, k, 1060, 768, 36)) for k in range(8)],
                         [bHT[0], bWA])
                em.op("act", lambda e, ps=ps, s_=s_: e.activation(out=G36s[0:8, s_, :], in_=ps[0:8, 0:36], func=AF.Sigmoid),
                      reads=[bps], writes=[bG36s])

            def pipeline(items, stage_a, stage_b):
                prev = None
                for it in items:
                    pi = stage_a(it)
                    if prev is not None:
                        stage_b(*prev)
                    prev = (it, pi)
                if prev is not None:
                    stage_b(*prev)

            def exp_pb(sc, bsc):
                pi = rot("pb", 3)
                em.op("act", lambda e: e.activation(out=PBs[pi][:, :], in_=sc[:, 0:24], func=AF.Exp), reads=[bsc], writes=[bPBs[pi]])
                return pi

            def select(pi, base, cm, tstep):
                em.op("pool", lambda e: e.affine_select(
                    out=PBs[pi][:, :].rearrange("p (h t) -> p h t", h=3), in_=PBs[pi][:, :].rearrange("p (h t) -> p h t", h=3),
                    pattern=[[0, 3], [tstep, 8]], compare_op=ALU.is_ge, fill=0.0, base=base, channel_multiplier=cm),
                    reads=[bPBs[pi]], writes=[bPBs[pi]])

            def pv(pi, acc, bacc, col0, v_ap, vb):
                for hi in range(3):
                    em.op("pe", lambda e, hi=hi: e.matmul(acc[0:8, col0 + hi * 65: col0 + (hi + 1) * 65], PBs[pi][:, hi * 8:(hi + 1) * 8],
                                                          v_ap, start=False, stop=True, skip_group_check=True),
                          reads=[bPBs[pi], vb], writes=[bacc], signal=(hi == 2))

            def one_seq(s_):
                def qa(kh):
                    return QAs[0:70, 3 * kh:3 * kh + 3, s_ * 8:(s_ + 1) * 8]
                em.dma("sp", lambda e: e.dma_start(out=CKs[0:64, :, :], in_=CKS[s_].rearrange("d (h n) -> d h n", h=4)),
                       reads=[bCSs], writes=[bCKs])
                for kh in range(4):
                    em.dma("pool", lambda e, kh=kh: e.dma_start(out=CKs[64:70, kh, :], in_=kaug_cs[:, :]), writes=[bCKs], par=True)
                em.dma("sp", lambda e: e.dma_start(out=CVs[:, :, :], in_=CVS[s_].rearrange("(nt p) c -> p nt c", p=128)),
                       reads=[bCSs], writes=[bCKs], par=True)
                em.dma("sp", lambda e: e.dma_start(out=KWs[0:64, :, :], in_=KWS[s_].rearrange("d (h n) -> d h n", h=4)),
                       reads=[bSCT], writes=[bKWs])
                for kh in range(4):
                    em.dma("pool", lambda e, kh=kh: e.dma_start(out=KWs[64:70, kh, :], in_=kaug_ws[:, :]), writes=[bKWs], par=True)
                em.dma("sp", lambda e: e.dma_start(out=VWs[:, :, :], in_=VWS[s_].rearrange("(nt p) c -> p nt c", p=128)),
                       reads=[bSCT], writes=[bKWs], par=True)
                for kh in range(4):
                    oc, psI = PS[0], PS[3]
                    em.op("dve", lambda e: e.memset(oc[0:8, 0:195], 0.0), writes=[bPS[0]])
                    em.op("dve", lambda e: e.memset(psI[0:8, 0:384], 0.0), writes=[bPS[3]])
                    def c_a(nt, kh=kh):
                        sc, bsc = sc_next()
                        em.op("pe", lambda e: e.matmul(sc[:, 0:24], CKs[0:70, kh, nt * 128:(nt + 1) * 128], qa(kh),
                                                       start=True, stop=True), reads=[bCKs, bQAs], writes=[bsc])
                        pi = exp_pb(sc, bsc)
                        if nt == 3:
                            select(pi, 510 - 384, -1, 0)
                        return pi

                    def c_b(nt, pi, kh=kh, oc=oc, psI=psI):
                        pv(pi, oc, bPS[0], 0, CVs[:, nt, kh * 65:(kh + 1) * 65], bCKs)
                        for hi in range(3):
                            em.op("pe", lambda e, hi=hi: e.matmul(
                                psI[0:8, hi * 128 + nt * 32: hi * 128 + nt * 32 + 32], PBs[pi][:, hi * 8:(hi + 1) * 8], GMs[:, :],
                                start=False, stop=True, skip_group_check=True), reads=[bPBs[pi], bTKs], writes=[bPS[3]], signal=(hi == 2))
                    pipeline(list(range(4)), c_a, c_b)
                    em.op("act", lambda e, kh=kh: e.copy(out=OCs[0:8, kh, :], in_=oc[0:8, 0:195]), reads=[bPS[0]], writes=[bO])
                    em.op("dve", lambda e, kh=kh: e.tensor_scalar(
                        out=RCs[0:8, 0:3], in0=OCs[0:8, kh, :].rearrange("p (h c) -> p h c", c=65)[:, :, 64], scalar1=1e-30, scalar2=None,
                        op0=ALU.max), reads=[bO], writes=[bTKs])
                    em.op("dve", lambda e: e.reciprocal(out=RCs[0:8, 0:3], in_=RCs[0:8, 0:3]), reads=[bTKs], writes=[bTKs])
                    em.op("dve", lambda e: e.tensor_scalar(out=IMPs[0:8, :], in0=psI[0:8, 0:128], scalar1=RCs[0:8, 0:1], scalar2=None,
                                                           op0=ALU.mult), reads=[bPS[3], bTKs], writes=[bTKs])
                    for hi in (1, 2):
                        em.op("dve", lambda e, hi=hi: e.scalar_tensor_tensor(
                            out=IMPs[0:8, :], in0=psI[0:8, hi * 128:(hi + 1) * 128], scalar=RCs[0:8, hi:hi + 1], in1=IMPs[0:8, :],
                            op0=ALU.mult, op1=ALU.add), reads=[bPS[3], bTKs], writes=[bTKs])
                    em.op("dve", lambda e: e.tensor_copy(out=SCRs[0:8, :], in_=FRCs[0:8, :]), reads=[bFRs], writes=[bTKs])
                    em.op("dve", lambda e: e.tensor_tensor(out=SCRs[0:8, 0:128], in0=IMPs[0:8, :], in1=FRCs[0:8, 0:128], op=ALU.max),
                          reads=[bTKs, bFRs], writes=[bTKs])
                    em.op("dve", lambda e: e.max(out=M8s[0:8, 0:8], in_=SCRs[0:8, :]), reads=[bTKs], writes=[bTKs])
                    em.op("dve", lambda e: e.match_replace(out=SC2s[0:8, :], in_to_replace=M8s[0:8, 0:8], in_values=SCRs[0:8, :],
                                                           imm_value=-1e30), reads=[bTKs], writes=[bTKs])
                    em.op("dve", lambda e: e.max(out=M8s[0:8, 8:16], in_=SC2s[0:8, :]), reads=[bTKs], writes=[bTKs])
                    em.op("dve", lambda e: e.tensor_reduce(out=M8s[0:8, 16:17], in_=M8s[0:8, 8:16], axis=AX.X, op=ALU.min),
                          reads=[bTKs], writes=[bTKs])
                    em.op("dve", lambda e, kh=kh: e.tensor_scalar(out=NMts[0:8, kh, :], in0=SCRs[0:8, :], scalar1=M8s[0:8, 16:17],
                                                                  scalar2=None, op0=ALU.is_ge), reads=[bTKs], writes=[bTKs])
                    em.op("dve", lambda e, kh=kh: e.tensor_scalar(out=NMts[0:8, kh, :], in0=NMts[0:8, kh, :], scalar1=30000.0,
                                                                  scalar2=-30000.0, op0=ALU.mult, op1=ALU.add), reads=[bTKs], writes=[bTKs])
                em.op("dve", lambda e: e.memset(PS[0][0:8, 0:390], 0.0), writes=[bPS[0]])
                em.op("dve", lambda e: e.memset(PS[1][0:8, 0:390], 0.0), writes=[bPS[1]])
                def s_loads(ch):
                    nk = 512 if ch < 16 else 128
                    ci = ch % 2
                    em.dma("sp", lambda e: e.dma_start(
                        out=KSc[ci][0:64, :, 0:nk], in_=KSS[s_].rearrange("d (h n) -> d h n", h=4)[:, :, ch * 512: ch * 512 + nk]),
                        reads=[bSCT], writes=[bKSc[ci]])
                    for kh in range(4):
                        em.dma("pool", lambda e, kh=kh: e.dma_start(
                            out=KSc[ci][64:70, kh, 0:nk], in_=kaug_s[:, ch * 512: ch * 512 + nk]), writes=[bKSc[ci]], par=True)
                    em.dma("sp", lambda e: e.dma_start(
                        out=VSc[ci][:, 0:nk // 128, :], in_=VSS[s_][ch * 512: ch * 512 + nk, :].rearrange("(kb p) c -> p kb c", p=128)),
                        reads=[bSCT], writes=[bVSc[ci]])

                def s_a(it):
                    kb, kbl, kh, ci = it
                    xi = rot("nx", 2)
                    em.op("dve", lambda e: e.tensor_copy(
                        out=NMXs[xi][0:8, :].rearrange("p (j r) -> p j r", r=64),
                        in_=NMts[0:8, kh, 2 * kb:2 * kb + 2].unsqueeze(2).to_broadcast([8, 2, 64])),
                        reads=[bTKs], writes=[bNX[xi]])
                    sc, bsc = sc_next()
                    em.op("pe", lambda e: e.matmul(
                        sc[:, 0:24], KSc[ci][0:70, kh, kbl * 128:(kbl + 1) * 128], qa(kh), start=True, stop=False),
                        reads=[bKSc[ci], bQAs], writes=[bsc], signal=False)
                    for hi in range(3):
                        em.op("pe", lambda e, hi=hi: e.matmul(
                            sc[:, hi * 8:(hi + 1) * 8], NMXs[xi][0:8, :], IDB[0:8, 0:8], start=False, stop=(hi == 2),
                            skip_group_check=True), reads=[bNX[xi], bC], writes=[bsc], signal=(hi == 2))
                    pi = exp_pb(sc, bsc)
                    if kb == 64:
                        select(pi, 0, -1, 1)
                    return pi

                def s_b(it, pi):
                    kb, kbl, kh, ci = it
                    acc, bacc = (PS[0], bPS[0]) if kh < 2 else (PS[1], bPS[1])
                    pv(pi, acc, bacc, (kh % 2) * 195, VSc[ci][:, kbl, kh * 65:(kh + 1) * 65], bVSc[ci])
                prev = None
                s_loads(0)
                for ch in range(17):
                    if prev is not None:
                        s_b(*prev)
                        prev = None
                    if ch + 1 < 17:
                        s_loads(ch + 1)
                    nk = 512 if ch < 16 else 128
                    for kbl in range(nk // 128):
                        for kh in range(4):
                            it = (ch * 4 + kbl, kbl, kh, ch % 2)
                            pi = s_a(it)
                            if prev is not None:
                                s_b(*prev)
                            prev = (it, pi)
                if prev is not None:
                    s_b(*prev)
                em.op("act", lambda e: e.copy(out=OSs[0:8, 0:2, :], in_=PS[0][0:8, 0:390].rearrange("p (k c) -> p k c", k=2)),
                      reads=[bPS[0]], writes=[bO])
                em.op("act", lambda e: e.copy(out=OSs[0:8, 2:4, :], in_=PS[1][0:8, 0:390].rearrange("p (k c) -> p k c", k=2)),
                      reads=[bPS[1]], writes=[bO])
                em.op("dve", lambda e: e.memset(PS[2][0:8, 0:390], 0.0), writes=[bPS[2]])
                em.op("dve", lambda e: e.memset(PS[3][0:8, 0:390], 0.0), writes=[bPS[3]])
                def w_a(it):
                    kh, kb = it
                    sc, bsc = sc_next()
                    em.op("pe", lambda e: e.matmul(sc[:, 0:24], KWs[0:70, kh, kb * 128:(kb + 1) * 128], qa(kh),
                                                   start=True, stop=True), reads=[bKWs, bQAs], writes=[bsc])
                    pi = exp_pb(sc, bsc)
                    if kb == 0:
                        select(pi, 0, 1, -1)
                    if kb == 4:
                        select(pi, 0, -1, 1)
                    return pi

                def w_b(it, pi):
                    kh, kb = it
                    acc, bacc = (PS[2], bPS[2]) if kh < 2 else (PS[3], bPS[3])
                    pv(pi, acc, bacc, (kh % 2) * 195, VWs[:, kb, kh * 65:(kh + 1) * 65], bKWs)
                pipeline([(kh, kb) for kh in range(4) for kb in range(5)], w_a, w_b)
                em.op("act", lambda e: e.copy(out=OWs[0:8, 0:2, :], in_=PS[2][0:8, 0:390].rearrange("p (k c) -> p k c", k=2)),
                      reads=[bPS[2]], writes=[bO])
                em.op("act", lambda e: e.copy(out=OWs[0:8, 2:4, :], in_=PS[3][0:8, 0:390].rearrange("p (k c) -> p k c", k=2)),
                      reads=[bPS[3]], writes=[bO])
                for kh in range(4):
                    for br, Ob in enumerate((OCs, OSs, OWs)):
                        em.op("dve", lambda e, br=br, Ob=Ob, kh=kh: e.tensor_scalar(
                            out=R9s[0:8, br:9:3], in0=Ob[0:8, kh, :].rearrange("p (h c) -> p h c", c=65)[:, :, 64], scalar1=1e-30,
                            scalar2=None, op0=ALU.max), reads=[bO], writes=[bR9s])
                    em.op("dve", lambda e: e.reciprocal(out=R9s[0:8, :], in_=R9s[0:8, :]), reads=[bR9s], writes=[bR9s])
                    em.op("dve", lambda e, kh=kh: e.tensor_tensor(out=R9s[0:8, :], in0=R9s[0:8, :], in1=G36s[0:8, s_, kh * 9:(kh + 1) * 9],
                                                                  op=ALU.mult), reads=[bR9s, bG36s], writes=[bR9s])
                    for hi in range(3):
                        h = 3 * kh + hi
                        em.op("dve", lambda e, hi=hi, kh=kh: e.tensor_scalar(
                            out=TYs[0:8, :], in0=OCs[0:8, kh, hi * 65: hi * 65 + 64], scalar1=R9s[0:8, 3 * hi:3 * hi + 1], scalar2=None,
                            op0=ALU.mult), reads=[bO, bR9s], writes=[bTKs])
                        em.op("dve", lambda e, hi=hi, kh=kh: e.scalar_tensor_tensor(
                            out=TYs[0:8, :], in0=OSs[0:8, kh, hi * 65: hi * 65 + 64], scalar=R9s[0:8, 3 * hi + 1:3 * hi + 2],
                            in1=TYs[0:8, :], op0=ALU.mult, op1=ALU.add), reads=[bO, bR9s, bTKs], writes=[bTKs])
                        em.op("dve", lambda e, hi=hi, kh=kh, h=h: e.scalar_tensor_tensor(
                            out=Yts[0:8, s_, h * 64:(h + 1) * 64], in0=OWs[0:8, kh, hi * 65: hi * 65 + 64],
                            scalar=R9s[0:8, 3 * hi + 2:3 * hi + 3], in1=TYs[0:8, :], op0=ALU.mult, op1=ALU.add),
                            reads=[bO, bR9s, bTKs], writes=[bYs])
                pt, bpt = pt_next()
                for j in range(6):
                    em.op("pe", lambda e, j=j: e.transpose(pt[:, j * 8:(j + 1) * 8], Yts[0:8, s_, j * 128:(j + 1) * 128], IDB[0:8, 0:8]),
                          reads=[bYs, bC], writes=[bpt], signal=(j == 5))
                for j in range(6):
                    em.op("act", lambda e, j=j: e.copy(out=BA[:, j * T + s_ * 8: j * T + (s_ + 1) * 8], in_=pt[:, j * 8:(j + 1) * 8]),
                          reads=[bpt], writes=[bBA[j]])
            for s_ in range(4):
                one_seq(s_)
            for bi, (r0, nr) in enumerate(td.blocks):
                for half in range(2):
                    ps, bps = ps_next()
                    mm_group(ps[0:nr, :], bps,
                             [(BA[:, k * T + r0: k * T + r0 + nr], wa(W_OB, k, 1024, half * 512, 512)) for k in range(8)],
                             [bBA[k] for k in range(8)] + [bWA])
                    em.op("dve", lambda e, ps=ps, bi=bi, nr=nr, half=half: e.tensor_tensor(
                        out=XT[0][0:nr, bi, half * 512:(half + 1) * 512], in0=ps[0:nr, :],
                        in1=XT[0][0:nr, bi, half * 512:(half + 1) * 512], op=ALU.add),
                        reads=[bps, bXT[0]], writes=[bXT[0]])
            store_x(dst, dkey, td, 0)

        def phase_final(src, skey):
            load_gain(g_final[0:1, :])
            load_x(src, skey, tiles[0], 0)
            for i, td in enumerate(tiles):
                slot = i % 2
                if i + 1 < len(tiles):
                    load_x(src, skey, tiles[i + 1], (i + 1) % 2)
                xt = XT[slot]
                for bi, (r0, nr) in enumerate(td.blocks):
                    si = nxt("ss", 8)
                    em.op("pool", lambda e, si=si: e.memset(SS[:, si:si + 1], 0.0), writes=[bSS[si]])
                    em.op("act", lambda e, bi=bi, nr=nr, si=si, xt=xt: e.activation(
                        out=JK[0:nr, :], in_=xt[0:nr, bi, :], func=AF.Square, accum_out=SS[0:nr, si:si + 1]),
                        reads=[bXT[slot]], writes=[bJK, bSS[si]])
                    em.op("dve", lambda e, nr=nr, si=si: e.tensor_scalar(
                        out=SS[0:nr, si:si + 1], in0=SS[0:nr, si:si + 1], scalar1=1.0 / D, scalar2=EPS,
                        op0=ALU.mult, op1=ALU.add), reads=[bSS[si]], writes=[bSS[si]])
                    em.op("act", lambda e, nr=nr, si=si: e.sqrt(out=SS[0:nr, si:si + 1], in_=SS[0:nr, si:si + 1]),
                          reads=[bSS[si]], writes=[bSS[si]])
                    em.op("dve", lambda e, nr=nr, si=si: e.reciprocal(out=SS[0:nr, si:si + 1], in_=SS[0:nr, si:si + 1]),
                          reads=[bSS[si]], writes=[bSS[si]])
                    em.op("dve", lambda e, bi=bi, nr=nr, si=si, xt=xt: e.scalar_tensor_tensor(
                        out=xt[0:nr, bi, :], in0=xt[0:nr, bi, :], scalar=SS[0:nr, si:si + 1], in1=GB[0:nr, :],
                        op0=ALU.mult, op1=ALU.mult), reads=[bXT[slot], bSS[si], bGB], writes=[bXT[slot]])
                store_x(y_out, "y", td, slot)

        def run_all():
            nonlocal UT, ST8, SO8, CW, KB16, TS, VA, G, ZR, PTB, PTF, IDX, IOPF
            nonlocal W1, W1B, W2, PEV, BIAS, XCc, ACTK, ACTV, CKTt, CVt
            nonlocal QAs, Q1S, GMs, FRCs, G36s, CKs, CVs, KWs, VWs, KSc, VSc, PBs, NMXs, NMts, OCs, OSs, OWs, Yts, RCs, IMPs, SCRs, SC2s, M8s, R9s, TYs
            nonlocal VSB, KWB, VWB, CKB, CVB, QA, PB, Yt, NMX, G36, FRC, IMP, SCR, SC2, NMt, M8, R9, RC, TY, GM, Q1
            if stop_after == "consts":
                return
            with Scope() as sc:
                UT = sc.sb("UT", [128, 6, T + 2], F32)
                ST8 = sc.sb("ST8", [8, CONV], F32)
                SO8 = sc.sb("SO8", [8, CONV], F32)
                CW = sc.sb("CW", [128, 2, 6, 3], F32)
                KB16 = sc.sb("KB16", [128, 512], BF16)
                TS = sc.sb("TS", [64, 8, 128], BF16)
                VA = [sc.sb("VA%d" % i, [128, 4, 65], BF16) for i in range(2)]
                for l in range(2):
                    for k in range(3):
                        em.dma("sp", lambda e, l=l, k=k: e.dma_start(
                            out=CW[:, l, :, k], in_=conv_w[l, k, :].rearrange("(j p) -> p j", p=128),
                            allow_slow_non_contiguous=True), writes=[bC])
                phase_mem()
                if stop_after == "mem":
                    return
                cur, ckey = x_in, "in"
                for l in range(2):
                    phase_mixer_a(l, cur, ckey, XS[0], "s0")
                    if stop_after == "mixer%d" % l:
                        return
                    phase_ffn(l, 0, XS[0], "s0", None, None, XS[1], "s1")
                    phase_ffn(l, 1, None, None, XS[1], "s1", XS[2], "s2")
                    if stop_after == "ffnb%d" % l:
                        return
                    cur, ckey = XS[2], "s2"
                phase_kv(cur, ckey)
            if stop_after == "kv":
                return
            if SAMPLE_NSA:
                with Scope() as sc:
                    sc.sb("UTd", [128, 6, T + 2], F32)
                    sc.sb("ST8d", [8, CONV], F32)
                    sc.sb("SO8d", [8, CONV], F32)
                    sc.sb("CWd", [128, 2, 6, 3], F32)
                    a_old = [nc.lookup_mloc(KB16).addr, nc.lookup_mloc(TS).addr, nc.lookup_mloc(VA[0]).addr, nc.lookup_mloc(VA[1]).addr]
                    KB16 = sc.sb("KB16", [128, 512], BF16)
                    TS = sc.sb("TS", [64, 8, 128], BF16)
                    VA = [sc.sb("VA%d" % i, [128, 4, 65], BF16) for i in range(2)]
                    assert a_old == [nc.lookup_mloc(KB16).addr, nc.lookup_mloc(TS).addr, nc.lookup_mloc(VA[0]).addr, nc.lookup_mloc(VA[1]).addr]
                    G = [sc.sb("G%d" % i, [128, 512], F32) for i in range(2)]
                    ZR = sc.sb("ZR", [128, 260], BF16)
                    PTB = sc.sb("PTB", [128, 64], I32)
                    PTF = sc.sb("PTF", [128, 64], F32)
                    IDX = sc.sb("IDX", [128, 64], I32)
                    IOPF = sc.sb("IOPF", [128, 1], F32)
                    phase_sample_ctx()
                if stop_after == "sctx":
                    return
            with Scope(xt1=False) as sc:
                W1 = sc.sb("W1", [64, 2, 32, 128], BF16)
                W1B = sc.sb("W1B", [128, 2, 16, 128], BF16)
                W2 = sc.sb("W2", [128, 2, 64], BF16)
                PEV = sc.sb("PEV", [128, 2, 16], BF16)
                BIAS = sc.sb("BIAS", [128, 2], F32)
                XCc = sc.sb("XCc", [64, 8, 1040], BF16)
                ACTK = sc.sb("ACTK", [128, 4, 64], BF16)
                ACTV = sc.sb("ACTV", [128, 4, 128], BF16)
                CKTt = sc.sb("CKTt", [64, 4, 64], BF16)
                CVt = sc.sb("CVt", [128, 4, 65], BF16)
                jobs = [(XC_T, 4, CKT_S, 256, CV_S, bKVS, False)]
                if SAMPLE_NSA:
                    jobs += [(XCS[s_], 8, CKS[s_], 512, CVS[s_], bSCT, True) for s_ in range(4)]
                bCS_ = phase_compress(jobs)
                bCSs_box[0] = bCS_
            if stop_after == "cmp":
                return
            for l in (2, 3):
                with Scope(xt1=False) as sc:
                    VSB = sc.sb("VSB", [128, 32, 260], BF16)
                    KWB = sc.sb("KWB", [70, 4, 1024], BF16)
                    VWB = sc.sb("VWB", [128, 8, 260], BF16)
                    CKB = sc.sb("CKB", [70, 4, 256], BF16)
                    CVB = sc.sb("CVB", [128, 2, 260], BF16)
                    QA = sc.sb("QA", [70, 3, T], BF16)
                    PB = [sc.sb("PB%d" % i, [128, 384], BF16) for i in range(3)]
                    Yt = sc.sb("Yt", [128, 4, 768], BF16)
                    NMX = sc.sb("NMX", [128, 4096], BF16)
                    G36 = sc.sb("G36", [128, 4, 36], F32)
                    FRC = sc.sb("FRC", [128, 4, 64], F32)
                    IMP = sc.sb("IMP", [128, 64], F32)
                    SCR = sc.sb("SCR", [128, 64], F32)
                    SC2 = sc.sb("SC2", [128, 64], F32)
                    NMt = sc.sb("NMt", [128, 64], F32)
                    M8 = sc.sb("M8", [128, 24], F32)
                    R9 = sc.sb("R9", [128, 9], F32)
                    RC = sc.sb("RC", [128, 3], F32)
                    TY = sc.sb("TY", [128, 64], F32)
                    GM = sc.sb("GM", [128, 32], BF16)
                    Q1 = sc.sb("Q1", [128, T], BF16)
                    phase_mixer_b(l, cur, ckey, XS[0], "s0", bCS_)
                if stop_after == "mixer%d" % l:
                    return
                if SAMPLE_NSA:
                    with Scope(xt1=False) as sc:
                        QAs = sc.sb("QAs", [70, 12, NS], BF16)
                        Q1S = sc.sb("Q1S", [128, NS], BF16)
                        GMs = sc.sb("GMs", [128, 32], BF16)
                        FRCs = sc.sb("FRCs", [8, 130], F32)
                        G36s = sc.sb("G36s", [8, 4, 36], F32)
                        CKs = sc.sb("CKs", [70, 4, 512], BF16)
                        CVs = sc.sb("CVs", [128, 4, 260], BF16)
                        KWs = sc.sb("KWs", [70, 4, 640], BF16)
                        VWs = sc.sb("VWs", [128, 5, 260], BF16)
                        KSc = [sc.sb("KSc%d" % i, [70, 4, 512], BF16) for i in range(2)]
                        VSc = [sc.sb("VSc%d" % i, [128, 4, 260], BF16) for i in range(2)]
                        PBs = [sc.sb("PBs%d" % i, [128, 24], BF16) for i in range(3)]
                        NMXs = [sc.sb("NMXs%d" % i, [8, 128], BF16) for i in range(2)]
                        NMts = sc.sb("NMts", [8, 4, 130], F32)
                        OCs = sc.sb("OCs", [8, 4, 195], F32)
                        OSs = sc.sb("OSs", [8, 4, 195], F32)
                        OWs = sc.sb("OWs", [8, 4, 195], F32)
                        Yts = sc.sb("Yts", [8, 4, 768], BF16)
                        RCs = sc.sb("RCs", [8, 3], F32)
                        IMPs = sc.sb("IMPs", [8, 128], F32)
                        SCRs = sc.sb("SCRs", [8, 130], F32)
                        SC2s = sc.sb("SC2s", [8, 130], F32)
                        M8s = sc.sb("M8s", [8, 24], F32)
                        R9s = sc.sb("R9s", [8, 9], F32)
                        TYs = sc.sb("TYs", [8, 64], F32)
                        phase_mixer_b_sample(l, cur, ckey, XS[0], "s0")
                if stop_after == "smixer%d" % l:
                    return
                with Scope() as sc:
                    phase_ffn(l, 0, XS[0], "s0", None, None, XS[1], "s1")
                    phase_ffn(l, 1, None, None, XS[1], "s1", XS[2], "s2")
                cur, ckey = XS[2], "s2"
            with Scope(ht1=False) as sc:
                phase_final(cur, ckey)
        UT = ST8 = SO8 = CW = KB16 = TS = VA = G = ZR = PTB = PTF = IDX = IOPF = None
        W1 = W1B = W2 = PEV = BIAS = XCc = ACTK = ACTV = CKTt = CVt = None
        QAs = Q1S = GMs = FRCs = G36s = CKs = CVs = KWs = VWs = KSc = VSc = PBs = NMXs = NMts = OCs = OSs = OWs = Yts = None
        RCs = IMPs = SCRs = SC2s = M8s = R9s = TYs = None
        VSB = KWB = VWB = CKB = CVB = QA = PB = Yt = NMX = G36 = FRC = IMP = SCR = SC2 = NMt = M8 = R9 = RC = TY = GM = Q1 = None
        run_all()
        em.finish()
        em.emit()
    return nc


_CACHE = {}


def _trunc_bf16(x):
    x = np.ascontiguousarray(x, dtype=np.float32)
    return (x.view(np.uint32) & np.uint32(0xFFFF0000)).view(np.float32)


def _alibi_slopes(n):
    def pow2(m):
        start = 2.0 ** (-8.0 / m)
        return [start ** (i + 1) for i in range(m)]
    if n & (n - 1) == 0:
        return pow2(n)
    c = 2 ** int(np.floor(np.log2(n)))
    return pow2(c) + _alibi_slopes(2 * c)[0::2][: n - c]


def _const_tables():
    def key_rows(pos):
        a = (pos // 64) * 64
        b = pos % 64
        one = np.ones_like(pos)
        return np.stack([a, a, b, b, one, one]).astype(np.float32)
    slopes = np.array(_alibi_slopes(12), dtype=np.float32)
    sh = _trunc_bf16(slopes)
    sl = _trunc_bf16(slopes - sh)
    t = np.arange(4096, dtype=np.float64)
    qaug = np.zeros((12, 6, 4096), dtype=np.float32)
    for h in range(12):
        v = (np.float64(slopes[h]) * t).astype(np.float32)
        hi = _trunc_bf16(v)
        lo = _trunc_bf16(v - hi)
        qaug[h, 0], qaug[h, 1], qaug[h, 2], qaug[h, 3] = sh[h], sl[h], sh[h], sl[h]
        qaug[h, 4], qaug[h, 5] = -hi, -lo
    tt = np.arange(4096)
    cur = tt // 64
    frc = -np.ones((4096, 64), dtype=np.float32)
    frc[tt[cur >= 1], cur[cur >= 1] - 1] = 1.0e4
    frc[tt, cur] = 2.0e4
    frc[:, 0] = 3.0e4
    gm = (np.arange(128)[:, None] // 4 == np.arange(32)[None, :]).astype(np.float32)
    qaug_s = np.zeros((6, 12, 32), dtype=np.float32)
    for h in range(12):
        v = (np.float64(slopes[h]) * (8192.0 + np.arange(8))).astype(np.float32)
        hi = _trunc_bf16(v)
        lo = _trunc_bf16(v - hi)
        qaug_s[0, h], qaug_s[1, h], qaug_s[2, h], qaug_s[3, h] = sh[h], sl[h], sh[h], sl[h]
        qaug_s[4, h] = np.tile(-hi, 4)
        qaug_s[5, h] = np.tile(-lo, 4)
    frc_s = -np.ones((8, 130), dtype=np.float32)
    frc_s[:, 0], frc_s[:, 128], frc_s[:, 127], frc_s[:, 129] = 3.0e4, 2.0e4, 1.0e4, -1.0e30
    return {"kaug_t": key_rows(np.arange(4096)), "kaug_c": key_rows(16 * np.arange(256) + 31),
            "qaug_t": qaug, "frc_t": frc, "gm_t": gm,
            "kaug_s": key_rows(np.arange(8320)), "kaug_cs": key_rows(16 * np.arange(512) + 31),
            "kaug_ws": key_rows(7680 + np.arange(640)), "qaug_s": np.ascontiguousarray(qaug_s.reshape(6, 384)),
            "frc_s": frc_s}


def kernel(**inputs):
    f = lambda a: np.ascontiguousarray(np.asarray(a, dtype=np.float32))
    if "nc" not in _CACHE:
        _CACHE["nc"] = build_program(_CACHE.get("stop"))
    nc = _CACHE["nc"]
    xp = f(inputs["x_prompt"])
    xs = f(inputs["x_sample"])
    stc = f(inputs["state_conv"])
    cmemkv = f(inputs["cache_mem_kv"])
    win = f(inputs["state_win_kv"])
    shared = {
        "g_mix": f(inputs["g_mix"]), "w_in_a": f(inputs["w_in_a"]), "conv_w": f(inputs["conv_w"]),
        "w_o": f(inputs["w_o"]), "w_mkv": f(inputs["w_mkv"]), "g_mem": f(inputs["g_mem"]).reshape(1, D),
        "g_kv": f(inputs["g_kv"]).reshape(1, D), "w_kv": f(inputs["w_kv"]), "g_ffn": f(inputs["g_ffn"]),
        "w_gu": f(inputs["w_gu"]), "w_dn": f(inputs["w_dn"]), "g_final": f(inputs["g_final"]).reshape(1, D),
    }
    shared.update({k: f(inputs[k]) for k in ("w_in_b", "w1_ck", "w1_cv", "w2_ck", "w2_cv", "pe_ck", "pe_cv")})
    shared.update(_const_tables())
    shared["pool_c"] = f(inputs["cache_cmp_kv"]).reshape(2560 * 128, 512)
    shared["pool_s"] = f(inputs["cache_slc_kv"]).reshape(2560 * 128, 512)
    ptab = np.ascontiguousarray(np.asarray(inputs["page_table"], dtype=np.int32))
    in_maps = []
    for c in range(8):
        b = c % 4
        m = dict(shared)
        m["x_in"] = np.ascontiguousarray(np.concatenate([xp[b], xs[4 * c:4 * c + 4].reshape(NS, D)], axis=0))
        m["stc"] = np.ascontiguousarray(stc[:, 4 * c:4 * c + 4].reshape(2, 8, CONV))
        m["cmem"] = np.ascontiguousarray(cmemkv[:, 4 * c:4 * c + 4].reshape(4, 4, 256, 512))
        m["memp"] = f(inputs["mem_prompt"])[b]
        m["win_in"] = np.ascontiguousarray(win[4 * c:4 * c + 4].reshape(4, 512, 512))
        m["ptab"] = np.ascontiguousarray(ptab[4 * c:4 * c + 4])
        in_maps.append(m)
    res = run_bass_kernel_spmd(nc, in_maps, core_ids=list(range(8)))
    R = res.results
    y_prompt = np.stack([R[b]["y_out"][:SEQ] for b in range(4)])
    y_sample = np.concatenate([R[c]["y_out"][SEQ:].reshape(4, 8, D) for c in range(8)])
    conv_p = np.stack([R[b]["cs_p"] for b in range(4)], axis=1)
    conv_s = np.concatenate([R[c]["cs_s"].reshape(2, 4, 2, CONV) for c in range(8)], axis=1)
    mem_kv_p = np.stack([R[b]["mkv_p"] for b in range(4)], axis=1).reshape(4, 4, 256, 2, 4, 64)
    cmp_p = np.stack([R[b]["cmp_o"][:SEQ] for b in range(4)]).reshape(4, SEQ, 2, 4, 64)
    slc_p = np.stack([R[b]["slc_o"][:SEQ] for b in range(4)]).reshape(4, SEQ, 2, 4, 64)
    win_p = np.stack([R[b]["winp_o"] for b in range(4)]).reshape(4, 512, 2, 4, 64)
    cmp_s = np.concatenate([R[c]["cmp_o"][SEQ:].reshape(4, 8, 2, 4, 64) for c in range(8)])
    slc_s = np.concatenate([R[c]["slc_o"][SEQ:].reshape(4, 8, 2, 4, 64) for c in range(8)])
    win_s = np.concatenate([R[c]["wins_o"].reshape(4, 512, 2, 4, 64) for c in range(8)])
    return (y_prompt, y_sample, conv_p, conv_s, mem_kv_p, cmp_p, slc_p, win_p, cmp_s, slc_s, win_s)
```
